# Optimizing a Trainium2 kernel written in Bass

```python
import math
import jax
import jax.numpy as jnp
from jax import lax
import numpy as np


D_MODEL = 1024
BATCH = 8
SEQ = 2048
DEPTH = 4

GRID_W = 64
CTX_LEN = 256
N_MIXERS = 3
HEAD_DIM = 64
ROPE_QUARTER = HEAD_DIM // 4
ROPE_BASE = 10000.0
EPS = 1e-6
NEG = -1e30
DA_HEADS = 8
DA_HEAD_DIM = HEAD_DIM
DA_IN_DIM = 3 * DA_HEADS * 2 * DA_HEAD_DIM
Q_BLOCK = 128
ML_HEADS = 8
ML_QK_DIM = 64
ML_V_DIM = D_MODEL // ML_HEADS
ML_CHUNK = 64
ML_FORGET_BIAS = 3.0
ML_IN_DIM = 2 * ML_HEADS * ML_QK_DIM + 2 * ML_HEADS * ML_V_DIM + 4 * ML_HEADS
SW_Q_HEADS = 16
SW_KV_HEADS = 4
SW_HEAD_DIM = HEAD_DIM
SW_WINDOW = 128
SW_BLOCK = 128
SW_IN_DIM = (SW_Q_HEADS + 2 * SW_KV_HEADS) * SW_HEAD_DIM
FFN_DIM = 2816
FFN_CONV = 3

kernel_name = 'hybrid_diffusion_interleaved_block'


def rmsnorm(x, g):
    xf = x.astype(jnp.float32)
    y = xf * lax.rsqrt(jnp.mean(xf * xf, axis=-1, keepdims=True) + EPS)
    return (y * g.astype(jnp.float32)).astype(x.dtype)


def modulate(h, shift, scale):
    return h * (1.0 + scale) + shift


def axial_rope_tables(n_tokens):
    rows = n_tokens // GRID_W
    row = jnp.repeat(jnp.arange(rows, dtype=jnp.float32), GRID_W)
    col = jnp.tile(jnp.arange(GRID_W, dtype=jnp.float32), rows)
    inv = ROPE_BASE ** (-jnp.arange(ROPE_QUARTER, dtype=jnp.float32) / ROPE_QUARTER)
    ang_r = row[:, None] * inv
    ang_c = col[:, None] * inv
    return (jnp.cos(ang_r)[:, None, :], jnp.sin(ang_r)[:, None, :],
            jnp.cos(ang_c)[:, None, :], jnp.sin(ang_c)[:, None, :])


def rope_half(x, cos, sin):
    x1, x2 = jnp.split(x, 2, axis=-1)
    cos = cos.astype(x.dtype)
    sin = sin.astype(x.dtype)
    return jnp.concatenate([x1 * cos - x2 * sin, x2 * cos + x1 * sin], axis=-1)


def apply_axial_rope(x, rope):
    cr, sr, cc, sc = rope
    half = x.shape[-1] // 2
    return jnp.concatenate([rope_half(x[..., :half], cr, sr), rope_half(x[..., half:], cc, sc)], axis=-1)


def diff_core(q, k, v, lam):
    s = jnp.einsum('bqhcd,bkhcd->bhcqk', q, k).astype(jnp.float32) * (DA_HEAD_DIM ** -0.5)
    p = jax.nn.softmax(s, axis=-1)
    w = p[:, :, 0] - lam * p[:, :, 1]
    return jnp.einsum('bhqk,bkhe->bqhe', w.astype(v.dtype), v)


def diff_attention(h_lat, h_ctx, w_in, lam_q1, lam_k1, lam_q2, lam_k2, subln_g, w_out, layer_idx, rope, need_ctx):
    f32 = jnp.float32
    lam_init = 0.8 - 0.6 * math.exp(-0.3 * layer_idx)
    lam = (jnp.exp(jnp.sum(lam_q1.astype(f32) * lam_k1.astype(f32)))
           - jnp.exp(jnp.sum(lam_q2.astype(f32) * lam_k2.astype(f32))) + lam_init)

    def project(h, use_rope):
        b, t, _ = h.shape
        q, k, v = jnp.split(h @ w_in, 3, axis=-1)
        q = q.reshape(b, t, 2 * DA_HEADS, DA_HEAD_DIM)
        k = k.reshape(b, t, 2 * DA_HEADS, DA_HEAD_DIM)
        if use_rope:
            q = apply_axial_rope(q, rope)
            k = apply_axial_rope(k, rope)
        return (q.reshape(b, t, DA_HEADS, 2, DA_HEAD_DIM),
                k.reshape(b, t, DA_HEADS, 2, DA_HEAD_DIM),
                v.reshape(b, t, DA_HEADS, 2 * DA_HEAD_DIM))

    def finish(o):
        b, t = o.shape[:2]
        return (rmsnorm(o, subln_g) * (1.0 - lam_init)).reshape(b, t, -1) @ w_out

    ql, kl, vl = project(h_lat, True)
    qc, kc, vc = project(h_ctx, False)
    k_all = jnp.concatenate([kc, kl], axis=1)
    v_all = jnp.concatenate([vc, vl], axis=1)
    b, s = h_lat.shape[:2]
    nb = s // Q_BLOCK
    q_blocks = jnp.moveaxis(ql.reshape(b, nb, Q_BLOCK, DA_HEADS, 2, DA_HEAD_DIM), 1, 0)
    o_blocks = lax.map(lambda qb: diff_core(qb, k_all, v_all, lam), q_blocks)
    o_lat = jnp.moveaxis(o_blocks, 0, 1).reshape(b, s, DA_HEADS, 2 * DA_HEAD_DIM)
    out_ctx = finish(diff_core(qc, kc, vc, lam)) if need_ctx else None
    return finish(o_lat), out_ctx


def mlstm_chunkwise(q, k, v, ig, lf, state):
    b, h, t, _ = q.shape
    nc = t // ML_CHUNK

    def to_chunks(a):
        return jnp.moveaxis(a.reshape(a.shape[:2] + (nc, ML_CHUNK) + a.shape[3:]), 2, 0)

    tril = jnp.tril(jnp.ones((ML_CHUNK, ML_CHUNK), dtype=bool))

    def step(carry, xs):
        c_prev, n_prev, m_prev = carry
        qc, kc, vc, ic, fc = xs
        bcum = jnp.cumsum(fc, axis=-1)
        dmat = jnp.where(tril, bcum[..., :, None] - bcum[..., None, :] + ic[..., None, :], -jnp.inf)
        m_t = jnp.maximum(bcum + m_prev[..., None], jnp.max(dmat, axis=-1))
        inter = jnp.exp(bcum + m_prev[..., None] - m_t)
        s = jnp.einsum('bhtd,bhsd->bhts', qc, kc) * jnp.exp(dmat - m_t[..., None])
        num = inter[..., None] * jnp.einsum('bhed,bhtd->bhte', c_prev, qc) + jnp.einsum('bhts,bhse->bhte', s, vc)
        den = inter * jnp.einsum('bhd,bhtd->bht', n_prev, qc) + jnp.sum(s, axis=-1)
        h_out = num / jnp.maximum(jnp.abs(den), jnp.exp(-m_t))[..., None]
        m_new = m_t[..., -1]
        g = jnp.exp(bcum[..., -1:] - bcum + ic - m_new[..., None])
        decay = jnp.exp(bcum[..., -1] + m_prev - m_new)
        c_new = decay[..., None, None] * c_prev + jnp.einsum('bhs,bhse,bhsd->bhed', g, vc, kc)
        n_new = decay[..., None] * n_prev + jnp.einsum('bhs,bhsd->bhd', g, kc)
        return (c_new, n_new, m_new), h_out

    state, hs = lax.scan(step, state, (to_chunks(q), to_chunks(k), to_chunks(v), to_chunks(ig), to_chunks(lf)))
    return jnp.moveaxis(hs, 0, 2).reshape(b, h, t, v.shape[-1]), state


def mlstm_mixer(h_lat, h_ctx, w_in, gate_b, norm_g, w_out, need_ctx):
    f32 = jnp.float32
    nqk = ML_HEADS * ML_QK_DIM
    nv = ML_HEADS * ML_V_DIM

    def project(h):
        b, t, _ = h.shape
        y = h @ w_in
        heads = lambda a, d: jnp.swapaxes(a.reshape(b, t, ML_HEADS, d), 1, 2).astype(f32)
        q = heads(y[..., :nqk], ML_QK_DIM) * (ML_QK_DIM ** -0.5)
        k = heads(y[..., nqk:2 * nqk], ML_QK_DIM)
        v = heads(y[..., 2 * nqk:2 * nqk + nv], ML_V_DIM)
        o = y[..., 2 * nqk + nv:2 * nqk + 2 * nv]
        g = (y[..., 2 * nqk + 2 * nv:] + gate_b).astype(f32).reshape(b, t, 4, ML_HEADS)
        g = jnp.moveaxis(g, 1, 3)
        return (q, k, v, o, g[:, 0], jax.nn.log_sigmoid(g[:, 1]), g[:, 2], jax.nn.log_sigmoid(g[:, 3]))

    def finish(hsum, o):
        b, t = o.shape[:2]
        hh = rmsnorm(jnp.swapaxes(hsum, 1, 2).astype(o.dtype), norm_g.reshape(ML_HEADS, ML_V_DIM))
        hh = hh * jax.nn.sigmoid(o).reshape(b, t, ML_HEADS, ML_V_DIM)
        return hh.reshape(b, t, nv) @ w_out

    rev = lambda a: jnp.flip(a, axis=2)
    qc, kc, vc, oc, icf, lcf, icb, lcb = project(h_ctx)
    ql, kl, vl, ol, ilf, llf, ilb, llb = project(h_lat)
    b = h_lat.shape[0]
    state0 = (jnp.zeros((b, ML_HEADS, ML_V_DIM, ML_QK_DIM), f32),
              jnp.zeros((b, ML_HEADS, ML_QK_DIM), f32),
              jnp.zeros((b, ML_HEADS), f32))
    hcf, st_f = mlstm_chunkwise(qc, kc, vc, icf, lcf, state0)
    hlf, _ = mlstm_chunkwise(ql, kl, vl, ilf, llf, st_f)
    hcb, st_b = mlstm_chunkwise(rev(qc), rev(kc), rev(vc), rev(icb), rev(lcb), state0)
    hlb, _ = mlstm_chunkwise(rev(ql), rev(kl), rev(vl), rev(ilb), rev(llb), st_b)
    out_lat = finish(hlf + rev(hlb), ol)
    out_ctx = finish(hcf + rev(hcb), oc) if need_ctx else None
    return out_lat, out_ctx


def sink_attention(q, k, v, sink_f, mask):
    s = jnp.einsum('bqhgd,bkhd->bhgqk', q, k).astype(jnp.float32) * (SW_HEAD_DIM ** -0.5)
    if mask is not None:
        s = jnp.where(mask[None, None, None], s, NEG)
    sink_col = jnp.broadcast_to(sink_f[None, :, :, None, None], s.shape[:-1] + (1,))
    p = jax.nn.softmax(jnp.concatenate([s, sink_col], axis=-1), axis=-1)[..., :-1]
    return jnp.einsum('bhgqk,bkhd->bqhgd', p.astype(v.dtype), v)


def swa_attention(h_lat, h_ctx, w_in, sink, w_out, rope, need_ctx):
    hq, hkv, dh = SW_Q_HEADS, SW_KV_HEADS, SW_HEAD_DIM
    grp = hq // hkv
    sink_f = sink.astype(jnp.float32).reshape(hkv, grp)

    def project(h):
        b, t, _ = h.shape
        y = h @ w_in
        q = y[..., :hq * dh].reshape(b, t, hq, dh)
        k = y[..., hq * dh:(hq + hkv) * dh].reshape(b, t, hkv, dh)
        v = y[..., (hq + hkv) * dh:].reshape(b, t, hkv, dh)
        return q, k, v

    ql, kl, vl = project(h_lat)
    ql = apply_axial_rope(ql, rope)
    kl = apply_axial_rope(kl, rope)
    qc, kc, vc = project(h_ctx)
    b, s = h_lat.shape[:2]
    n_ctx = kc.shape[1]
    nb = s // SW_BLOCK

    def band(a):
        ap = jnp.pad(a, ((0, 0), (SW_BLOCK, SW_BLOCK), (0, 0), (0, 0))).reshape(b, nb + 2, SW_BLOCK, hkv, dh)
        return jnp.moveaxis(jnp.concatenate([ap[:, :-2], ap[:, 1:-1], ap[:, 2:]], axis=2), 1, 0)

    kb, vb = band(kl), band(vl)
    qb = jnp.moveaxis(ql.reshape(b, nb, SW_BLOCK, hkv, grp, dh), 1, 0)
    start = jnp.arange(nb)[:, None] * SW_BLOCK
    qpos = start + jnp.arange(SW_BLOCK)[None]
    kpos = start - SW_BLOCK + jnp.arange(3 * SW_BLOCK)[None]
    band_mask = ((jnp.abs(qpos[:, :, None] - kpos[:, None, :]) <= SW_WINDOW)
                 & (kpos[:, None, :] >= 0) & (kpos[:, None, :] < s))
    ctx_mask = jnp.ones((nb, SW_BLOCK, n_ctx), dtype=bool)
    full_mask = jnp.concatenate([ctx_mask, band_mask], axis=-1)

    def block(args):
        q_j, k_j, v_j, m_j = args
        return sink_attention(q_j, jnp.concatenate([kc, k_j], axis=1), jnp.concatenate([vc, v_j], axis=1), sink_f, m_j)

    o = lax.map(block, (qb, kb, vb, full_mask))
    out_lat = jnp.moveaxis(o, 0, 1).reshape(b, s, hq * dh) @ w_out
    out_ctx = None
    if need_ctx:
        oc = sink_attention(qc.reshape(b, n_ctx, hkv, grp, dh), kc, vc, sink_f, None)
        out_ctx = oc.reshape(b, n_ctx, hq * dh) @ w_out
    return out_lat, out_ctx


def conv_ffn(h, w_up, conv_w, conv_b, w_down):
    u = h @ w_up
    t = u.shape[1]
    pad = (FFN_CONV - 1) // 2
    up = jnp.pad(u, ((0, 0), (pad, pad), (0, 0)))
    u = conv_b + sum(up[:, j:j + t] * conv_w[j] for j in range(FFN_CONV))
    a, g = jnp.split(u, 2, axis=-1)
    return (a * jax.nn.silu(g)) @ w_down


def setup_inputs(seed: int = 0) -> dict:
    key = jax.random.key(seed)
    keys = iter(jax.random.split(key, 128))
    f32 = jnp.float32

    def normal(shape, scale=1.0):
        return scale * jax.random.normal(next(keys), shape, f32)

    def dense(fan_in, fan_out, gain=1.0):
        return normal((fan_in, fan_out), gain * fan_in ** -0.5)

    def norm_gain(n):
        return 1.0 + normal((n,), 0.1)

    def bias(n):
        return normal((n,), 0.02)

    d = D_MODEL
    inputs = {
        'x': normal((BATCH, SEQ, d)),
        'c': normal((BATCH, d)),
        'ctx': normal((BATCH, CTX_LEN, d)),
        'c_ctx': normal((d,)),
    }
    for i in range(DEPTH):
        p = 'l%d_' % i
        kind = i % N_MIXERS
        inputs[p + 'ada_w'] = dense(d, 6 * d, 0.5)
        inputs[p + 'ada_b'] = bias(6 * d)
        inputs[p + 'norm1_g'] = norm_gain(d)
        if kind == 0:
            inputs[p + 'da_w_in'] = dense(d, DA_IN_DIM)
            for nm in ('q1', 'k1', 'q2', 'k2'):
                inputs[p + 'da_lam_' + nm] = normal((DA_HEAD_DIM,), 0.1)
            inputs[p + 'da_subln_g'] = norm_gain(2 * DA_HEAD_DIM)
            inputs[p + 'da_w_out'] = dense(d, d)
        elif kind == 1:
            inputs[p + 'ml_w_in'] = dense(d, ML_IN_DIM)
            forget_offset = jnp.tile(jnp.repeat(jnp.array([0.0, ML_FORGET_BIAS], f32), ML_HEADS), 2)
            inputs[p + 'ml_gate_b'] = forget_offset + normal((4 * ML_HEADS,), 0.1)
            inputs[p + 'ml_norm_g'] = norm_gain(ML_HEADS * ML_V_DIM)
            inputs[p + 'ml_w_out'] = dense(ML_HEADS * ML_V_DIM, d)
        else:
            inputs[p + 'sw_w_in'] = dense(d, SW_IN_DIM)
            inputs[p + 'sw_sink'] = normal((SW_Q_HEADS,), 0.5)
            inputs[p + 'sw_w_out'] = dense(SW_Q_HEADS * SW_HEAD_DIM, d)
        inputs[p + 'norm2_g'] = norm_gain(d)
        inputs[p + 'ffn_w_up'] = dense(d, 2 * FFN_DIM)
        inputs[p + 'ffn_conv_w'] = normal((FFN_CONV, 2 * FFN_DIM), FFN_CONV ** -0.5)
        inputs[p + 'ffn_conv_b'] = bias(2 * FFN_DIM)
        inputs[p + 'ffn_w_down'] = dense(FFN_DIM, d)
    inputs['final_norm_g'] = norm_gain(d)
    return inputs


def reference(x, c, ctx, c_ctx,
              l0_ada_w, l0_ada_b, l0_norm1_g, l0_da_w_in, l0_da_lam_q1, l0_da_lam_k1, l0_da_lam_q2, l0_da_lam_k2,
              l0_da_subln_g, l0_da_w_out, l0_norm2_g, l0_ffn_w_up, l0_ffn_conv_w, l0_ffn_conv_b, l0_ffn_w_down,
              l1_ada_w, l1_ada_b, l1_norm1_g, l1_ml_w_in, l1_ml_gate_b, l1_ml_norm_g, l1_ml_w_out,
              l1_norm2_g, l1_ffn_w_up, l1_ffn_conv_w, l1_ffn_conv_b, l1_ffn_w_down,
              l2_ada_w, l2_ada_b, l2_norm1_g, l2_sw_w_in, l2_sw_sink, l2_sw_w_out,
              l2_norm2_g, l2_ffn_w_up, l2_ffn_conv_w, l2_ffn_conv_b, l2_ffn_w_down,
              l3_ada_w, l3_ada_b, l3_norm1_g, l3_da_w_in, l3_da_lam_q1, l3_da_lam_k1, l3_da_lam_q2, l3_da_lam_k2,
              l3_da_subln_g, l3_da_w_out, l3_norm2_g, l3_ffn_w_up, l3_ffn_conv_w, l3_ffn_conv_b, l3_ffn_w_down,
              final_norm_g):
    layers = (
        (l0_ada_w, l0_ada_b, l0_norm1_g,
         (l0_da_w_in, l0_da_lam_q1, l0_da_lam_k1, l0_da_lam_q2, l0_da_lam_k2, l0_da_subln_g, l0_da_w_out),
         l0_norm2_g, (l0_ffn_w_up, l0_ffn_conv_w, l0_ffn_conv_b, l0_ffn_w_down)),
        (l1_ada_w, l1_ada_b, l1_norm1_g,
         (l1_ml_w_in, l1_ml_gate_b, l1_ml_norm_g, l1_ml_w_out),
         l1_norm2_g, (l1_ffn_w_up, l1_ffn_conv_w, l1_ffn_conv_b, l1_ffn_w_down)),
        (l2_ada_w, l2_ada_b, l2_norm1_g,
         (l2_sw_w_in, l2_sw_sink, l2_sw_w_out),
         l2_norm2_g, (l2_ffn_w_up, l2_ffn_conv_w, l2_ffn_conv_b, l2_ffn_w_down)),
        (l3_ada_w, l3_ada_b, l3_norm1_g,
         (l3_da_w_in, l3_da_lam_q1, l3_da_lam_k1, l3_da_lam_q2, l3_da_lam_k2, l3_da_subln_g, l3_da_w_out),
         l3_norm2_g, (l3_ffn_w_up, l3_ffn_conv_w, l3_ffn_conv_b, l3_ffn_w_down)),
    )
    rope = axial_rope_tables(x.shape[1])
    for i in range(DEPTH):
        ada_w, ada_b, norm1_g, mix, norm2_g, ffn = layers[i]
        kind = i % N_MIXERS
        need_ctx = i < DEPTH - 1
        mod_l = jnp.split((jax.nn.silu(c) @ ada_w + ada_b)[:, None, :], 6, axis=-1)
        mod_c = jnp.split((jax.nn.silu(c_ctx) @ ada_w + ada_b)[None, None, :], 6, axis=-1)
        h_lat = modulate(rmsnorm(x, norm1_g), mod_l[0], mod_l[1])
        h_ctx = modulate(rmsnorm(ctx, norm1_g), mod_c[0], mod_c[1])
        if kind == 0:
            o_lat, o_ctx = diff_attention(h_lat, h_ctx, *mix, i, rope, need_ctx)
        elif kind == 1:
            o_lat, o_ctx = mlstm_mixer(h_lat, h_ctx, *mix, need_ctx)
        else:
            o_lat, o_ctx = swa_attention(h_lat, h_ctx, *mix, rope, need_ctx)
        x = x + mod_l[2] * o_lat
        x = x + mod_l[5] * conv_ffn(modulate(rmsnorm(x, norm2_g), mod_l[3], mod_l[4]), *ffn)
        if need_ctx:
            ctx = ctx + mod_c[2] * o_ctx
            ctx = ctx + mod_c[5] * conv_ffn(modulate(rmsnorm(ctx, norm2_g), mod_c[3], mod_c[4]), *ffn)
    return rmsnorm(x, final_norm_g)
```

```python
import math
from contextlib import ExitStack
import numpy as np
import ml_dtypes
import concourse.bass as bass
import concourse.mybir as mybir
from concourse.bass_utils import run_bass_kernel_spmd

F32 = mybir.dt.float32
BF16 = mybir.dt.bfloat16
AF = mybir.ActivationFunctionType
ALU = mybir.AluOpType
AX = mybir.AxisListType

ENGS = ("pe", "act", "dve", "pool", "sp")
RELAX_SAME = False
NDMA_SEM = 8
NDMA_Q = {"sp": 8, "pool": 3, "act": 8}


class T:
    __slots__ = ("name", "lw", "rs")

    def __init__(self, name=""):
        self.name = name
        self.lw = None
        self.rs = []


class Op:
    __slots__ = ("eng", "fn", "deps", "dma", "signal", "idx", "cnt", "dslot", "dval", "nosame", "pesync")

    def __init__(self, eng, fn, dma):
        self.eng = eng
        self.fn = fn
        self.deps = []
        self.dma = dma
        self.signal = False
        self.idx = -1
        self.cnt = -1
        self.dslot = -1
        self.dval = -1
        self.nosame = False
        self.pesync = False


class Prog:
    def __init__(self, nc, same_engine_sync=True):
        self.nc = nc
        self.same = same_engine_sync
        self.relax = RELAX_SAME
        self.sem = {}
        self.sem_ctx = []
        for e in ("pe", "act", "dve", "pool"):
            self.sem[e] = self._mksem("s_" + e)
        self.dsem = {}
        for q in ("sp", "pool", "act"):
            self.dsem[q] = [self._mksem("d_%s%d" % (q, i)) for i in range(NDMA_SEM)]
        self.bar = self._mksem("bar")
        self.bar_cnt = 0
        self.cnt = {e: 0 for e in ("pe", "act", "dve", "pool")}
        self.dcnt = {q: 0 for q in ("sp", "pool", "act")}
        self.ops = {e: [] for e in ENGS}
        self.tiles = []
        self.n_instr = 0

    def _mksem(self, name):
        ctx = self.nc.semaphore(name)
        s = ctx.__enter__()
        self.sem_ctx.append(ctx)
        return s

    def close(self):
        for ctx in reversed(self.sem_ctx):
            ctx.__exit__(None, None, None)

    def tile(self, name=""):
        t = T(name)
        self.tiles.append(t)
        return t

    def skip_same(self, o, d):
        if d.eng != o.eng or o.dma or d.dma:
            return False
        if o.eng == "pe":
            return not o.pesync
        return (not self.same) or o.nosame

    def op(self, eng, fn, reads=(), writes=(), dma=False, nosame=False, pesync=False):
        o = Op(eng, fn, dma)
        o.nosame = nosame
        o.pesync = pesync
        deps = []
        for t in reads:
            if t.lw is not None:
                deps.append(t.lw)
        for t in writes:
            if t.lw is not None and (dma or t.lw.dma or t.lw.eng != eng or not self.relax):
                deps.append(t.lw)
            for r in t.rs:
                if dma or r.dma or r.eng != eng or not self.relax:
                    deps.append(r)
        for t in reads:
            t.rs.append(o)
        for t in writes:
            t.lw = o
            t.rs = []
        seen = set()
        best = {}
        for d in deps:
            if d is o or id(d) in seen:
                continue
            seen.add(id(d))
            if d.dma:
                o.deps.append(d)
            else:
                b = best.get(d.eng)
                if b is None or d.idx > b.idx:
                    best[d.eng] = d
        o.deps.extend(best.values())
        o.idx = len(self.ops[eng])
        self.ops[eng].append(o)
        return o

    def dma(self, q, out, in_, reads=(), writes=(), **kw):
        def fn(e):
            return e.dma_start(out=out, in_=in_, **kw)
        return self.op(q, fn, reads, writes, dma=True)

    def flush(self, final=False):
        nc = self.nc
        ops = self.ops
        for e in ENGS:
            for o in ops[e]:
                for d in o.deps:
                    if not d.dma:
                        if self.skip_same(o, d):
                            continue
                        d.signal = True
        for e in ("pe", "act", "dve", "pool"):
            comp = [o for o in ops[e] if not o.dma]
            if comp:
                comp[-1].signal = True
        for e in ("pe", "act", "dve", "pool"):
            c = self.cnt[e]
            for o in ops[e]:
                if o.dma:
                    continue
                if o.signal:
                    c += 1
                o.cnt = c if o.signal else -1
            nxt = None
            for o in reversed(ops[e]):
                if o.dma:
                    continue
                if o.signal:
                    nxt = o.cnt
                else:
                    o.cnt = nxt
            self.cnt[e] = c
        for q in ("sp", "pool", "act"):
            n = self.dcnt[q]
            for o in ops[q]:
                if o.dma:
                    o.dslot = n % NDMA_Q[q]
                    o.dval = 16 * (n // NDMA_Q[q] + 1)
                    n += 1
            self.dcnt[q] = n
        final_cnt = dict(self.cnt)
        final_d = {}
        for q in ("sp", "pool", "act"):
            n = self.dcnt[q]
            final_d[q] = [16 * ((n - s + NDMA_Q[q] - 1) // NDMA_Q[q]) if s < NDMA_Q[q] else 0 for s in range(NDMA_SEM)]
        self.bar_cnt += 1
        bar_val = self.bar_cnt
        prog = self

        def emit(ename, eng):
            waited = {}

            def wait(sem, val, key):
                if waited.get(key, 0) >= val:
                    return
                waited[key] = val
                eng.wait_ge(sem, val)
                prog.n_instr += 1

            for o in ops[ename]:
                for d in o.deps:
                    if d.dma:
                        wait(prog.dsem[d.eng][d.dslot], d.dval, ("d", d.eng, d.dslot))
                    else:
                        if prog.skip_same(o, d):
                            continue
                        wait(prog.sem[d.eng], d.cnt, ("c", d.eng))
                if o.dma:
                    if o.dval > 16:
                        wait(prog.dsem[ename][o.dslot], o.dval - 16, ("d", ename, o.dslot))
                    ins = o.fn(eng)
                    ins.then_inc(prog.dsem[ename][o.dslot], 16)
                else:
                    ins = o.fn(eng)
                    if o.signal:
                        ins.then_inc(prog.sem[ename], 1)
                prog.n_instr += 1
            if ename == "sp":
                for e2 in ("pe", "act", "dve", "pool"):
                    if final_cnt[e2] > 0:
                        wait(prog.sem[e2], final_cnt[e2], ("c", e2))
                for q in ("sp", "pool", "act"):
                    for s in range(NDMA_SEM):
                        if final_d[q][s] > 0:
                            wait(prog.dsem[q][s], final_d[q][s], ("d", q, s))
                eng.sem_inc(prog.bar, 1)
                if final:
                    eng.wait_ge(prog.bar, bar_val)
            else:
                eng.wait_ge(prog.bar, bar_val)

        with nc.Block() as block:
            @block.tensor
            def _(e):
                emit("pe", e)

            @block.scalar
            def _(e):
                emit("act", e)

            @block.vector
            def _(e):
                emit("dve", e)

            @block.gpsimd
            def _(e):
                emit("pool", e)

            @block.sync
            def _(e):
                emit("sp", e)

        self.ops = {e: [] for e in ENGS}
        for t in self.tiles:
            t.lw = None
            t.rs = []
        self.tiles = []

EPS = 1e-6
NT, NL, NCX = 2304, 2048, 256
BLKS = [(0, 512), (512, 1024), (1024, 1536), (1536, 2048), (2048, 2304)]
CBLKS = [(0, 410), (410, 820), (820, 1230), (1230, 1640), (1640, 2048), (2048, 2304)]
KINDS = [0, 1, 2, 0]
FFN = 2816
NFC = 22


def blk_of(t):
    return min(t // 512, 4)


def blks_overlap(lo, hi):
    return sorted(set(blk_of(t) for t in (lo, hi - 1)) | set(range(blk_of(lo), blk_of(hi - 1) + 1)))


class Ring:
    def __init__(self, P, bufs):
        self.P = P
        self.b = list(bufs)
        self.t = [None] * len(self.b)
        self.i = 0

    def next(self):
        k = self.i % len(self.b)
        self.i += 1
        if self.t[k] is None:
            self.t[k] = self.P.tile()
        return self.b[k], self.t[k]


class Builder:
    def __init__(self, n_layers=4, debug=False):
        self.nc = bass.Bass("TRN2", target_bir_lowering=False)
        self.P = Prog(self.nc)
        self.uid = 0
        self.n_layers = n_layers
        self.debug = debug
        self.es = None

    def din(self, name, shape, dt=F32):
        return self.nc.dram_tensor(name, list(shape), dt, kind="ExternalInput").ap()

    def sb(self, shape, dt=F32, es=None):
        self.uid += 1
        return (es or self.es).enter_context(self.nc.sbuf_tensor("sb%d" % self.uid, list(shape), dt))

    def ps(self, shape, dt=F32, es=None):
        self.uid += 1
        return (es or self.es).enter_context(self.nc.psum_tensor("ps%d" % self.uid, list(shape), dt))

    def ring(self, n, shape, dt=F32, psum=False):
        return Ring(self.P, [(self.ps if psum else self.sb)(shape, dt) for _ in range(n)])

    def mm(self, out, lhsT, rhs, start, stop, R, W, sgc=False, pesync=False):
        if sgc:
            self.P.op("pe", lambda e: e.matmul(out, lhsT, rhs, start=start, stop=stop, skip_group_check=True), R, W,
                      pesync=pesync)
        else:
            self.P.op("pe", lambda e: e.matmul(out, lhsT, rhs, start=start, stop=stop), R, W, pesync=pesync)

    def act(self, out, in_, func, R, W, bias=0.0, scale=1.0, accum=None):
        if accum is None:
            self.P.op("act", lambda e: e.activation(out=out, in_=in_, func=func, bias=bias, scale=scale), R, W)
        else:
            self.P.op("act", lambda e: e.activation(out=out, in_=in_, func=func, bias=bias, scale=scale,
                                                    accum_out=accum), R, W)

    def tt(self, eng, out, in0, in1, op, R, W):
        self.P.op(eng, lambda e: e.tensor_tensor(out=out, in0=in0, in1=in1, op=op), R, W)

    def ts(self, eng, out, in0, s1, s2, op0, op1, R, W):
        self.P.op(eng, lambda e: e.tensor_scalar(out=out, in0=in0, scalar1=s1, scalar2=s2, op0=op0, op1=op1), R, W)

    def stt(self, eng, out, in0, sc, in1, op0, op1, R, W):
        self.P.op(eng, lambda e: e.scalar_tensor_tensor(out=out, in0=in0, scalar=sc, in1=in1, op0=op0, op1=op1), R, W)

    def cp(self, eng, out, in_, R, W):
        if eng == "act":
            self.P.op("act", lambda e: e.activation(out=out, in_=in_, func=AF.Copy), R, W)
        else:
            self.P.op(eng, lambda e: e.tensor_copy(out=out, in_=in_), R, W)

    def memset(self, eng, ap, val, R, W):
        self.P.op(eng, lambda e: e.memset(ap, val), R, W)

    def recip(self, out, in_, R, W):
        self.P.op("dve", lambda e: e.reciprocal(out=out, in_=in_), R, W)

    def dma(self, q, out, in_, R=(), W=()):
        self.P.dma(q, out, in_, R, W)

    def begin(self):
        self.es = ExitStack()
        self.tx = [[self.P.tile() for _ in range(8)] for _ in range(5)]
        self.th = [self.P.tile() for _ in range(5)]

    def end(self, final=False):
        self.P.flush(final=final)
        self.es.close()
        self.es = None

    def txall(self, bi):
        return list(self.tx[bi])

    def build(self):
        nc = self.nc
        d = {}
        d["x"] = self.din("x", [NL, 1024])
        d["ctx"] = self.din("ctx", [NCX, 1024])
        d["cc"] = self.din("cc", [128, 16])
        d["ident"] = self.din("ident", [128, 128])
        d["rotT"] = self.din("rotT", [128, 128])
        d["cos"] = self.din("cos", [128, NL])
        d["sin"] = self.din("sin", [128, NL])
        d["fng"] = self.din("fng", [128, 8])
        d["swmask"] = self.din("swmask", [128, 6, 512], BF16)
        d["mlmask"] = self.din("mlmask", [128, 2, 128])
        for L in range(self.n_layers):
            p = "l%d_" % L
            d[p + "ada_w"] = self.din(p + "ada_w", [1024, 6144])
            d[p + "ada_b"] = self.din(p + "ada_b", [128, 48])
            d[p + "n1g"] = self.din(p + "n1g", [128, 8])
            d[p + "n2g"] = self.din(p + "n2g", [128, 8])
            d[p + "w_up"] = self.din(p + "w_up", [1024, 2 * FFN])
            d[p + "w_down"] = self.din(p + "w_down", [FFN, 1024])
            d[p + "cw"] = self.din(p + "cw", [128, 44, 3])
            d[p + "cb"] = self.din(p + "cb", [128, 44])
            k = KINDS[L]
            if k == 0:
                d[p + "w_in"] = self.din(p + "w_in", [1024, 3072])
                d[p + "lam"] = self.din(p + "lam", [128, 4, 64])
                d[p + "subgc"] = self.din(p + "subgc", [128, 1])
            elif k == 1:
                d[p + "w_in"] = self.din(p + "w_in", [1024, 3104])
                d[p + "gate_b"] = self.din(p + "gate_b", [128, 32])
                d[p + "mlg"] = self.din(p + "mlg", [128, 1024])
            else:
                d[p + "w_in"] = self.din(p + "w_in", [1024, 1536])
                d[p + "sink"] = self.din(p + "sink", [128, 16])
            d[p + "w_out"] = self.din(p + "w_out", [1024, 1024])
        if self.debug:
            self.out_d = nc.dram_tensor("out", [NT, 1024], F32, kind="ExternalOutput").ap()
        else:
            self.out_d = nc.dram_tensor("out", [NL, 1024], F32, kind="ExternalOutput").ap()
        self.d = d

        top = ExitStack()
        self.top = top
        self.xT = self.sb([128, 8, NT], F32, top)
        self.ident = self.sb([128, 128], F32, top)
        self.ones_bf = self.sb([128, 128], BF16, top)
        self.modv = self.sb([128, 4, 48, 2], F32, top)
        self.gs1 = self.sb([128, 4, 8, 2], F32, top)
        self.gs2 = self.sb([128, 4, 8, 2], F32, top)
        self.fng = self.sb([128, 8], F32, top)

        self.phase_load()
        self.phase_ada()
        for L in range(self.n_layers):
            need_ctx = L < 3
            k = KINDS[L]
            self.phase_norm(self.gs1, L, 0, list(range(5)))
            if k == 0:
                self.phase_attn(L, need_ctx, "da")
            elif k == 1:
                self.phase_mlstm(L, need_ctx)
            else:
                self.phase_attn(L, need_ctx, "sw")
            oblks = list(range(5 if need_ctx else 4))
            if self.debug == "mix" and L == self.n_layers - 1:
                break
            self.phase_norm(self.gs2, L, 24, oblks)
            self.phase_ffn(L, oblks)
        self.phase_final()
        top.close()
        self.P.close()
        return nc

    def phase_load(self):
        self.begin()
        d = self.d
        xin = self.ring(3, [128, 1024])
        pst = self.ring(4, [128, 512], psum=True)
        tc = self.P.tile()
        self.dma("sp", self.ident[:], d["ident"], W=[tc])
        self.dma("sp", self.fng[:], d["fng"], W=[tc])
        self.memset("dve", self.ones_bf[:], 1.0, [], [tc])
        for tt_ in range(18):
            src = d["x"][tt_ * 128:(tt_ + 1) * 128, :] if tt_ < 16 else d["ctx"][(tt_ - 16) * 128:(tt_ - 15) * 128, :]
            xb, xt_ = xin.next()
            self.dma("sp", xb[:], src, W=[xt_])
            bi = blk_of(tt_ * 128)
            for half in range(2):
                pb, pt = pst.next()
                for c4 in range(4):
                    c = half * 4 + c4
                    self.mm(pb[:, c4 * 128:(c4 + 1) * 128], xb[:, c * 128:(c + 1) * 128], self.ident[:], True, True,
                            [xt_, tc], [pt])
                self.cp("dve" if half == 0 else "act",
                        self.xT[:, half * 4:(half + 1) * 4, tt_ * 128:(tt_ + 1) * 128],
                        pb[:].rearrange("p (c t) -> p c t", c=4), [pt], self.tx[bi][half * 4:(half + 1) * 4])
        self.end()

    def phase_ada(self):
        self.begin()
        d = self.d
        cc = self.sb([128, 16])
        scT = self.sb([128, 16], BF16)
        tcc = self.P.tile()
        self.dma("sp", cc[:], d["cc"], W=[tcc])
        tsc = self.P.tile()
        self.act(scT[:], cc[:], AF.Silu, [tcc], [tsc])
        wring = self.ring(2, [128, 8, 768], BF16)
        pmod = self.ring(2, [128, 512], psum=True)
        adab = self.sb([128, 4, 48])
        n1g = self.sb([128, 4, 8])
        n2g = self.sb([128, 4, 8])
        tsm = self.P.tile()
        for L in range(self.n_layers):
            p = "l%d_" % L
            self.dma("sp", adab[:, L, :], d[p + "ada_b"], W=[tsm])
            self.dma("sp", n1g[:, L, :], d[p + "n1g"], W=[tsm])
            self.dma("sp", n2g[:, L, :], d[p + "n2g"], W=[tsm])
        for L in range(self.n_layers):
            p = "l%d_" % L
            wv = d[p + "ada_w"].rearrange("(kc kp) f -> kp kc f", kp=128)
            pb, pt = pmod.next()
            for fg in range(8):
                wb, wt = wring.next()
                self.dma("pool", wb[:], wv[:, :, fg * 768:(fg + 1) * 768], W=[wt])
                for f6 in range(6):
                    f = fg * 6 + f6
                    for kc in range(8):
                        self.mm(pb[:, f * 2:(f + 1) * 2], wb[:, kc, f6 * 128:(f6 + 1) * 128], scT[:, kc:16:8],
                                kc == 0, kc == 7, [wt, tsc], [pt])
            tm = self.P.tile()
            for j in range(2):
                self.tt("dve", self.modv[:, L, :, j], pb[:, j:96:2], adab[:, L, :], ALU.add, [pt, tsm], [tm])
            for j in range(2):
                self.stt("dve", self.gs1[:, L, :, j], self.modv[:, L, 8:16, j], 1.0, n1g[:, L, :], ALU.add, ALU.mult,
                         [tm, tsm], [tm])
                self.stt("dve", self.gs2[:, L, :, j], self.modv[:, L, 32:40, j], 1.0, n2g[:, L, :], ALU.add, ALU.mult,
                         [tm, tsm], [tm])
        self.end()

    def alloc_hT(self):
        self.hes = ExitStack()
        self.hT = self.sb([128, 8, NT], BF16, self.hes)

    def free_hT(self):
        self.hes.close()

    def phase_norm(self, gs, L, shift_base, blks):
        self.alloc_hT()
        self.begin()
        sq = self.ring(2, [128, 8, 512], BF16)
        pss = self.ring(2, [128, 512], psum=True)
        rr = self.ring(2, [128, 512])
        tmp = self.ring(4, [128, 512])
        for bi in blks:
            s, e = BLKS[bi]
            w = e - s
            j = 1 if bi == 4 else 0
            sqb, sqt = sq.next()
            self.act(sqb[:, :, :w], self.xT[:, :, s:e], AF.Square, self.txall(bi), [sqt])
            pb, pt = pss.next()
            for c in range(8):
                self.mm(pb[:, :w], self.ones_bf[:], sqb[:, c, :w], c == 0, c == 7, [sqt], [pt])
            rb, rt = rr.next()
            self.act(rb[:, :w], pb[:, :w], AF.Sqrt, [pt], [rt], bias=EPS, scale=1.0 / 1024)
            self.recip(rb[:, :w], rb[:, :w], [rt], [rt])
            for c in range(8):
                tb, tt_ = tmp.next()
                self.stt("dve", tb[:, :w], self.xT[:, c, s:e], gs[:, L, c, j:j + 1], rb[:, :w], ALU.mult, ALU.mult,
                         [self.tx[bi][c], rt], [tt_])
                self.act(self.hT[:, c, s:e], tb[:, :w], AF.Identity, [tt_], [self.th[bi]],
                         bias=self.modv[:, L, shift_base + c, j:j + 1])
        self.end()

    def phase_ffn(self, L, oblks):
        self.begin()
        d = self.d
        p = "l%d_" % L
        cblks = CBLKS if 4 in oblks else CBLKS[:5]
        cw = self.sb([128, 44, 3])
        cb = self.sb([128, 44])
        tcw = self.P.tile()
        self.dma("sp", cw[:], d[p + "cw"], W=[tcw])
        self.dma("sp", cb[:], d[p + "cb"], W=[tcw])
        wuv = d[p + "w_up"].rearrange("(kc kp) f -> kp kc f", kp=128)
        wuring = self.ring(3, [128, 8, 256], BF16)
        GS = 4
        groups = [list(range(a, min(a + GS, NFC))) for a in range(0, NFC, GS)]
        zring = self.ring(2, [128, GS, NT], BF16)
        wdring = self.ring(2, [128, GS, 1024], BF16)
        psr = self.ring(6, [128, 512], psum=True)
        pdr = self.ring(2, [128, 512], psum=True)
        vr = self.ring(8, [128, 512])
        for grp in groups:
            G = len(grp)
            wd, twd = wdring.next()
            zb, _ = zring.next()
            tz = [[self.P.tile() for _ in cblks] for _ in range(G)]
            for gi, jf in enumerate(grp):
                self.dma("pool", wd[:, gi, :], d[p + "w_down"][jf * 128:(jf + 1) * 128, :], W=[twd])
            its = []
            for gi, jf in enumerate(grp):
                for cbi in range(len(cblks)):
                    its.append((gi, jf, cbi))
            wcur = {}

            def stage1(it):
                gi, jf, cbi = it
                if cbi == 0:
                    wub, wut = wuring.next()
                    self.dma("pool", wub[:, :, 0:128], wuv[:, :, jf * 128:(jf + 1) * 128], W=[wut])
                    self.dma("pool", wub[:, :, 128:256], wuv[:, :, FFN + jf * 128:FFN + (jf + 1) * 128], W=[wut])
                    wcur[jf] = (wub, wut)
                wub, wut = wcur[jf]
                s, e = cblks[cbi]
                seq0, seq1 = (0, NL) if s < NL else (NL, NT)
                lo = max(s - 1, seq0)
                hi = min(e + 1, seq1)
                n = hi - lo
                m = e - s
                hb = [self.th[b_] for b_ in blks_overlap(lo, hi)]
                pa, pat = psr.next()
                for kc in range(8):
                    self.mm(pa[:, :n], wub[:, kc, 0:128], self.hT[:, kc, lo:hi], kc == 0, kc == 7, [wut] + hb, [pat])
                pg, pgt = psr.next()
                for kc in range(8):
                    self.mm(pg[:, :n], wub[:, kc, 128:256], self.hT[:, kc, lo:hi], kc == 0, kc == 7, [wut] + hb, [pgt])
                va, vat = vr.next()
                vg, vgt = vr.next()
                ag = ((pa, pat, va, vat, jf), (pg, pgt, vg, vgt, NFC + jf))
                o0 = s - lo
                for (pp, ppt, vv, vvt, fi) in ag:
                    self.act(vv[:, :m], pp[:, o0:o0 + m], AF.Identity, [ppt, tcw], [vvt],
                             bias=cb[:, fi:fi + 1], scale=cw[:, fi, 1:2])
                return (ag, s, e, lo, m, seq0, seq1)

            def stage2(st):
                ag, s, e, lo, m, seq0, seq1 = st
                ls = max(s, seq0 + 1)
                re_ = min(e, seq1 - 1)
                for (pp, ppt, vv, vvt, fi) in ag:
                    self.stt("dve", vv[:, ls - s:m], pp[:, ls - 1 - lo:e - 1 - lo], cw[:, fi, 0:1], vv[:, ls - s:m],
                             ALU.mult, ALU.add, [ppt, vvt, tcw], [vvt])
                for (pp, ppt, vv, vvt, fi) in ag:
                    self.stt("dve", vv[:, 0:re_ - s], pp[:, s + 1 - lo:re_ + 1 - lo], cw[:, fi, 2:3], vv[:, 0:re_ - s],
                             ALU.mult, ALU.add, [ppt, vvt, tcw], [vvt])

            def stage3(it, st):
                gi, jf, cbi = it
                ag, s, e, lo, m, seq0, seq1 = st
                (pa, pat, va, vat, _), (pg, pgt, vg, vgt, _) = ag
                self.act(vg[:, :m], vg[:, :m], AF.Silu, [vgt], [vgt])
                self.tt("pool", zb[:, gi, s:e], va[:, :m], vg[:, :m], ALU.mult, [vat, vgt], [tz[gi][cbi]])

            pend = stage1(its[0])
            for k_it, it in enumerate(its):
                cur = pend
                if k_it + 1 < len(its):
                    pend = stage1(its[k_it + 1])
                stage2(cur)
                stage3(it, cur)
            for bi in oblks:
                s, e = BLKS[bi]
                w = e - s
                j = 1 if bi == 4 else 0
                zt = []
                for gi in range(G):
                    for cbi, (cs, ce) in enumerate(cblks):
                        if cs < e and ce > s:
                            zt.append(tz[gi][cbi])
                for dc in range(8):
                    pb, pt = pdr.next()
                    for gi in range(G):
                        self.mm(pb[:, :w], wd[:, gi, dc * 128:(dc + 1) * 128], zb[:, gi, s:e], gi == 0, gi == G - 1,
                                [twd] + zt, [pt])
                    self.stt("dve", self.xT[:, dc, s:e], pb[:, :w], self.modv[:, L, 40 + dc, j:j + 1],
                             self.xT[:, dc, s:e], ALU.mult, ALU.add, [pt, self.tx[bi][dc]], [self.tx[bi][dc]])
        self.end()
        self.free_hT()

    def phase_final(self):
        self.begin()
        sq = self.ring(2, [128, 8, 512], BF16)
        pss = self.ring(2, [128, 512], psum=True)
        rr = self.ring(2, [128, 512])
        yb = self.ring(2, [128, 8, 512])
        ptr = self.ring(4, [128, 512], psum=True)
        ob = self.ring(3, [128, 1024])
        nb = 5 if self.debug else 4
        for bi in range(nb):
            s, e = BLKS[bi]
            w = e - s
            y, yt = yb.next()
            if self.debug:
                for c in range(8):
                    self.cp("dve", y[:, c, :w], self.xT[:, c, s:e], [self.tx[bi][c]], [yt])
            else:
                sqb, sqt = sq.next()
                self.act(sqb[:, :, :w], self.xT[:, :, s:e], AF.Square, self.txall(bi), [sqt])
                pb, pt = pss.next()
                for c in range(8):
                    self.mm(pb[:, :w], self.ones_bf[:], sqb[:, c, :w], c == 0, c == 7, [sqt], [pt])
                rb, rt = rr.next()
                self.act(rb[:, :w], pb[:, :w], AF.Sqrt, [pt], [rt], bias=EPS, scale=1.0 / 1024)
                self.recip(rb[:, :w], rb[:, :w], [rt], [rt])
                for c in range(8):
                    self.stt("dve", y[:, c, :w], self.xT[:, c, s:e], self.fng[:, c:c + 1], rb[:, :w], ALU.mult, ALU.mult,
                             [self.tx[bi][c], rt], [yt])
            for q in range(w // 128):
                o, ot = ob.next()
                for half in range(2):
                    pb2, pt2 = ptr.next()
                    for c4 in range(4):
                        c = half * 4 + c4
                        self.mm(pb2[:, c4 * 128:(c4 + 1) * 128], y[:, c, q * 128:(q + 1) * 128], self.ident[:], True, True,
                                [yt], [pt2])
                    self.cp("act" if half == 0 else "dve", o[:, half * 512:(half + 1) * 512], pb2[:], [pt2], [ot])
                r0 = s + q * 128
                self.dma("sp", self.out_d[r0:r0 + 128, :], o[:], R=[ot])
        self.end(final=True)

    def phase_attn(self, L, need_ctx, mode):
        self.begin()
        d = self.d
        p = "l%d_" % L
        da = mode == "da"
        DV = 128 if da else 64
        AW = DV + 1
        inv_sqrt = 0.125
        tconst = self.P.tile()
        rotT = self.sb([128, 128])
        cosT = self.sb([128, NL])
        sinT = self.sb([128, NL])
        self.dma("sp", rotT[:], d["rotT"], W=[tconst])
        self.dma("sp", cosT[:], d["cos"], W=[tconst])
        self.dma("sp", sinT[:], d["sin"], W=[tconst])
        if da:
            lam_init = 0.8 - 0.6 * math.exp(-0.3 * L)
            lamv = self.sb([128, 4, 64])
            self.dma("sp", lamv[:], d[p + "lam"], W=[tconst])
            lt = self.sb([128, 2, 64])
            ls = self.sb([128, 2])
            nlam = self.sb([128, 1])
            tl = self.P.tile()
            self.tt("dve", lt[:, 0, :], lamv[:, 0, :], lamv[:, 1, :], ALU.mult, [tconst], [tl])
            self.tt("dve", lt[:, 1, :], lamv[:, 2, :], lamv[:, 3, :], ALU.mult, [tconst], [tl])
            self.P.op("dve", lambda e: e.reduce_sum(out=ls[:], in_=lt[:], axis=AX.X), [tl], [tl])
            self.act(ls[:], ls[:], AF.Exp, [tl], [tl])
            self.tt("dve", nlam[:], ls[:, 1:2], ls[:, 0:1], ALU.subtract, [tl], [tl])
            self.ts("dve", nlam[:], nlam[:], -lam_init, 0.0, ALU.add, ALU.add, [tl], [tl])
        else:
            esink = self.sb([128, 16])
            self.dma("sp", esink[:], d[p + "sink"], W=[tconst])
            self.act(esink[:], esink[:], AF.Exp, [tconst], [tconst])
            swm = self.sb([128, 6, 512], BF16)
            self.dma("sp", swm[:], d["swmask"], W=[tconst])

        wv = d[p + "w_in"].rearrange("(kc kp) f -> kp kc f", kp=128)
        wring = self.ring(2, [128, 8, 384], BF16)
        woring = self.ring(2, [128, 1024], BF16)
        QT = self.sb([128, NT], BF16)
        KT = self.sb([128, NT], BF16)
        Vt = self.sb([128, 18, 128], BF16)
        aT = self.sb([128, NT], BF16)
        tq = [self.P.tile() for _ in range(5)]
        tk = [self.P.tile() for _ in range(5)]
        tv = [self.P.tile() for _ in range(5)]
        ta = [self.P.tile() for _ in range(5)]
        q32r = self.ring(2, [128, 512])
        tmpr = self.ring(4, [128, 512])
        misc = self.ring(2, [128, 512], psum=True)
        sring = self.ring(2, [128, 1024], psum=True)
        accO = self.ps([128, 512])
        accZ = self.ps([128, 512])
        tacc = [self.P.tile(), self.P.tile()]
        ering = self.ring(3, [128, 1024], BF16)
        rring = self.ring(2, [128, 512])
        o0ring = self.ring(2, [128, 512])
        if da:
            oring = self.ring(2, [128, 512])
            sqring = self.ring(2, [128, 512], BF16)
        qblocks = [0, 1, 2, 3] + ([4] if need_ctx else [])
        if da:
            subgc = self.sb([128, 1])
            self.dma("sp", subgc[:], d[p + "subgc"], W=[tconst])
            self.ts("dve", subgc[:], subgc[:], 1.0 - lam_init, 0.0, ALU.mult, ALU.add, [tconst], [tconst])

        def proj_fm(wb, wt, col0, dst, tdst, rope=True, blks=range(5)):
            for bi in blks:
                s, e = BLKS[bi]
                w = e - s
                pb, pt = misc.next()
                for kc in range(8):
                    self.mm(pb[:, :w], wb[:, kc, col0:col0 + 128], self.hT[:, kc, s:e], kc == 0, kc == 7,
                            [wt, self.th[bi]], [pt])
                if bi == 4 or not rope:
                    self.cp("act", dst[:, s:e], pb[:, :w], [pt], [tdst[bi]])
                else:
                    qb, qt = q32r.next()
                    self.cp("act", qb[:, :w], pb[:, :w], [pt], [qt])
                    pb2, pt2 = misc.next()
                    self.mm(pb2[:, :w], rotT[:], qb[:, :w], True, True, [qt, tconst], [pt2])
                    t1, t1t = tmpr.next()
                    self.tt("pool", t1[:, :w], qb[:, :w], cosT[:, s:e], ALU.mult, [qt, tconst], [t1t])
                    t2, t2t = tmpr.next()
                    self.tt("dve", t2[:, :w], pb2[:, :w], sinT[:, s:e], ALU.mult, [pt2, tconst], [t2t])
                    self.tt("dve", dst[:, s:e], t1[:, :w], t2[:, :w], ALU.add, [t1t, t2t], [tdst[bi]])

        def proj_v(wb, wt, col0):
            for g in range(5):
                tts = list(range(g * 4, min(g * 4 + 4, 18)))
                n = len(tts)
                pb, pt = misc.next()
                for i, t_ in enumerate(tts):
                    for kc in range(8):
                        self.mm(pb[:, i * 128:(i + 1) * 128], self.hT[:, kc, t_ * 128:(t_ + 1) * 128],
                                wb[:, kc, col0:col0 + 128], kc == 0, kc == 7, [self.th[blk_of(t_ * 128)], wt], [pt])
                self.cp("act", Vt[:, g * 4:g * 4 + n, :], pb[:, :n * 128].rearrange("p (a b) -> p a b", a=n), [pt], [tv[g]])

        nunits = 8

        def do_proj(u):
            wb, wt = wring.next()
            wob, wot = woring.next()
            self.dma("pool", wob[:], d[p + "w_out"][u * 128:(u + 1) * 128, :], W=[wot])
            if da:
                for i in range(3):
                    self.dma("pool", wb[:, :, i * 128:(i + 1) * 128], wv[:, :, i * 1024 + u * 128:i * 1024 + (u + 1) * 128],
                             W=[wt])
                proj_fm(wb, wt, 0, QT, tq)
                proj_fm(wb, wt, 128, KT, tk)
                proj_v(wb, wt, 256)
            else:
                g = u // 2
                self.dma("pool", wb[:, :, 0:128], wv[:, :, u * 128:(u + 1) * 128], W=[wt])
                proj_fm(wb, wt, 0, QT, tq)
                if u % 2 == 0:
                    for i in range(2):
                        self.dma("pool", wb[:, :, 128 + i * 64:128 + (i + 1) * 64], wv[:, :, 1024 + g * 64:1024 + (g + 1) * 64],
                                 W=[wt])
                        self.dma("pool", wb[:, :, 256 + i * 64:256 + (i + 1) * 64], wv[:, :, 1280 + g * 64:1280 + (g + 1) * 64],
                                 W=[wt])
                    proj_fm(wb, wt, 128, KT, tk)
                    proj_v(wb, wt, 256)
            return wob, wot

        def outproj_block(wpair, bi):
            wob_, wot_ = wpair
            s, e = BLKS[bi]
            w = e - s
            j = 1 if bi == 4 else 0
            for dc in range(8):
                pb, pt = misc.next()
                self.mm(pb[:, :w], wob_[:, dc * 128:(dc + 1) * 128], aT[:, s:e], True, True, [wot_, ta[bi]], [pt])
                self.stt("dve", self.xT[:, dc, s:e], pb[:, :w], self.modv[:, L, 16 + dc, j:j + 1], self.xT[:, dc, s:e],
                         ALU.mult, ALU.add, [pt, self.tx[bi][dc]], [self.tx[bi][dc]])

        pend_w = do_proj(0)
        prev = None
        for u in range(nunits):
            wob, wot = pend_w
            ginfo = {}
            items = []
            for qi in qblocks:
                q0, q1 = BLKS[qi]
                w = q1 - q0
                isctx = qi == 4
                qt0 = q0 // 128
                if isctx:
                    kts = [16, 17]
                elif da:
                    kts = list(range(18))
                else:
                    kts = [k_ for k_ in range(qt0 - 1, qt0 + 5) if 0 <= k_ < 16] + [16, 17]
                ginfo[qi] = (q0, q1, w, isctx, qt0, kts)
                for c in range(2):
                    for pi in range(0, len(kts), 2):
                        items.append((qi, c, pi))

            def issue_S(it):
                qi, c, pi = it
                q0, q1, w, isctx, qt0, kts = ginfo[qi]
                pair = kts[pi:pi + 2]
                sbuf_, st_ = sring.next()
                for i, kt in enumerate(pair):
                    self.mm(sbuf_[:, i * 512:i * 512 + w], KT[c * 64:(c + 1) * 64, kt * 128:(kt + 1) * 128],
                            QT[c * 64:(c + 1) * 64, q0:q1], True, True, [tk[blk_of(kt * 128)], tq[qi]], [st_])
                return sbuf_, st_

            def finish(qi, c, o0):
                q0, q1, w, isctx, qt0, kts = ginfo[qi]
                rb, rt = rring.next()
                self.cp("dve" if da else "act", rb[:, :w], accZ[:, :w], [tacc[1]], [rt])
                if da:
                    if c == 0:
                        ob_, ot_ = o0ring.next()
                        self.cp("dve", ob_[:, :w], accO[:, :w], [tacc[0]], [ot_])
                        self.recip(rb[:, :w], rb[:, :w], [rt], [rt])
                        self.tt("dve", ob_[:, :w], ob_[:, :w], rb[:, :w], ALU.mult, [ot_, rt], [ot_])
                        return (ob_, ot_)
                    ob0, ot0 = o0
                    o1, o1t = oring.next()
                    self.cp("dve", o1[:, :w], accO[:, :w], [tacc[0]], [o1t])
                    self.recip(rb[:, :w], rb[:, :w], [rt], [rt])
                    self.tt("dve", o1[:, :w], o1[:, :w], rb[:, :w], ALU.mult, [o1t, rt], [o1t])
                    self.stt("dve", aT[:, q0:q1], o1[:, :w], nlam[:, 0:1], ob0[:, :w], ALU.mult, ALU.add, [o1t, ot0, tl], [ta[qi]])
                    return None
                hq = 2 * u + c
                ob_, ot_ = o0ring.next()
                self.cp("act", ob_[c * 64:(c + 1) * 64, :w], accO[c * 64:(c + 1) * 64, :w], [tacc[0]], [ot_])
                self.ts("dve", rb[:, :w], rb[:, :w], esink[:, hq:hq + 1], 0.0, ALU.add, ALU.add, [rt, tconst], [rt])
                self.recip(rb[:, :w], rb[:, :w], [rt], [rt])
                self.tt("dve", aT[c * 64:(c + 1) * 64, q0:q1], ob_[c * 64:(c + 1) * 64, :w], rb[c * 64:(c + 1) * 64, :w],
                        ALU.mult, [ot_, rt], [ta[qi]])
                return None

            def subln_head():
                for qi in qblocks:
                    q0, q1, w, isctx, qt0, kts = ginfo[qi]
                    sq, sqt = sqring.next()
                    self.act(sq[:, :w], aT[:, q0:q1], AF.Square, [ta[qi]], [sqt])
                    pb, pt = misc.next()
                    self.mm(pb[:, :w], self.ones_bf[:], sq[:, :w], True, True, [sqt], [pt])
                    rb, rt = (rring, rring, o0ring, o0ring, oring)[len(rstd_jobs)].next()
                    self.cp("dve", rb[:, :w], pb[:, :w], [pt], [rt])
                    rstd_jobs.append((qi, rb, rt))
                for (qi, rb, rt) in rstd_jobs:
                    q0, q1, w, isctx, qt0, kts = ginfo[qi]
                    self.act(rb[:, :w], rb[:, :w], AF.Sqrt, [rt], [rt], bias=EPS, scale=1.0 / 128)
                for (qi, rb, rt) in rstd_jobs:
                    q0, q1, w, isctx, qt0, kts = ginfo[qi]
                    self.recip(rb[:, :w], rb[:, :w], [rt], [rt])
                    self.stt("dve", aT[:, q0:q1], aT[:, q0:q1], subgc[:, 0:1], rb[:, :w], ALU.mult, ALU.mult,
                             [ta[qi], rt, tconst], [ta[qi]])
                del rstd_jobs[:]

            rstd_jobs = []
            pend = issue_S(items[0])
            if prev is not None and da:
                subln_head()
            o0 = None
            for k_it, it in enumerate(items):
                sbuf_, st_ = pend
                if k_it + 1 < len(items):
                    pend = issue_S(items[k_it + 1])
                qi, c, pi = it
                q0, q1, w, isctx, qt0, kts = ginfo[qi]
                pair = kts[pi:pi + 2]
                eb, et = ering.next()
                npair = len(pair)
                if w == 512:
                    self.act(eb[:, :npair * 512], sbuf_[:, :npair * 512], AF.Exp, [st_], [et], scale=inv_sqrt)
                else:
                    self.act(eb[:].rearrange("p (a b) -> p a b", a=2)[:, :npair, :w],
                             sbuf_[:].rearrange("p (a b) -> p a b", a=2)[:, :npair, :w], AF.Exp, [st_], [et],
                             scale=inv_sqrt)
                for i, kt in enumerate(pair):
                    if (not da) and (not isctx) and kt < 16:
                        rel = kt - qt0 + 1
                        self.tt("dve", eb[:, i * 512:(i + 1) * 512], eb[:, i * 512:(i + 1) * 512], swm[:, rel, :],
                                ALU.mult, [et, tconst], [et])
                    self.mm(accO[:, :w], Vt[:, kt, :], eb[:, i * 512:i * 512 + w], kt == kts[0], kt == kts[-1],
                            [et, tv[kt // 4]], [tacc[0]])
                    self.mm(accZ[:, :w], self.ones_bf[:], eb[:, i * 512:i * 512 + w], kt == kts[0], kt == kts[-1],
                            [et], [tacc[1]])
                if pi + 2 >= len(kts):
                    if c == 0 and prev is not None:
                        outproj_block(prev, qi)
                    o0 = finish(qi, c, o0)
            if u + 1 < nunits:
                pend_w = do_proj(u + 1)
            prev = (wob, wot)
        if da:
            subln_head()
        for bi in qblocks:
            outproj_block(prev, bi)
        self.end()
        self.free_hT()


    def phase_mlstm(self, L, need_ctx):
        import os
        MLSTOP = int(os.environ.get('MLSTOP', '0'))
        MLSUB = int(os.environ.get('MLSUB', '9'))
        MLPART = int(os.environ.get('MLPART', '0'))
        d = self.d
        p = "l%d_" % L
        self.P.relax = False
        wv = d[p + "w_in"].rearrange("(kc kp) f -> kp kc f", kp=128)
        oblks = list(range(5 if need_ctx else 4))
        order = {0: [16, 17] + list(range(16)), 1: [17, 16] + list(range(15, -1, -1))}
        for u in range(4):
            if MLSTOP == 9:
                continue
            ues = ExitStack()
            msk = self.sb([128, 2, 128], F32, ues)
            ones32 = self.sb([128, 128], F32, ues)
            gball = self.sb([128, 32], F32, ues)
            mlg = self.sb([128, 256], F32, ues)
            gbu = self.sb([128, 8], F32, ues)
            wo2 = self.sb([128, 2, 1024], BF16, ues)
            QT = self.sb([128, NT], BF16, ues)
            KT = self.sb([128, NT], BF16, ues)
            KV = self.sb([128, 18, 384], BF16, ues)
            sigo = self.sb([128, 18, 256], BF16, ues)
            G = self.sb([128, 18, 8], F32, ues)
            nlf = self.sb([128, 18, 2, 2], F32, ues)
            nb = self.sb([128, 18, 2, 2], F32, ues)
            nbt = self.sb([128, 18, 2, 2], F32, ues)
            cexp = self.sb([128, 18, 2, 2], F32, ues)
            ebt = self.sb([128, 18, 2, 2], F32, ues)
            aK = self.sb([128, 18, 2, 2], F32, ues)
            aKc = self.sb([128, 18, 2], F32, ues)
            self.begin()
            tconst = self.P.tile()
            self.dma("sp", msk[:], d["mlmask"], W=[tconst])
            self.dma("sp", gball[:], d[p + "gate_b"], W=[tconst])
            self.dma("sp", mlg[:], d[p + "mlg"][:, u * 256:(u + 1) * 256], W=[tconst])
            self.memset("pool", ones32[:], 1.0, [], [tconst])
            wb = self.sb([128, 8, 800], BF16)
            wt = self.P.tile()
            self.dma("pool", wb[:, :, 0:128], wv[:, :, u * 128:(u + 1) * 128], W=[wt])
            self.dma("pool", wb[:, :, 128:256], wv[:, :, 512 + u * 128:512 + (u + 1) * 128], W=[wt])
            self.dma("pool", wb[:, :, 256:512], wv[:, :, 1024 + u * 256:1024 + (u + 1) * 256], W=[wt])
            self.dma("pool", wb[:, :, 512:768], wv[:, :, 2048 + u * 256:2048 + (u + 1) * 256], W=[wt])
            wg32 = self.sb([128, 8, 32])
            twg = self.P.tile()
            self.dma("sp", wg32[:], wv[:, :, 3072:3104], W=[twg])
            for kd in range(4):
                self.cp("dve", wb[:, :, 768 + kd * 2:770 + kd * 2], wg32[:, :, kd * 8 + 2 * u:kd * 8 + 2 * u + 2], [twg], [wt])
            wot = self.P.tile()
            for c in range(2):
                self.dma("pool", wo2[:, c, :], d[p + "w_out"][(2 * u + c) * 128:(2 * u + c + 1) * 128, :], W=[wot])
            for kd in range(4):
                self.cp("dve", gbu[:, kd * 2:kd * 2 + 2], gball[:, kd * 8 + 2 * u:kd * 8 + 2 * u + 2], [tconst], [tconst])

            tqk = [self.P.tile() for _ in range(5)]
            tkv = [self.P.tile() for _ in range(18)]
            tso = self.P.tile()
            tg = self.P.tile()
            misc = self.ring(3, [128, 512], psum=True)

            if MLSUB == 0:
                self.end()
                ues.close()
                continue
            for which, dst in ((0, QT), (1, KT)):
                for bi in range(5):
                    s, e = BLKS[bi]
                    w = e - s
                    pb, pt = misc.next()
                    for kc in range(8):
                        self.mm(pb[:, :w], wb[:, kc, which * 128:(which + 1) * 128], self.hT[:, kc, s:e], kc == 0, kc == 7,
                                [wt, self.th[bi]], [pt])
                    self.act(dst[:, s:e], pb[:, :w], AF.Copy, [pt], [tqk[bi]], scale=(0.125 if which == 0 else 1.0))
            if MLSUB == 1:
                self.end()
                ues.close()
                continue
            for t_ in range(18):
                pb, pt = misc.next()
                for kc in range(8):
                    self.mm(pb[:, 0:384], self.hT[:, kc, t_ * 128:(t_ + 1) * 128], wb[:, kc, 128:512], kc == 0, kc == 7,
                            [wt, self.th[blk_of(t_ * 128)]], [pt])
                self.cp("dve", KV[:, t_, :], pb[:, 0:384], [pt], [tkv[t_]])
            if MLSUB == 2:
                self.end()
                ues.close()
                continue
            sgr = self.ring(2, [128, 256])
            for t_ in range(18):
                pb, pt = misc.next()
                for kc in range(8):
                    self.mm(pb[:, 0:264], self.hT[:, kc, t_ * 128:(t_ + 1) * 128], wb[:, kc, 512:776], kc == 0, kc == 7,
                            [wt, self.th[blk_of(t_ * 128)]], [pt])
                if MLSUB != 3:
                    self.tt("dve", G[:, t_, :], pb[:, 256:264], gbu[:], ALU.add, [pt, tconst], [tg])
                if MLSUB != 4:
                    eb_, et_ = sgr.next()
                    self.act(eb_[:], pb[:, 0:256], AF.Exp, [pt, tg], [et_], scale=-1.0)
                    self.ts("dve", eb_[:], eb_[:], 1.0, 0.0, ALU.add, ALU.add, [et_], [et_])
                    self.recip(eb_[:], eb_[:], [et_], [et_])
                    self.cp("act", sigo[:, t_, :], eb_[:], [et_], [tso])
            if MLSTOP == 1:
                self.end()
                ues.close()
                continue
            Gv = G[:].rearrange("p t (k c) -> p t k c", c=2)
            tgp = self.P.tile()
            self.act(nlf[:], Gv[:, :, 1:4:2, :], AF.Exp, [tg], [tgp], scale=-1.0)
            self.act(nlf[:], nlf[:], AF.Ln, [tgp], [tgp], bias=1.0)
            if MLSUB == 5:
                self.end()
                ues.close()
                continue
            for dr in range(2):
                pb, pt = misc.next()
                self.mm(pb[:, 0:36], msk[:, dr, :], nlf[:, :, dr, :], True, True, [tgp, tconst], [pt])
                self.cp("dve", nb[:, :, dr, :], pb[:, 0:36].rearrange("p (t c) -> p t c", c=2), [pt], [tgp])
            pb, pt = misc.next()
            self.mm(pb[:, 0:72], ones32[:], nlf[:].rearrange("p t a c -> p (t a c)"), True, True, [tgp, tconst], [pt])
            self.cp("dve", nbt[:].rearrange("p t a c -> p (t a c)"), pb[:, 0:72], [pt], [tgp])
            if MLSUB == 6:
                self.end()
                ues.close()
                continue
            self.tt("dve", cexp[:], Gv[:, :, 0:4:2, :], nb[:], ALU.add, [tg, tgp], [tgp])
            self.act(cexp[:], cexp[:], AF.Exp, [tgp], [tgp])
            self.act(ebt[:], nb[:], AF.Exp, [tgp], [tgp], scale=-1.0)
            self.act(aK[:], nbt[:], AF.Exp, [tgp], [tgp], scale=-1.0)
            for c in range(2):
                self.cp("dve", aKc[c * 64:(c + 1) * 64, :, :], aK[c * 64:(c + 1) * 64, :, :, c], [tgp], [tgp])

            if MLSTOP == 2:
                self.end()
                ues.close()
                continue
            if MLSUB == 7:
                self.end()
                ues.close()
                continue
            self.end()
            self.begin()
            tconst = self.P.tile()
            tgp = self.P.tile()
            tso = self.P.tile()
            wot = self.P.tile()
            tqk = [self.P.tile() for _ in range(5)]
            tkv = [self.P.tile() for _ in range(18)]
            ths = [self.P.tile() for _ in range(18)]
            hs = self.sb([128, 18, 256])
            self.memset("pool", hs[:], 0.0, [], ths)
            misc = self.ring(3, [128, 512], psum=True)
            outp = self.ring(2, [128, 512], psum=True)
            kvp = self.ring(2, [128, 512], psum=True)
            S = [self.sb([128, 130]) for _ in range(2)]
            Sbf = [self.sb([128, 130], BF16) for _ in range(2)]
            tS = [self.P.tile(), self.P.tile()]
            tSb = [self.P.tile(), self.P.tile()]
            for dr in range(2):
                self.memset("pool", S[dr][:], 0.0, [], [tS[dr]])
                self.memset("pool", Sbf[dr][:], 0.0, [], [tSb[dr]])
            amr = self.ring(4, [128, 2, 128], BF16)
            vpr = self.ring(4, [128, 2, 130], BF16)
            ndr = self.ring(3, [128, 2, 130])
            smr = self.ring(4, [128, 4])
            hsteps = [(order[dr][step], dr) for step in range(18) for dr in range(2)]

            def stageA(t_, dr):
                bi = blk_of(t_ * 128)
                cs = slice(t_ * 128, (t_ + 1) * 128)
                pa, pat = misc.next()
                for c in range(2):
                    self.mm(pa[:, c * 128:(c + 1) * 128], KT[c * 64:(c + 1) * 64, cs], QT[c * 64:(c + 1) * 64, cs], True, True,
                            [tqk[bi]], [pat], sgc=True, pesync=(c == 1))
                am, amt = amr.next()
                self.tt("dve", am[:], pa[:, 0:256].rearrange("p (c t) -> p c t", c=2),
                        msk[:, dr:dr + 1, :].to_broadcast([128, 2, 128]), ALU.mult, [pat, tconst], [amt])
                vp, vpt = vpr.next()
                for c in range(2):
                    self.ts("pool" if c == 0 else "dve", vp[:, c, 0:128], KV[:, t_, 128 + c * 128:256 + c * 128],
                            cexp[:, t_, dr, c:c + 1], 0.0, ALU.mult, ALU.add, [tkv[t_], tgp], [vpt])
                self.cp("dve", vp[:, :, 128], cexp[:, t_, dr, :], [tgp], [vpt])
                pk, pkt = kvp.next()
                for c in range(2):
                    self.mm(pk[:, c * 129:c * 129 + 129], KV[:, t_, 0:128], vp[:, c, 0:129], c == 0, True, [tkv[t_], vpt], [pkt],
                            sgc=True)
                return (am, amt, vp, vpt, pk, pkt)

            def stageB(t_, dr, st):
                am, amt, vp, vpt, pk, pkt = st
                bi = blk_of(t_ * 128)
                cs = slice(t_ * 128, (t_ + 1) * 128)
                po, pot = outp.next()
                for c in range(2):
                    self.mm(po[:, c * 129:c * 129 + 129], am[:, c, :], vp[:, c, 0:129], c == 0, False, [amt, vpt], [pot], sgc=True)
                for c in range(2):
                    self.mm(po[:, c * 129:c * 129 + 129], QT[c * 64:(c + 1) * 64, cs], Sbf[dr][c * 64:(c + 1) * 64, 0:129],
                            False, True, [tqk[bi], tSb[dr]], [pot], sgc=True, pesync=(c == 1))
                for c in range(2):
                    self.tt("dve", S[dr][c * 64:(c + 1) * 64, 0:129], S[dr][c * 64:(c + 1) * 64, 0:129],
                            pk[c * 64:(c + 1) * 64, c * 129:c * 129 + 129], ALU.add, [pkt, tS[dr]], [tS[dr]])
                self.ts("dve", S[dr][:, 0:129], S[dr][:, 0:129], aKc[:, t_, dr:dr + 1], 0.0, ALU.mult, ALU.add,
                        [tS[dr], tgp], [tS[dr]])
                self.cp("act", Sbf[dr][:, 0:129], S[dr][:, 0:129], [tS[dr]], [tSb[dr]])
                nd, ndt = ndr.next()
                for c in range(2):
                    self.act(nd[:, c, 0:129], po[:, c * 129:c * 129 + 129], AF.Copy, [pot, tgp], [ndt],
                             scale=ebt[:, t_, dr, c:c + 1])
                sm, smt = smr.next()
                self.act(sm[:, 0:2], nd[:, :, 128], AF.Abs, [ndt], [smt])
                self.ts("dve", sm[:, 0:2], sm[:, 0:2], 1.0, 0.0, ALU.max, ALU.add, [smt], [smt])
                self.recip(sm[:, 0:2], sm[:, 0:2], [smt], [smt])
                for c in range(2):
                    self.stt("dve", hs[:, t_, c * 128:(c + 1) * 128], nd[:, c, 0:128], sm[:, c:c + 1],
                             hs[:, t_, c * 128:(c + 1) * 128], ALU.mult, ALU.add, [ndt, smt, ths[t_]], [ths[t_]])

            pend = stageA(*hsteps[0])
            for k_hs, (t_, dr) in enumerate(hsteps):
                cur = pend
                if k_hs + 1 < len(hsteps):
                    pend = stageA(*hsteps[k_hs + 1])
                stageB(t_, dr, cur)

            if MLSTOP == 3:
                self.end()
                ues.close()
                continue
            hnr = self.ring(2, [128, 256])
            jr = self.ring(2, [128, 128])
            for t_ in range(18):
                bi = blk_of(t_ * 128)
                if bi not in oblks:
                    continue
                sm, smt = smr.next()
                self.memset("dve", sm[:, 0:2], 0.0, [], [smt])
                for c in range(2):
                    jb, jt = jr.next()
                    self.act(jb[:], hs[:, t_, c * 128:(c + 1) * 128], AF.Square, [ths[t_], smt], [jt, smt], accum=sm[:, c:c + 1])
                self.act(sm[:, 2:4], sm[:, 0:2], AF.Sqrt, [smt], [smt], bias=EPS, scale=1.0 / 128)
                self.recip(sm[:, 2:4], sm[:, 2:4], [smt], [smt])
                hn, hnt = hnr.next()
                for c in range(2):
                    self.stt("dve", hn[:, c * 128:(c + 1) * 128], hs[:, t_, c * 128:(c + 1) * 128], sm[:, 2 + c:3 + c],
                             mlg[:, c * 128:(c + 1) * 128], ALU.mult, ALU.mult, [ths[t_], smt, tconst], [hnt])
                self.tt("pool", hn[:], hn[:], sigo[:, t_, :], ALU.mult, [hnt, tso], [hnt])
                for c in range(2):
                    pb, pt = misc.next()
                    self.mm(pb[:, 0:128], hn[:, c * 128:(c + 1) * 128], self.ident[:], True, True, [hnt], [pt])
                    self.cp("act", (QT if c == 0 else KT)[:, t_ * 128:(t_ + 1) * 128], pb[:, 0:128], [pt], [tqk[bi]])
            for bi in oblks:
                s, e = BLKS[bi]
                w = e - s
                j = 1 if bi == 4 else 0
                for dc in range(8):
                    pb, pt = misc.next()
                    for c in range(2):
                        self.mm(pb[:, :w], wo2[:, c, dc * 128:(dc + 1) * 128], (QT if c == 0 else KT)[:, s:e], c == 0, c == 1, [wot, tqk[bi]], [pt])
                    self.stt("dve", self.xT[:, dc, s:e], pb[:, :w], self.modv[:, L, 16 + dc, j:j + 1], self.xT[:, dc, s:e],
                             ALU.mult, ALU.add, [pt, self.tx[bi][dc]], [self.tx[bi][dc]])
            self.end()
            ues.close()
        self.P.relax = RELAX_SAME
        self.free_hT()


_BF = ml_dtypes.bfloat16


def _pk(v, k):
    return np.ascontiguousarray(np.asarray(v, np.float32).reshape(k, 128).T)


def _consts():
    c = {}
    c["ident"] = np.eye(128, dtype=np.float32)
    R = np.zeros((128, 128), np.float32)
    for dd in range(128):
        i = dd % 32
        if i < 16:
            R[dd, dd + 16] = -1.0
        else:
            R[dd, dd - 16] = 1.0
    c["rotT"] = np.ascontiguousarray(R.T)
    t = np.arange(NL)
    row = (t // 64).astype(np.float32)
    col = (t % 64).astype(np.float32)
    inv = (10000.0 ** (-np.arange(16, dtype=np.float32) / 16)).astype(np.float32)
    cos = np.zeros((128, NL), np.float32)
    sin = np.zeros((128, NL), np.float32)
    for p in range(128):
        dd = p % 64
        j = dd % 16
        pos = row if dd < 32 else col
        ang = (pos * inv[j]).astype(np.float32)
        cos[p] = np.cos(ang)
        sin[p] = np.sin(ang)
    c["cos"] = cos
    c["sin"] = sin
    m = np.zeros((128, 6, 512), np.float32)
    kk = np.arange(128)[:, None]
    qq = np.arange(512)[None, :]
    for r in range(6):
        rel = r - 1
        m[:, r, :] = (np.abs(qq - kk - rel * 128) <= 128)
    c["swmask"] = m.astype(_BF)
    mm_ = np.zeros((128, 2, 128), np.float32)
    ss = np.arange(128)[:, None]
    tt = np.arange(128)[None, :]
    mm_[:, 0, :] = ss <= tt
    mm_[:, 1, :] = ss >= tt
    c["mlmask"] = mm_
    return c


_CACHE = {}


def _get_nc(n_layers=4, debug=False):
    key = (n_layers, debug)
    if key not in _CACHE:
        _CACHE[key] = Builder(n_layers, debug).build()
    return _CACHE[key]


def _shared_inputs(inputs, n_layers):
    sh = dict(_consts())
    sh["fng"] = _pk(inputs["final_norm_g"], 8)
    for L in range(n_layers):
        p = "l%d_" % L
        f = lambda n: np.asarray(inputs[p + n], np.float32)
        sh[p + "ada_w"] = np.ascontiguousarray(f("ada_w"))
        sh[p + "ada_b"] = _pk(f("ada_b"), 48)
        sh[p + "n1g"] = _pk(f("norm1_g"), 8)
        sh[p + "n2g"] = _pk(f("norm2_g"), 8)
        sh[p + "w_up"] = np.ascontiguousarray(f("ffn_w_up"))
        sh[p + "w_down"] = np.ascontiguousarray(f("ffn_w_down"))
        cw = f("ffn_conv_w")
        sh[p + "cw"] = np.ascontiguousarray(cw.reshape(3, 44, 128).transpose(2, 1, 0))
        sh[p + "cb"] = _pk(f("ffn_conv_b"), 44)
        k = KINDS[L]
        if k == 0:
            sh[p + "w_in"] = np.ascontiguousarray(f("da_w_in"))
            lam = np.stack([f("da_lam_q1"), f("da_lam_k1"), f("da_lam_q2"), f("da_lam_k2")], 0)
            sh[p + "lam"] = np.ascontiguousarray(np.broadcast_to(lam[None], (128, 4, 64)))
            sh[p + "subgc"] = np.ascontiguousarray(f("da_subln_g").reshape(128, 1))
            sh[p + "w_out"] = np.ascontiguousarray(f("da_w_out"))
        elif k == 1:
            sh[p + "w_in"] = np.ascontiguousarray(f("ml_w_in"))
            sh[p + "gate_b"] = np.ascontiguousarray(np.broadcast_to(f("ml_gate_b")[None], (128, 32)))
            sh[p + "mlg"] = np.ascontiguousarray(np.broadcast_to(f("ml_norm_g")[None], (128, 1024)))
            sh[p + "w_out"] = np.ascontiguousarray(f("ml_w_out"))
        else:
            sh[p + "w_in"] = np.ascontiguousarray(f("sw_w_in"))
            sh[p + "sink"] = np.ascontiguousarray(np.broadcast_to(f("sw_sink")[None], (128, 16)))
            sh[p + "w_out"] = np.ascontiguousarray(f("sw_w_out"))
    return sh


def _run(inputs, n_layers=4, debug=False):
    nc = _get_nc(n_layers, debug)
    sh = _shared_inputs(inputs, n_layers)
    x = np.asarray(inputs["x"], np.float32)
    c = np.asarray(inputs["c"], np.float32)
    ctx = np.asarray(inputs["ctx"], np.float32)
    c_ctx = np.asarray(inputs["c_ctx"], np.float32)
    in_maps = []
    for b in range(8):
        m = dict(sh)
        m["x"] = np.ascontiguousarray(x[b])
        m["ctx"] = np.ascontiguousarray(ctx[b])
        m["cc"] = np.ascontiguousarray(np.concatenate([_pk(c[b], 8), _pk(c_ctx, 8)], axis=1))
        in_maps.append(m)
    res = run_bass_kernel_spmd(nc, in_maps, core_ids=list(range(8)))
    return np.stack([np.asarray(r["out"], np.float32) for r in res.results], 0)


_INPUT_NAMES = (
    "x", "c", "ctx", "c_ctx",
    "l0_ada_w", "l0_ada_b", "l0_norm1_g", "l0_da_w_in",
    "l0_da_lam_q1", "l0_da_lam_k1", "l0_da_lam_q2", "l0_da_lam_k2",
    "l0_da_subln_g", "l0_da_w_out", "l0_norm2_g", "l0_ffn_w_up",
    "l0_ffn_conv_w", "l0_ffn_conv_b", "l0_ffn_w_down", "l1_ada_w",
    "l1_ada_b", "l1_norm1_g", "l1_ml_w_in", "l1_ml_gate_b",
    "l1_ml_norm_g", "l1_ml_w_out", "l1_norm2_g", "l1_ffn_w_up",
    "l1_ffn_conv_w", "l1_ffn_conv_b", "l1_ffn_w_down", "l2_ada_w",
    "l2_ada_b", "l2_norm1_g", "l2_sw_w_in", "l2_sw_sink",
    "l2_sw_w_out", "l2_norm2_g", "l2_ffn_w_up", "l2_ffn_conv_w",
    "l2_ffn_conv_b", "l2_ffn_w_down", "l3_ada_w", "l3_ada_b",
    "l3_norm1_g", "l3_da_w_in", "l3_da_lam_q1", "l3_da_lam_k1",
    "l3_da_lam_q2", "l3_da_lam_k2", "l3_da_subln_g", "l3_da_w_out",
    "l3_norm2_g", "l3_ffn_w_up", "l3_ffn_conv_w", "l3_ffn_conv_b",
    "l3_ffn_w_down", "final_norm_g",
)


def kernel(**inputs):
    missing = [n for n in _INPUT_NAMES if n not in inputs]
    assert not missing, missing
    return _run(inputs, 4, False)
```

```python
import math
from contextlib import ExitStack
import numpy as np
import ml_dtypes
import concourse.bass as bass
import concourse.mybir as mybir
from concourse.bass_utils import run_bass_kernel_spmd

F32 = mybir.dt.float32
BF16 = mybir.dt.bfloat16
AF = mybir.ActivationFunctionType
ALU = mybir.AluOpType
AX = mybir.AxisListType

ENGS = ("pe", "act", "dve", "pool", "sp")
RELAX_SAME = False
NDMA_SEM = 8
NDMA_Q = {"sp": 8, "pool": 3, "act": 8}


class T:
    __slots__ = ("name", "lw", "rs")

    def __init__(self, name=""):
        self.name = name
        self.lw = None
        self.rs = []


class Op:
    __slots__ = ("eng", "fn", "deps", "dma", "signal", "idx", "cnt", "dslot", "dval", "nosame", "pesync")

    def __init__(self, eng, fn, dma):
        self.eng = eng
        self.fn = fn
        self.deps = []
        self.dma = dma
        self.signal = False
        self.idx = -1
        self.cnt = -1
        self.dslot = -1
        self.dval = -1
        self.nosame = False
        self.pesync = False


class Prog:
    def __init__(self, nc, same_engine_sync=True):
        self.nc = nc
        self.same = same_engine_sync
        self.relax = RELAX_SAME
        self.sem = {}
        self.sem_ctx = []
        for e in ("pe", "act", "dve", "pool"):
            self.sem[e] = self._mksem("s_" + e)
        self.dsem = {}
        for q in ("sp", "pool", "act"):
            self.dsem[q] = [self._mksem("d_%s%d" % (q, i)) for i in range(NDMA_SEM)]
        self.bar = self._mksem("bar")
        self.bar_cnt = 0
        self.cnt = {e: 0 for e in ("pe", "act", "dve", "pool")}
        self.dcnt = {q: 0 for q in ("sp", "pool", "act")}
        self.ops = {e: [] for e in ENGS}
        self.tiles = []
        self.n_instr = 0

    def _mksem(self, name):
        ctx = self.nc.semaphore(name)
        s = ctx.__enter__()
        self.sem_ctx.append(ctx)
        return s

    def close(self):
        for ctx in reversed(self.sem_ctx):
            ctx.__exit__(None, None, None)

    def tile(self, name=""):
        t = T(name)
        self.tiles.append(t)
        return t

    def skip_same(self, o, d):
        if d.eng != o.eng or o.dma or d.dma:
            return False
        if o.eng == "pe":
            return not o.pesync
        return (not self.same) or o.nosame

    def op(self, eng, fn, reads=(), writes=(), dma=False, nosame=False, pesync=False):
        o = Op(eng, fn, dma)
        o.nosame = nosame
        o.pesync = pesync
        deps = []
        for t in reads:
            if t.lw is not None:
                deps.append(t.lw)
        for t in writes:
            if t.lw is not None and (dma or t.lw.dma or t.lw.eng != eng or not self.relax):
                deps.append(t.lw)
            for r in t.rs:
                if dma or r.dma or r.eng != eng or not self.relax:
                    deps.append(r)
        for t in reads:
            t.rs.append(o)
        for t in writes:
            t.lw = o
            t.rs = []
        seen = set()
        best = {}
        for d in deps:
            if d is o or id(d) in seen:
                continue
            seen.add(id(d))
            if d.dma:
                o.deps.append(d)
            else:
                b = best.get(d.eng)
                if b is None or d.idx > b.idx:
                    best[d.eng] = d
        o.deps.extend(best.values())
        o.idx = len(self.ops[eng])
        self.ops[eng].append(o)
        return o

    def dma(self, q, out, in_, reads=(), writes=(), **kw):
        def fn(e):
            return e.dma_start(out=out, in_=in_, **kw)
        return self.op(q, fn, reads, writes, dma=True)

    def flush(self, final=False):
        nc = self.nc
        ops = self.ops
        for e in ENGS:
            for o in ops[e]:
                for d in o.deps:
                    if not d.dma:
                        if self.skip_same(o, d):
                            continue
                        d.signal = True
        for e in ("pe", "act", "dve", "pool"):
            comp = [o for o in ops[e] if not o.dma]
            if comp:
                comp[-1].signal = True
        for e in ("pe", "act", "dve", "pool"):
            c = self.cnt[e]
            for o in ops[e]:
                if o.dma:
                    continue
                if o.signal:
                    c += 1
                o.cnt = c if o.signal else -1
            nxt = None
            for o in reversed(ops[e]):
                if o.dma:
                    continue
                if o.signal:
                    nxt = o.cnt
                else:
                    o.cnt = nxt
            self.cnt[e] = c
        for q in ("sp", "pool", "act"):
            n = self.dcnt[q]
            for o in ops[q]:
                if o.dma:
                    o.dslot = n % NDMA_Q[q]
                    o.dval = 16 * (n // NDMA_Q[q] + 1)
                    n += 1
            self.dcnt[q] = n
        final_cnt = dict(self.cnt)
        final_d = {}
        for q in ("sp", "pool", "act"):
            n = self.dcnt[q]
            final_d[q] = [16 * ((n - s + NDMA_Q[q] - 1) // NDMA_Q[q]) if s < NDMA_Q[q] else 0 for s in range(NDMA_SEM)]
        self.bar_cnt += 1
        bar_val = self.bar_cnt
        prog = self

        def emit(ename, eng):
            waited = {}

            def wait(sem, val, key):
                if waited.get(key, 0) >= val:
                    return
                waited[key] = val
                eng.wait_ge(sem, val)
                prog.n_instr += 1

            for o in ops[ename]:
                for d in o.deps:
                    if d.dma:
                        wait(prog.dsem[d.eng][d.dslot], d.dval, ("d", d.eng, d.dslot))
                    else:
                        if prog.skip_same(o, d):
                            continue
                        wait(prog.sem[d.eng], d.cnt, ("c", d.eng))
                if o.dma:
                    if o.dval > 16:
                        wait(prog.dsem[ename][o.dslot], o.dval - 16, ("d", ename, o.dslot))
                    ins = o.fn(eng)
                    ins.then_inc(prog.dsem[ename][o.dslot], 16)
                else:
                    ins = o.fn(eng)
                    if o.signal:
                        ins.then_inc(prog.sem[ename], 1)
                prog.n_instr += 1
            if ename == "sp":
                for e2 in ("pe", "act", "dve", "pool"):
                    if final_cnt[e2] > 0:
                        wait(prog.sem[e2], final_cnt[e2], ("c", e2))
                for q in ("sp", "pool", "act"):
                    for s in range(NDMA_SEM):
                        if final_d[q][s] > 0:
                            wait(prog.dsem[q][s], final_d[q][s], ("d", q, s))
                eng.sem_inc(prog.bar, 1)
                if final:
                    eng.wait_ge(prog.bar, bar_val)
            else:
                eng.wait_ge(prog.bar, bar_val)

        with nc.Block() as block:
            @block.tensor
            def _(e):
                emit("pe", e)

            @block.scalar
            def _(e):
                emit("act", e)

            @block.vector
            def _(e):
                emit("dve", e)

            @block.gpsimd
            def _(e):
                emit("pool", e)

            @block.sync
            def _(e):
                emit("sp", e)

        self.ops = {e: [] for e in ENGS}
        for t in self.tiles:
            t.lw = None
            t.rs = []
        self.tiles = []

EPS = 1e-6
NT, NL, NCX = 2304, 2048, 256
BLKS = [(0, 512), (512, 1024), (1024, 1536), (1536, 2048), (2048, 2304)]
CBLKS = [(0, 410), (410, 820), (820, 1230), (1230, 1640), (1640, 2048), (2048, 2304)]
KINDS = [0, 1, 2, 0]
FFN = 2816
NFC = 22


def blk_of(t):
    return min(t // 512, 4)


def blks_overlap(lo, hi):
    return sorted(set(blk_of(t) for t in (lo, hi - 1)) | set(range(blk_of(lo), blk_of(hi - 1) + 1)))


class Ring:
    def __init__(self, P, bufs):
        self.P = P
        self.b = list(bufs)
        self.t = [None] * len(self.b)
        self.i = 0

    def next(self):
        k = self.i % len(self.b)
        self.i += 1
        if self.t[k] is None:
            self.t[k] = self.P.tile()
        return self.b[k], self.t[k]


class Builder:
    def __init__(self, n_layers=4, debug=False):
        self.nc = bass.Bass("TRN2", target_bir_lowering=False)
        self.P = Prog(self.nc)
        self.uid = 0
        self.n_layers = n_layers
        self.debug = debug
        self.es = None

    def din(self, name, shape, dt=F32):
        return self.nc.dram_tensor(name, list(shape), dt, kind="ExternalInput").ap()

    def sb(self, shape, dt=F32, es=None):
        self.uid += 1
        return (es or self.es).enter_context(self.nc.sbuf_tensor("sb%d" % self.uid, list(shape), dt))

    def ps(self, shape, dt=F32, es=None):
        self.uid += 1
        return (es or self.es).enter_context(self.nc.psum_tensor("ps%d" % self.uid, list(shape), dt))

    def ring(self, n, shape, dt=F32, psum=False):
        return Ring(self.P, [(self.ps if psum else self.sb)(shape, dt) for _ in range(n)])

    def mm(self, out, lhsT, rhs, start, stop, R, W, sgc=False, pesync=False):
        if sgc:
            self.P.op("pe", lambda e: e.matmul(out, lhsT, rhs, start=start, stop=stop, skip_group_check=True), R, W,
                      pesync=pesync)
        else:
            self.P.op("pe", lambda e: e.matmul(out, lhsT, rhs, start=start, stop=stop), R, W, pesync=pesync)

    def act(self, out, in_, func, R, W, bias=0.0, scale=1.0, accum=None):
        if accum is None:
            self.P.op("act", lambda e: e.activation(out=out, in_=in_, func=func, bias=bias, scale=scale), R, W)
        else:
            self.P.op("act", lambda e: e.activation(out=out, in_=in_, func=func, bias=bias, scale=scale,
                                                    accum_out=accum), R, W)

    def tt(self, eng, out, in0, in1, op, R, W):
        self.P.op(eng, lambda e: e.tensor_tensor(out=out, in0=in0, in1=in1, op=op), R, W)

    def ts(self, eng, out, in0, s1, s2, op0, op1, R, W):
        self.P.op(eng, lambda e: e.tensor_scalar(out=out, in0=in0, scalar1=s1, scalar2=s2, op0=op0, op1=op1), R, W)

    def stt(self, eng, out, in0, sc, in1, op0, op1, R, W):
        self.P.op(eng, lambda e: e.scalar_tensor_tensor(out=out, in0=in0, scalar=sc, in1=in1, op0=op0, op1=op1), R, W)

    def cp(self, eng, out, in_, R, W):
        if eng == "act":
            self.P.op("act", lambda e: e.activation(out=out, in_=in_, func=AF.Copy), R, W)
        else:
            self.P.op(eng, lambda e: e.tensor_copy(out=out, in_=in_), R, W)

    def memset(self, eng, ap, val, R, W):
        self.P.op(eng, lambda e: e.memset(ap, val), R, W)

    def recip(self, out, in_, R, W):
        self.P.op("dve", lambda e: e.reciprocal(out=out, in_=in_), R, W)

    def dma(self, q, out, in_, R=(), W=()):
        self.P.dma(q, out, in_, R, W)

    def begin(self):
        self.es = ExitStack()
        self.tx = [[self.P.tile() for _ in range(8)] for _ in range(5)]
        self.th = [self.P.tile() for _ in range(5)]

    def end(self, final=False):
        self.P.flush(final=final)
        self.es.close()
        self.es = None

    def txall(self, bi):
        return list(self.tx[bi])

    def build(self):
        nc = self.nc
        d = {}
        d["x"] = self.din("x", [NL, 1024])
        d["ctx"] = self.din("ctx", [NCX, 1024])
        d["cc"] = self.din("cc", [128, 16])
        d["ident"] = self.din("ident", [128, 128])
        d["rotT"] = self.din("rotT", [128, 128])
        d["cos"] = self.din("cos", [128, NL])
        d["sin"] = self.din("sin", [128, NL])
        d["fng"] = self.din("fng", [128, 8])
        d["swmask"] = self.din("swmask", [128, 6, 512], BF16)
        d["mlmask"] = self.din("mlmask", [128, 2, 128])
        for L in range(self.n_layers):
            p = "l%d_" % L
            d[p + "ada_w"] = self.din(p + "ada_w", [1024, 6144])
            d[p + "ada_b"] = self.din(p + "ada_b", [128, 48])
            d[p + "n1g"] = self.din(p + "n1g", [128, 8])
            d[p + "n2g"] = self.din(p + "n2g", [128, 8])
            d[p + "w_up"] = self.din(p + "w_up", [1024, 2 * FFN])
            d[p + "w_down"] = self.din(p + "w_down", [FFN, 1024])
            d[p + "cw"] = self.din(p + "cw", [128, 44, 3])
            d[p + "cb"] = self.din(p + "cb", [128, 44])
            k = KINDS[L]
            if k == 0:
                d[p + "w_in"] = self.din(p + "w_in", [1024, 3072])
                d[p + "lam"] = self.din(p + "lam", [128, 4, 64])
                d[p + "subgc"] = self.din(p + "subgc", [128, 1])
            elif k == 1:
                d[p + "w_in"] = self.din(p + "w_in", [1024, 3104])
                d[p + "gate_b"] = self.din(p + "gate_b", [128, 32])
                d[p + "mlg"] = self.din(p + "mlg", [128, 1024])
            else:
                d[p + "w_in"] = self.din(p + "w_in", [1024, 1536])
                d[p + "sink"] = self.din(p + "sink", [128, 16])
            d[p + "w_out"] = self.din(p + "w_out", [1024, 1024])
        if self.debug:
            self.out_d = nc.dram_tensor("out", [NT, 1024], F32, kind="ExternalOutput").ap()
        else:
            self.out_d = nc.dram_tensor("out", [NL, 1024], F32, kind="ExternalOutput").ap()
        self.d = d

        top = ExitStack()
        self.top = top
        self.xT = self.sb([128, 8, NT], F32, top)
        self.ident = self.sb([128, 128], F32, top)
        self.ones_bf = self.sb([128, 128], BF16, top)
        self.modv = self.sb([128, 4, 48, 2], F32, top)
        self.gs1 = self.sb([128, 4, 8, 2], F32, top)
        self.gs2 = self.sb([128, 4, 8, 2], F32, top)
        self.fng = self.sb([128, 8], F32, top)

        self.phase_load()
        self.phase_ada()
        for L in range(self.n_layers):
            need_ctx = L < 3
            k = KINDS[L]
            self.phase_norm(self.gs1, L, 0, list(range(5)))
            if k == 0:
                self.phase_attn(L, need_ctx, "da")
            elif k == 1:
                self.phase_mlstm(L, need_ctx)
            else:
                self.phase_attn(L, need_ctx, "sw")
            oblks = list(range(5 if need_ctx else 4))
            if self.debug == "mix" and L == self.n_layers - 1:
                break
            self.phase_norm(self.gs2, L, 24, oblks)
            self.phase_ffn(L, oblks)
        self.phase_final()
        top.close()
        self.P.close()
        return nc

    def phase_load(self):
        self.begin()
        d = self.d
        xin = self.ring(3, [128, 1024])
        pst = self.ring(4, [128, 512], psum=True)
        tc = self.P.tile()
        self.dma("sp", self.ident[:], d["ident"], W=[tc])
        self.dma("sp", self.fng[:], d["fng"], W=[tc])
        self.memset("dve", self.ones_bf[:], 1.0, [], [tc])
        for tt_ in range(18):
            src = d["x"][tt_ * 128:(tt_ + 1) * 128, :] if tt_ < 16 else d["ctx"][(tt_ - 16) * 128:(tt_ - 15) * 128, :]
            xb, xt_ = xin.next()
            self.dma("sp", xb[:], src, W=[xt_])
            bi = blk_of(tt_ * 128)
            for half in range(2):
                pb, pt = pst.next()
                for c4 in range(4):
                    c = half * 4 + c4
                    self.mm(pb[:, c4 * 128:(c4 + 1) * 128], xb[:, c * 128:(c + 1) * 128], self.ident[:], True, True,
                            [xt_, tc], [pt])
                self.cp("dve" if half == 0 else "act",
                        self.xT[:, half * 4:(half + 1) * 4, tt_ * 128:(tt_ + 1) * 128],
                        pb[:].rearrange("p (c t) -> p c t", c=4), [pt], self.tx[bi][half * 4:(half + 1) * 4])
        self.end()

    def phase_ada(self):
        self.begin()
        d = self.d
        cc = self.sb([128, 16])
        scT = self.sb([128, 16], BF16)
        tcc = self.P.tile()
        self.dma("sp", cc[:], d["cc"], W=[tcc])
        tsc = self.P.tile()
        self.act(scT[:], cc[:], AF.Silu, [tcc], [tsc])
        wring = self.ring(2, [128, 8, 768], BF16)
        pmod = self.ring(2, [128, 512], psum=True)
        adab = self.sb([128, 4, 48])
        n1g = self.sb([128, 4, 8])
        n2g = self.sb([128, 4, 8])
        tsm = self.P.tile()
        for L in range(self.n_layers):
            p = "l%d_" % L
            self.dma("sp", adab[:, L, :], d[p + "ada_b"], W=[tsm])
            self.dma("sp", n1g[:, L, :], d[p + "n1g"], W=[tsm])
            self.dma("sp", n2g[:, L, :], d[p + "n2g"], W=[tsm])
        for L in range(self.n_layers):
            p = "l%d_" % L
            wv = d[p + "ada_w"].rearrange("(kc kp) f -> kp kc f", kp=128)
            pb, pt = pmod.next()
            for fg in range(8):
                wb, wt = wring.next()
                self.dma("pool", wb[:], wv[:, :, fg * 768:(fg + 1) * 768], W=[wt])
                for f6 in range(6):
                    f = fg * 6 + f6
                    for kc in range(8):
                        self.mm(pb[:, f * 2:(f + 1) * 2], wb[:, kc, f6 * 128:(f6 + 1) * 128], scT[:, kc:16:8],
                                kc == 0, kc == 7, [wt, tsc], [pt])
            tm = self.P.tile()
            for j in range(2):
                self.tt("dve", self.modv[:, L, :, j], pb[:, j:96:2], adab[:, L, :], ALU.add, [pt, tsm], [tm])
            for j in range(2):
                self.stt("dve", self.gs1[:, L, :, j], self.modv[:, L, 8:16, j], 1.0, n1g[:, L, :], ALU.add, ALU.mult,
                         [tm, tsm], [tm])
                self.stt("dve", self.gs2[:, L, :, j], self.modv[:, L, 32:40, j], 1.0, n2g[:, L, :], ALU.add, ALU.mult,
                         [tm, tsm], [tm])
        self.end()

    def alloc_hT(self):
        self.hes = ExitStack()
        self.hT = self.sb([128, 8, NT], BF16, self.hes)

    def free_hT(self):
        self.hes.close()

    def phase_norm(self, gs, L, shift_base, blks):
        self.alloc_hT()
        self.begin()
        sq = self.ring(2, [128, 8, 512], BF16)
        pss = self.ring(2, [128, 512], psum=True)
        rr = self.ring(2, [128, 512])
        tmp = self.ring(4, [128, 512])
        for bi in blks:
            s, e = BLKS[bi]
            w = e - s
            j = 1 if bi == 4 else 0
            sqb, sqt = sq.next()
            self.act(sqb[:, :, :w], self.xT[:, :, s:e], AF.Square, self.txall(bi), [sqt])
            pb, pt = pss.next()
            for c in range(8):
                self.mm(pb[:, :w], self.ones_bf[:], sqb[:, c, :w], c == 0, c == 7, [sqt], [pt])
            rb, rt = rr.next()
            self.act(rb[:, :w], pb[:, :w], AF.Sqrt, [pt], [rt], bias=EPS, scale=1.0 / 1024)
            self.recip(rb[:, :w], rb[:, :w], [rt], [rt])
            for c in range(8):
                tb, tt_ = tmp.next()
                self.stt("dve", tb[:, :w], self.xT[:, c, s:e], gs[:, L, c, j:j + 1], rb[:, :w], ALU.mult, ALU.mult,
                         [self.tx[bi][c], rt], [tt_])
                self.act(self.hT[:, c, s:e], tb[:, :w], AF.Identity, [tt_], [self.th[bi]],
                         bias=self.modv[:, L, shift_base + c, j:j + 1])
        self.end()

    def phase_ffn(self, L, oblks):
        self.begin()
        d = self.d
        p = "l%d_" % L
        cblks = CBLKS if 4 in oblks else CBLKS[:5]
        cw = self.sb([128, 44, 3])
        cb = self.sb([128, 44])
        tcw = self.P.tile()
        self.dma("sp", cw[:], d[p + "cw"], W=[tcw])
        self.dma("sp", cb[:], d[p + "cb"], W=[tcw])
        wuv = d[p + "w_up"].rearrange("(kc kp) f -> kp kc f", kp=128)
        wuring = self.ring(2, [128, 8, 512], BF16)
        GS = 4
        groups = [list(range(a, min(a + GS, NFC))) for a in range(0, NFC, GS)]
        zring = self.ring(2, [128, GS, NT], BF16)
        wdring = self.ring(2, [128, GS, 1024], BF16)
        psr = self.ring(4, [128, 512], psum=True)
        pdr = self.ring(3, [128, 512], psum=True)
        vr = self.ring(6, [128, 512])
        for grp in groups:
            G = len(grp)
            wd, twd = wdring.next()
            zb, _ = zring.next()
            tz = [[self.P.tile() for _ in cblks] for _ in range(G)]
            for gi, jf in enumerate(grp):
                self.dma("pool", wd[:, gi, :], d[p + "w_down"][jf * 128:(jf + 1) * 128, :], W=[twd])
            its = []
            for gi, jf in enumerate(grp):
                for cbi in range(len(cblks)):
                    its.append((gi, jf, cbi))
            wcur = {}

            def stage1(it):
                gi, jf, cbi = it
                if cbi == 0 and gi % 2 == 0:
                    wub, wut = wuring.next()
                    self.dma("pool", wub[:, :, 0:256], wuv[:, :, jf * 128:(jf + 2) * 128], W=[wut])
                    self.dma("pool", wub[:, :, 256:512], wuv[:, :, FFN + jf * 128:FFN + (jf + 2) * 128], W=[wut])
                    wcur[jf] = (wub, wut)
                    wcur[jf + 1] = (wub, wut)
                wub, wut = wcur[jf]
                qa = (gi % 2) * 128
                qg = 256 + (gi % 2) * 128
                s, e = cblks[cbi]
                seq0, seq1 = (0, NL) if s < NL else (NL, NT)
                lo = max(s - 1, seq0)
                hi = min(e + 1, seq1)
                n = hi - lo
                m = e - s
                hb = [self.th[b_] for b_ in blks_overlap(lo, hi)]
                pa, pat = psr.next()
                for kc in range(8):
                    self.mm(pa[:, :n], wub[:, kc, qa:qa + 128], self.hT[:, kc, lo:hi], kc == 0, kc == 7, [wut] + hb, [pat])
                pg, pgt = psr.next()
                for kc in range(8):
                    self.mm(pg[:, :n], wub[:, kc, qg:qg + 128], self.hT[:, kc, lo:hi], kc == 0, kc == 7, [wut] + hb, [pgt])
                va, vat = vr.next()
                vg, vgt = vr.next()
                ag = ((pa, pat, va, vat, jf), (pg, pgt, vg, vgt, NFC + jf))
                o0 = s - lo
                for (pp, ppt, vv, vvt, fi) in ag:
                    self.act(vv[:, :m], pp[:, o0:o0 + m], AF.Identity, [ppt, tcw], [vvt],
                             bias=cb[:, fi:fi + 1], scale=cw[:, fi, 1:2])
                return (ag, s, e, lo, m, seq0, seq1)

            def stage2(st):
                ag, s, e, lo, m, seq0, seq1 = st
                ls = max(s, seq0 + 1)
                re_ = min(e, seq1 - 1)
                for (pp, ppt, vv, vvt, fi) in ag:
                    self.stt("dve", vv[:, ls - s:m], pp[:, ls - 1 - lo:e - 1 - lo], cw[:, fi, 0:1], vv[:, ls - s:m],
                             ALU.mult, ALU.add, [ppt, vvt, tcw], [vvt])
                for (pp, ppt, vv, vvt, fi) in ag:
                    self.stt("dve", vv[:, 0:re_ - s], pp[:, s + 1 - lo:re_ + 1 - lo], cw[:, fi, 2:3], vv[:, 0:re_ - s],
                             ALU.mult, ALU.add, [ppt, vvt, tcw], [vvt])

            def stage3(it, st):
                gi, jf, cbi = it
                ag, s, e, lo, m, seq0, seq1 = st
                (pa, pat, va, vat, _), (pg, pgt, vg, vgt, _) = ag
                self.act(vg[:, :m], vg[:, :m], AF.Silu, [vgt], [vgt])
                self.tt("pool", zb[:, gi, s:e], va[:, :m], vg[:, :m], ALU.mult, [vat, vgt], [tz[gi][cbi]])

            pend = stage1(its[0])
            for k_it, it in enumerate(its):
                cur = pend
                if k_it + 1 < len(its):
                    pend = stage1(its[k_it + 1])
                stage2(cur)
                stage3(it, cur)
            for bi in oblks:
                s, e = BLKS[bi]
                w = e - s
                j = 1 if bi == 4 else 0
                zt = []
                for gi in range(G):
                    for cbi, (cs, ce) in enumerate(cblks):
                        if cs < e and ce > s:
                            zt.append(tz[gi][cbi])
                for dc in range(8):
                    pb, pt = pdr.next()
                    for gi in range(G):
                        self.mm(pb[:, :w], wd[:, gi, dc * 128:(dc + 1) * 128], zb[:, gi, s:e], gi == 0, gi == G - 1,
                                [twd] + zt, [pt])
                    self.stt("dve", self.xT[:, dc, s:e], pb[:, :w], self.modv[:, L, 40 + dc, j:j + 1],
                             self.xT[:, dc, s:e], ALU.mult, ALU.add, [pt, self.tx[bi][dc]], [self.tx[bi][dc]])
        self.end()
        self.free_hT()

    def phase_final(self):
        self.begin()
        sq = self.ring(2, [128, 8, 512], BF16)
        pss = self.ring(2, [128, 512], psum=True)
        rr = self.ring(2, [128, 512])
        yb = self.ring(2, [128, 8, 512])
        ptr = self.ring(4, [128, 512], psum=True)
        ob = self.ring(3, [128, 1024])
        nb = 5 if self.debug else 4
        for bi in range(nb):
            s, e = BLKS[bi]
            w = e - s
            y, yt = yb.next()
            if self.debug:
                for c in range(8):
                    self.cp("dve", y[:, c, :w], self.xT[:, c, s:e], [self.tx[bi][c]], [yt])
            else:
                sqb, sqt = sq.next()
                self.act(sqb[:, :, :w], self.xT[:, :, s:e], AF.Square, self.txall(bi), [sqt])
                pb, pt = pss.next()
                for c in range(8):
                    self.mm(pb[:, :w], self.ones_bf[:], sqb[:, c, :w], c == 0, c == 7, [sqt], [pt])
                rb, rt = rr.next()
                self.act(rb[:, :w], pb[:, :w], AF.Sqrt, [pt], [rt], bias=EPS, scale=1.0 / 1024)
                self.recip(rb[:, :w], rb[:, :w], [rt], [rt])
                for c in range(8):
                    self.stt("dve", y[:, c, :w], self.xT[:, c, s:e], self.fng[:, c:c + 1], rb[:, :w], ALU.mult, ALU.mult,
                             [self.tx[bi][c], rt], [yt])
            for q in range(w // 128):
                o, ot = ob.next()
                for half in range(2):
                    pb2, pt2 = ptr.next()
                    for c4 in range(4):
                        c = half * 4 + c4
                        self.mm(pb2[:, c4 * 128:(c4 + 1) * 128], y[:, c, q * 128:(q + 1) * 128], self.ident[:], True, True,
                                [yt], [pt2])
                    self.cp("act" if half == 0 else "dve", o[:, half * 512:(half + 1) * 512], pb2[:], [pt2], [ot])
                r0 = s + q * 128
                self.dma("sp", self.out_d[r0:r0 + 128, :], o[:], R=[ot])
        self.end(final=True)

    def phase_attn(self, L, need_ctx, mode):
        self.begin()
        d = self.d
        p = "l%d_" % L
        da = mode == "da"
        DV = 128 if da else 64
        AW = DV + 1
        inv_sqrt = 0.125
        tconst = self.P.tile()
        rotT = self.sb([128, 128])
        cosT = self.sb([128, NL])
        sinT = self.sb([128, NL])
        self.dma("sp", rotT[:], d["rotT"], W=[tconst])
        self.dma("sp", cosT[:], d["cos"], W=[tconst])
        self.dma("sp", sinT[:], d["sin"], W=[tconst])
        if da:
            lam_init = 0.8 - 0.6 * math.exp(-0.3 * L)
            lamv = self.sb([128, 4, 64])
            self.dma("sp", lamv[:], d[p + "lam"], W=[tconst])
            lt = self.sb([128, 2, 64])
            ls = self.sb([128, 2])
            nlam = self.sb([128, 1])
            tl = self.P.tile()
            self.tt("dve", lt[:, 0, :], lamv[:, 0, :], lamv[:, 1, :], ALU.mult, [tconst], [tl])
            self.tt("dve", lt[:, 1, :], lamv[:, 2, :], lamv[:, 3, :], ALU.mult, [tconst], [tl])
            self.P.op("dve", lambda e: e.reduce_sum(out=ls[:], in_=lt[:], axis=AX.X), [tl], [tl])
            self.act(ls[:], ls[:], AF.Exp, [tl], [tl])
            self.tt("dve", nlam[:], ls[:, 1:2], ls[:, 0:1], ALU.subtract, [tl], [tl])
            self.ts("dve", nlam[:], nlam[:], -lam_init, 0.0, ALU.add, ALU.add, [tl], [tl])
        else:
            esink = self.sb([128, 16])
            self.dma("sp", esink[:], d[p + "sink"], W=[tconst])
            self.act(esink[:], esink[:], AF.Exp, [tconst], [tconst])
            swm = self.sb([128, 6, 512], BF16)
            self.dma("sp", swm[:], d["swmask"], W=[tconst])

        wv = d[p + "w_in"].rearrange("(kc kp) f -> kp kc f", kp=128)
        wring = self.ring(2, [128, 8, 384], BF16)
        woring = self.ring(2, [128, 1024], BF16)
        QT = self.sb([128, NT], BF16)
        KT = self.sb([128, NT], BF16)
        Vt = self.sb([128, 18, 128], BF16)
        aT = self.sb([128, NT], BF16)
        tq = [self.P.tile() for _ in range(5)]
        tk = [self.P.tile() for _ in range(5)]
        tv = [self.P.tile() for _ in range(5)]
        ta = [self.P.tile() for _ in range(5)]
        q32r = self.ring(2, [128, 512])
        tmpr = self.ring(4, [128, 512])
        misc = self.ring(2, [128, 512], psum=True)
        sring = self.ring(2, [128, 1024], psum=True)
        accO = self.ps([128, 512])
        accZ = self.ps([128, 512])
        tacc = [self.P.tile(), self.P.tile()]
        ering = self.ring(3, [128, 1024], BF16)
        rring = self.ring(2, [128, 512])
        o0ring = self.ring(2, [128, 512])
        if da:
            oring = self.ring(2, [128, 512])
            sqring = self.ring(2, [128, 512], BF16)
        qblocks = [0, 1, 2, 3] + ([4] if need_ctx else [])
        if da:
            subgc = self.sb([128, 1])
            self.dma("sp", subgc[:], d[p + "subgc"], W=[tconst])
            self.ts("dve", subgc[:], subgc[:], 1.0 - lam_init, 0.0, ALU.mult, ALU.add, [tconst], [tconst])

        def proj_fm(wb, wt, col0, dst, tdst, rope=True, blks=range(5)):
            for bi in blks:
                s, e = BLKS[bi]
                w = e - s
                pb, pt = misc.next()
                for kc in range(8):
                    self.mm(pb[:, :w], wb[:, kc, col0:col0 + 128], self.hT[:, kc, s:e], kc == 0, kc == 7,
                            [wt, self.th[bi]], [pt])
                if bi == 4 or not rope:
                    self.cp("act", dst[:, s:e], pb[:, :w], [pt], [tdst[bi]])
                else:
                    qb, qt = q32r.next()
                    self.cp("act", qb[:, :w], pb[:, :w], [pt], [qt])
                    pb2, pt2 = misc.next()
                    self.mm(pb2[:, :w], rotT[:], qb[:, :w], True, True, [qt, tconst], [pt2])
                    t1, t1t = tmpr.next()
                    self.tt("pool", t1[:, :w], qb[:, :w], cosT[:, s:e], ALU.mult, [qt, tconst], [t1t])
                    t2, t2t = tmpr.next()
                    self.tt("dve", t2[:, :w], pb2[:, :w], sinT[:, s:e], ALU.mult, [pt2, tconst], [t2t])
                    self.tt("dve", dst[:, s:e], t1[:, :w], t2[:, :w], ALU.add, [t1t, t2t], [tdst[bi]])

        def proj_v(wb, wt, col0):
            for g in range(5):
                tts = list(range(g * 4, min(g * 4 + 4, 18)))
                n = len(tts)
                pb, pt = misc.next()
                for i, t_ in enumerate(tts):
                    for kc in range(8):
                        self.mm(pb[:, i * 128:(i + 1) * 128], self.hT[:, kc, t_ * 128:(t_ + 1) * 128],
                                wb[:, kc, col0:col0 + 128], kc == 0, kc == 7, [self.th[blk_of(t_ * 128)], wt], [pt])
                self.cp("act", Vt[:, g * 4:g * 4 + n, :], pb[:, :n * 128].rearrange("p (a b) -> p a b", a=n), [pt], [tv[g]])

        nunits = 8

        def do_proj(u):
            wb, wt = wring.next()
            wob, wot = woring.next()
            self.dma("pool", wob[:], d[p + "w_out"][u * 128:(u + 1) * 128, :], W=[wot])
            if da:
                for i in range(3):
                    self.dma("pool", wb[:, :, i * 128:(i + 1) * 128], wv[:, :, i * 1024 + u * 128:i * 1024 + (u + 1) * 128],
                             W=[wt])
                proj_fm(wb, wt, 0, QT, tq)
                proj_fm(wb, wt, 128, KT, tk)
                proj_v(wb, wt, 256)
            else:
                g = u // 2
                self.dma("pool", wb[:, :, 0:128], wv[:, :, u * 128:(u + 1) * 128], W=[wt])
                proj_fm(wb, wt, 0, QT, tq)
                if u % 2 == 0:
                    for i in range(2):
                        self.dma("pool", wb[:, :, 128 + i * 64:128 + (i + 1) * 64], wv[:, :, 1024 + g * 64:1024 + (g + 1) * 64],
                                 W=[wt])
                        self.dma("pool", wb[:, :, 256 + i * 64:256 + (i + 1) * 64], wv[:, :, 1280 + g * 64:1280 + (g + 1) * 64],
                                 W=[wt])
                    proj_fm(wb, wt, 128, KT, tk)
                    proj_v(wb, wt, 256)
            return wob, wot

        def outproj_block(wpair, bi):
            wob_, wot_ = wpair
            s, e = BLKS[bi]
            w = e - s
            j = 1 if bi == 4 else 0
            for dc in range(8):
                pb, pt = misc.next()
                self.mm(pb[:, :w], wob_[:, dc * 128:(dc + 1) * 128], aT[:, s:e], True, True, [wot_, ta[bi]], [pt])
                self.stt("dve", self.xT[:, dc, s:e], pb[:, :w], self.modv[:, L, 16 + dc, j:j + 1], self.xT[:, dc, s:e],
                         ALU.mult, ALU.add, [pt, self.tx[bi][dc]], [self.tx[bi][dc]])

        pend_w = do_proj(0)
        prev = None
        for u in range(nunits):
            wob, wot = pend_w
            ginfo = {}
            items = []
            for qi in qblocks:
                q0, q1 = BLKS[qi]
                w = q1 - q0
                isctx = qi == 4
                qt0 = q0 // 128
                if isctx:
                    kts = [16, 17]
                elif da:
                    kts = list(range(18))
                else:
                    kts = [k_ for k_ in range(qt0 - 1, qt0 + 5) if 0 <= k_ < 16] + [16, 17]
                ginfo[qi] = (q0, q1, w, isctx, qt0, kts)
                for c in range(2):
                    for pi in range(0, len(kts), 2):
                        items.append((qi, c, pi))

            def issue_S(it):
                qi, c, pi = it
                q0, q1, w, isctx, qt0, kts = ginfo[qi]
                pair = kts[pi:pi + 2]
                sbuf_, st_ = sring.next()
                for i, kt in enumerate(pair):
                    self.mm(sbuf_[:, i * 512:i * 512 + w], KT[c * 64:(c + 1) * 64, kt * 128:(kt + 1) * 128],
                            QT[c * 64:(c + 1) * 64, q0:q1], True, True, [tk[blk_of(kt * 128)], tq[qi]], [st_])
                return sbuf_, st_

            def finish(qi, c, o0):
                q0, q1, w, isctx, qt0, kts = ginfo[qi]
                rb, rt = rring.next()
                self.cp("dve" if da else "act", rb[:, :w], accZ[:, :w], [tacc[1]], [rt])
                if da:
                    if c == 0:
                        ob_, ot_ = o0ring.next()
                        self.cp("dve", ob_[:, :w], accO[:, :w], [tacc[0]], [ot_])
                        self.recip(rb[:, :w], rb[:, :w], [rt], [rt])
                        self.tt("dve", ob_[:, :w], ob_[:, :w], rb[:, :w], ALU.mult, [ot_, rt], [ot_])
                        return (ob_, ot_)
                    ob0, ot0 = o0
                    o1, o1t = oring.next()
                    self.cp("dve", o1[:, :w], accO[:, :w], [tacc[0]], [o1t])
                    self.recip(rb[:, :w], rb[:, :w], [rt], [rt])
                    self.tt("dve", o1[:, :w], o1[:, :w], rb[:, :w], ALU.mult, [o1t, rt], [o1t])
                    self.stt("dve", aT[:, q0:q1], o1[:, :w], nlam[:, 0:1], ob0[:, :w], ALU.mult, ALU.add, [o1t, ot0, tl], [ta[qi]])
                    return None
                hq = 2 * u + c
                ob_, ot_ = o0ring.next()
                self.cp("act", ob_[c * 64:(c + 1) * 64, :w], accO[c * 64:(c + 1) * 64, :w], [tacc[0]], [ot_])
                self.ts("dve", rb[:, :w], rb[:, :w], esink[:, hq:hq + 1], 0.0, ALU.add, ALU.add, [rt, tconst], [rt])
                self.recip(rb[:, :w], rb[:, :w], [rt], [rt])
                self.tt("dve", aT[c * 64:(c + 1) * 64, q0:q1], ob_[c * 64:(c + 1) * 64, :w], rb[c * 64:(c + 1) * 64, :w],
                        ALU.mult, [ot_, rt], [ta[qi]])
                return None

            def subln_head():
                for qi in qblocks:
                    q0, q1, w, isctx, qt0, kts = ginfo[qi]
                    sq, sqt = sqring.next()
                    self.act(sq[:, :w], aT[:, q0:q1], AF.Square, [ta[qi]], [sqt])
                    pb, pt = misc.next()
                    self.mm(pb[:, :w], self.ones_bf[:], sq[:, :w], True, True, [sqt], [pt])
                    rb, rt = (rring, rring, o0ring, o0ring, oring)[len(rstd_jobs)].next()
                    self.cp("dve", rb[:, :w], pb[:, :w], [pt], [rt])
                    rstd_jobs.append((qi, rb, rt))
                for (qi, rb, rt) in rstd_jobs:
                    q0, q1, w, isctx, qt0, kts = ginfo[qi]
                    self.act(rb[:, :w], rb[:, :w], AF.Sqrt, [rt], [rt], bias=EPS, scale=1.0 / 128)
                for (qi, rb, rt) in rstd_jobs:
                    q0, q1, w, isctx, qt0, kts = ginfo[qi]
                    self.recip(rb[:, :w], rb[:, :w], [rt], [rt])
                    self.stt("dve", aT[:, q0:q1], aT[:, q0:q1], subgc[:, 0:1], rb[:, :w], ALU.mult, ALU.mult,
                             [ta[qi], rt, tconst], [ta[qi]])
                del rstd_jobs[:]

            rstd_jobs = []
            pend = issue_S(items[0])
            if prev is not None and da:
                subln_head()
            o0 = None
            for k_it, it in enumerate(items):
                sbuf_, st_ = pend
                if k_it + 1 < len(items):
                    pend = issue_S(items[k_it + 1])
                qi, c, pi = it
                q0, q1, w, isctx, qt0, kts = ginfo[qi]
                pair = kts[pi:pi + 2]
                eb, et = ering.next()
                npair = len(pair)
                if w == 512:
                    self.act(eb[:, :npair * 512], sbuf_[:, :npair * 512], AF.Exp, [st_], [et], scale=inv_sqrt)
                else:
                    self.act(eb[:].rearrange("p (a b) -> p a b", a=2)[:, :npair, :w],
                             sbuf_[:].rearrange("p (a b) -> p a b", a=2)[:, :npair, :w], AF.Exp, [st_], [et],
                             scale=inv_sqrt)
                for i, kt in enumerate(pair):
                    if (not da) and (not isctx) and kt < 16:
                        rel = kt - qt0 + 1
                        self.tt("dve", eb[:, i * 512:(i + 1) * 512], eb[:, i * 512:(i + 1) * 512], swm[:, rel, :],
                                ALU.mult, [et, tconst], [et])
                    self.mm(accO[:, :w], Vt[:, kt, :], eb[:, i * 512:i * 512 + w], kt == kts[0], kt == kts[-1],
                            [et, tv[kt // 4]], [tacc[0]])
                    self.mm(accZ[:, :w], self.ones_bf[:], eb[:, i * 512:i * 512 + w], kt == kts[0], kt == kts[-1],
                            [et], [tacc[1]])
                if pi + 2 >= len(kts):
                    if c == 0 and prev is not None:
                        outproj_block(prev, qi)
                    o0 = finish(qi, c, o0)
            if u + 1 < nunits:
                pend_w = do_proj(u + 1)
            prev = (wob, wot)
        if da:
            subln_head()
        for bi in qblocks:
            outproj_block(prev, bi)
        self.end()
        self.free_hT()


    def phase_mlstm(self, L, need_ctx):
        import os
        MLSTOP = int(os.environ.get('MLSTOP', '0'))
        MLSUB = int(os.environ.get('MLSUB', '9'))
        MLPART = int(os.environ.get('MLPART', '0'))
        d = self.d
        p = "l%d_" % L
        self.P.relax = False
        wv = d[p + "w_in"].rearrange("(kc kp) f -> kp kc f", kp=128)
        oblks = list(range(5 if need_ctx else 4))
        order = {0: [16, 17] + list(range(16)), 1: [17, 16] + list(range(15, -1, -1))}
        for u in range(4):
            if MLSTOP == 9:
                continue
            ues = ExitStack()
            msk = self.sb([128, 2, 128], F32, ues)
            ones32 = self.sb([128, 128], F32, ues)
            gball = self.sb([128, 32], F32, ues)
            mlg = self.sb([128, 256], F32, ues)
            gbu = self.sb([128, 8], F32, ues)
            wo2 = self.sb([128, 2, 1024], BF16, ues)
            QT = self.sb([128, NT], BF16, ues)
            KT = self.sb([128, NT], BF16, ues)
            KV = self.sb([128, 18, 384], BF16, ues)
            sigo = self.sb([128, 18, 256], BF16, ues)
            G = self.sb([128, 18, 8], F32, ues)
            nlf = self.sb([128, 18, 2, 2], F32, ues)
            nb = self.sb([128, 18, 2, 2], F32, ues)
            nbt = self.sb([128, 18, 2, 2], F32, ues)
            cexp = self.sb([128, 18, 2, 2], F32, ues)
            ebt = self.sb([128, 18, 2, 2], F32, ues)
            aK = self.sb([128, 18, 2, 2], F32, ues)
            aKc = self.sb([128, 18, 2], F32, ues)
            self.begin()
            tconst = self.P.tile()
            self.dma("sp", msk[:], d["mlmask"], W=[tconst])
            self.dma("sp", gball[:], d[p + "gate_b"], W=[tconst])
            self.dma("sp", mlg[:], d[p + "mlg"][:, u * 256:(u + 1) * 256], W=[tconst])
            self.memset("pool", ones32[:], 1.0, [], [tconst])
            wb = self.sb([128, 8, 800], BF16)
            wt = self.P.tile()
            self.dma("pool", wb[:, :, 0:128], wv[:, :, u * 128:(u + 1) * 128], W=[wt])
            self.dma("pool", wb[:, :, 128:256], wv[:, :, 512 + u * 128:512 + (u + 1) * 128], W=[wt])
            self.dma("pool", wb[:, :, 256:512], wv[:, :, 1024 + u * 256:1024 + (u + 1) * 256], W=[wt])
            self.dma("pool", wb[:, :, 512:768], wv[:, :, 2048 + u * 256:2048 + (u + 1) * 256], W=[wt])
            wg32 = self.sb([128, 8, 32])
            twg = self.P.tile()
            self.dma("sp", wg32[:], wv[:, :, 3072:3104], W=[twg])
            for kd in range(4):
                self.cp("dve", wb[:, :, 768 + kd * 2:770 + kd * 2], wg32[:, :, kd * 8 + 2 * u:kd * 8 + 2 * u + 2], [twg], [wt])
            wot = self.P.tile()
            for c in range(2):
                self.dma("pool", wo2[:, c, :], d[p + "w_out"][(2 * u + c) * 128:(2 * u + c + 1) * 128, :], W=[wot])
            for kd in range(4):
                self.cp("dve", gbu[:, kd * 2:kd * 2 + 2], gball[:, kd * 8 + 2 * u:kd * 8 + 2 * u + 2], [tconst], [tconst])

            tqk = [self.P.tile() for _ in range(5)]
            tkv = [self.P.tile() for _ in range(18)]
            tso = self.P.tile()
            tg = self.P.tile()
            misc = self.ring(3, [128, 512], psum=True)

            if MLSUB == 0:
                self.end()
                ues.close()
                continue
            for which, dst in ((0, QT), (1, KT)):
                for bi in range(5):
                    s, e = BLKS[bi]
                    w = e - s
                    pb, pt = misc.next()
                    for kc in range(8):
                        self.mm(pb[:, :w], wb[:, kc, which * 128:(which + 1) * 128], self.hT[:, kc, s:e], kc == 0, kc == 7,
                                [wt, self.th[bi]], [pt])
                    self.act(dst[:, s:e], pb[:, :w], AF.Copy, [pt], [tqk[bi]], scale=(0.125 if which == 0 else 1.0))
            if MLSUB == 1:
                self.end()
                ues.close()
                continue
            for t_ in range(18):
                pb, pt = misc.next()
                for kc in range(8):
                    self.mm(pb[:, 0:384], self.hT[:, kc, t_ * 128:(t_ + 1) * 128], wb[:, kc, 128:512], kc == 0, kc == 7,
                            [wt, self.th[blk_of(t_ * 128)]], [pt])
                self.cp("dve", KV[:, t_, :], pb[:, 0:384], [pt], [tkv[t_]])
            if MLSUB == 2:
                self.end()
                ues.close()
                continue
            sgr = self.ring(2, [128, 256])
            for t_ in range(18):
                pb, pt = misc.next()
                for kc in range(8):
                    self.mm(pb[:, 0:264], self.hT[:, kc, t_ * 128:(t_ + 1) * 128], wb[:, kc, 512:776], kc == 0, kc == 7,
                            [wt, self.th[blk_of(t_ * 128)]], [pt])
                if MLSUB != 3:
                    self.tt("dve", G[:, t_, :], pb[:, 256:264], gbu[:], ALU.add, [pt, tconst], [tg])
                if MLSUB != 4:
                    eb_, et_ = sgr.next()
                    self.act(eb_[:], pb[:, 0:256], AF.Exp, [pt, tg], [et_], scale=-1.0)
                    self.ts("dve", eb_[:], eb_[:], 1.0, 0.0, ALU.add, ALU.add, [et_], [et_])
                    self.recip(eb_[:], eb_[:], [et_], [et_])
                    self.cp("act", sigo[:, t_, :], eb_[:], [et_], [tso])
            if MLSTOP == 1:
                self.end()
                ues.close()
                continue
            Gv = G[:].rearrange("p t (k c) -> p t k c", c=2)
            tgp = self.P.tile()
            self.act(nlf[:], Gv[:, :, 1:4:2, :], AF.Exp, [tg], [tgp], scale=-1.0)
            self.act(nlf[:], nlf[:], AF.Ln, [tgp], [tgp], bias=1.0)
            if MLSUB == 5:
                self.end()
                ues.close()
                continue
            for dr in range(2):
                pb, pt = misc.next()
                self.mm(pb[:, 0:36], msk[:, dr, :], nlf[:, :, dr, :], True, True, [tgp, tconst], [pt])
                self.cp("dve", nb[:, :, dr, :], pb[:, 0:36].rearrange("p (t c) -> p t c", c=2), [pt], [tgp])
            pb, pt = misc.next()
            self.mm(pb[:, 0:72], ones32[:], nlf[:].rearrange("p t a c -> p (t a c)"), True, True, [tgp, tconst], [pt])
            self.cp("dve", nbt[:].rearrange("p t a c -> p (t a c)"), pb[:, 0:72], [pt], [tgp])
            if MLSUB == 6:
                self.end()
                ues.close()
                continue
            self.tt("dve", cexp[:], Gv[:, :, 0:4:2, :], nb[:], ALU.add, [tg, tgp], [tgp])
            self.act(cexp[:], cexp[:], AF.Exp, [tgp], [tgp])
            self.act(ebt[:], nb[:], AF.Exp, [tgp], [tgp], scale=-1.0)
            self.act(aK[:], nbt[:], AF.Exp, [tgp], [tgp], scale=-1.0)
            for c in range(2):
                self.cp("dve", aKc[c * 64:(c + 1) * 64, :, :], aK[c * 64:(c + 1) * 64, :, :, c], [tgp], [tgp])

            if MLSTOP == 2:
                self.end()
                ues.close()
                continue
            if MLSUB == 7:
                self.end()
                ues.close()
                continue
            self.end()
            self.begin()
            tconst = self.P.tile()
            tgp = self.P.tile()
            tso = self.P.tile()
            wot = self.P.tile()
            tqk = [self.P.tile() for _ in range(5)]
            tkv = [self.P.tile() for _ in range(18)]
            ths = [self.P.tile() for _ in range(18)]
            hs = self.sb([128, 18, 256])
            self.memset("pool", hs[:], 0.0, [], ths)
            misc = self.ring(3, [128, 512], psum=True)
            outp = self.ring(2, [128, 512], psum=True)
            kvp = self.ring(2, [128, 512], psum=True)
            S = [self.sb([128, 130]) for _ in range(2)]
            Sbf = [self.sb([128, 130], BF16) for _ in range(2)]
            tS = [self.P.tile(), self.P.tile()]
            tSb = [self.P.tile(), self.P.tile()]
            for dr in range(2):
                self.memset("pool", S[dr][:], 0.0, [], [tS[dr]])
                self.memset("pool", Sbf[dr][:], 0.0, [], [tSb[dr]])
            amr = self.ring(4, [128, 2, 128], BF16)
            vpr = self.ring(4, [128, 2, 130], BF16)
            ndr = self.ring(3, [128, 2, 130])
            smr = self.ring(4, [128, 4])
            hsteps = [(order[dr][step], dr) for step in range(18) for dr in range(2)]

            def stageA(t_, dr):
                bi = blk_of(t_ * 128)
                cs = slice(t_ * 128, (t_ + 1) * 128)
                pa, pat = misc.next()
                for c in range(2):
                    self.mm(pa[:, c * 128:(c + 1) * 128], KT[c * 64:(c + 1) * 64, cs], QT[c * 64:(c + 1) * 64, cs], True, True,
                            [tqk[bi]], [pat], sgc=True, pesync=(c == 1))
                am, amt = amr.next()
                self.tt("dve", am[:], pa[:, 0:256].rearrange("p (c t) -> p c t", c=2),
                        msk[:, dr:dr + 1, :].to_broadcast([128, 2, 128]), ALU.mult, [pat, tconst], [amt])
                vp, vpt = vpr.next()
                for c in range(2):
                    self.ts("pool" if c == 0 else "dve", vp[:, c, 0:128], KV[:, t_, 128 + c * 128:256 + c * 128],
                            cexp[:, t_, dr, c:c + 1], 0.0, ALU.mult, ALU.add, [tkv[t_], tgp], [vpt])
                self.cp("dve", vp[:, :, 128], cexp[:, t_, dr, :], [tgp], [vpt])
                pk, pkt = kvp.next()
                for c in range(2):
                    self.mm(pk[:, c * 129:c * 129 + 129], KV[:, t_, 0:128], vp[:, c, 0:129], c == 0, True, [tkv[t_], vpt], [pkt],
                            sgc=True)
                return (am, amt, vp, vpt, pk, pkt)

            def stageB(t_, dr, st):
                am, amt, vp, vpt, pk, pkt = st
                bi = blk_of(t_ * 128)
                cs = slice(t_ * 128, (t_ + 1) * 128)
                po, pot = outp.next()
                for c in range(2):
                    self.mm(po[:, c * 129:c * 129 + 129], am[:, c, :], vp[:, c, 0:129], c == 0, False, [amt, vpt], [pot], sgc=True)
                for c in range(2):
                    self.mm(po[:, c * 129:c * 129 + 129], QT[c * 64:(c + 1) * 64, cs], Sbf[dr][c * 64:(c + 1) * 64, 0:129],
                            False, True, [tqk[bi], tSb[dr]], [pot], sgc=True, pesync=(c == 1))
                for c in range(2):
                    self.tt("dve", S[dr][c * 64:(c + 1) * 64, 0:129], S[dr][c * 64:(c + 1) * 64, 0:129],
                            pk[c * 64:(c + 1) * 64, c * 129:c * 129 + 129], ALU.add, [pkt, tS[dr]], [tS[dr]])
                self.ts("dve", S[dr][:, 0:129], S[dr][:, 0:129], aKc[:, t_, dr:dr + 1], 0.0, ALU.mult, ALU.add,
                        [tS[dr], tgp], [tS[dr]])
                self.cp("act", Sbf[dr][:, 0:129], S[dr][:, 0:129], [tS[dr]], [tSb[dr]])
                nd, ndt = ndr.next()
                for c in range(2):
                    self.act(nd[:, c, 0:129], po[:, c * 129:c * 129 + 129], AF.Copy, [pot, tgp], [ndt],
                             scale=ebt[:, t_, dr, c:c + 1])
                sm, smt = smr.next()
                self.act(sm[:, 0:2], nd[:, :, 128], AF.Abs, [ndt], [smt])
                self.ts("dve", sm[:, 0:2], sm[:, 0:2], 1.0, 0.0, ALU.max, ALU.add, [smt], [smt])
                self.recip(sm[:, 0:2], sm[:, 0:2], [smt], [smt])
                for c in range(2):
                    self.stt("dve", hs[:, t_, c * 128:(c + 1) * 128], nd[:, c, 0:128], sm[:, c:c + 1],
                             hs[:, t_, c * 128:(c + 1) * 128], ALU.mult, ALU.add, [ndt, smt, ths[t_]], [ths[t_]])

            pend = stageA(*hsteps[0])
            for k_hs, (t_, dr) in enumerate(hsteps):
                cur = pend
                if k_hs + 1 < len(hsteps):
                    pend = stageA(*hsteps[k_hs + 1])
                stageB(t_, dr, cur)

            if MLSTOP == 3:
                self.end()
                ues.close()
                continue
            hnr = self.ring(2, [128, 256])
            jr = self.ring(2, [128, 128])
            for t_ in range(18):
                bi = blk_of(t_ * 128)
                if bi not in oblks:
                    continue
                sm, smt = smr.next()
                self.memset("dve", sm[:, 0:2], 0.0, [], [smt])
                for c in range(2):
                    jb, jt = jr.next()
                    self.act(jb[:], hs[:, t_, c * 128:(c + 1) * 128], AF.Square, [ths[t_], smt], [jt, smt], accum=sm[:, c:c + 1])
                self.act(sm[:, 2:4], sm[:, 0:2], AF.Sqrt, [smt], [smt], bias=EPS, scale=1.0 / 128)
                self.recip(sm[:, 2:4], sm[:, 2:4], [smt], [smt])
                hn, hnt = hnr.next()
                for c in range(2):
                    self.stt("dve", hn[:, c * 128:(c + 1) * 128], hs[:, t_, c * 128:(c + 1) * 128], sm[:, 2 + c:3 + c],
                             mlg[:, c * 128:(c + 1) * 128], ALU.mult, ALU.mult, [ths[t_], smt, tconst], [hnt])
                self.tt("pool", hn[:], hn[:], sigo[:, t_, :], ALU.mult, [hnt, tso], [hnt])
                for c in range(2):
                    pb, pt = misc.next()
                    self.mm(pb[:, 0:128], hn[:, c * 128:(c + 1) * 128], self.ident[:], True, True, [hnt], [pt])
                    self.cp("act", (QT if c == 0 else KT)[:, t_ * 128:(t_ + 1) * 128], pb[:, 0:128], [pt], [tqk[bi]])
            for bi in oblks:
                s, e = BLKS[bi]
                w = e - s
                j = 1 if bi == 4 else 0
                for dc in range(8):
                    pb, pt = misc.next()
                    for c in range(2):
                        self.mm(pb[:, :w], wo2[:, c, dc * 128:(dc + 1) * 128], (QT if c == 0 else KT)[:, s:e], c == 0, c == 1, [wot, tqk[bi]], [pt])
                    self.stt("dve", self.xT[:, dc, s:e], pb[:, :w], self.modv[:, L, 16 + dc, j:j + 1], self.xT[:, dc, s:e],
                             ALU.mult, ALU.add, [pt, self.tx[bi][dc]], [self.tx[bi][dc]])
            self.end()
            ues.close()
        self.P.relax = RELAX_SAME
        self.free_hT()


_BF = ml_dtypes.bfloat16


def _pk(v, k):
    return np.ascontiguousarray(np.asarray(v, np.float32).reshape(k, 128).T)


def _consts():
    c = {}
    c["ident"] = np.eye(128, dtype=np.float32)
    R = np.zeros((128, 128), np.float32)
    for dd in range(128):
        i = dd % 32
        if i < 16:
            R[dd, dd + 16] = -1.0
        else:
            R[dd, dd - 16] = 1.0
    c["rotT"] = np.ascontiguousarray(R.T)
    t = np.arange(NL)
    row = (t // 64).astype(np.float32)
    col = (t % 64).astype(np.float32)
    inv = (10000.0 ** (-np.arange(16, dtype=np.float32) / 16)).astype(np.float32)
    cos = np.zeros((128, NL), np.float32)
    sin = np.zeros((128, NL), np.float32)
    for p in range(128):
        dd = p % 64
        j = dd % 16
        pos = row if dd < 32 else col
        ang = (pos * inv[j]).astype(np.float32)
        cos[p] = np.cos(ang)
        sin[p] = np.sin(ang)
    c["cos"] = cos
    c["sin"] = sin
    m = np.zeros((128, 6, 512), np.float32)
    kk = np.arange(128)[:, None]
    qq = np.arange(512)[None, :]
    for r in range(6):
        rel = r - 1
        m[:, r, :] = (np.abs(qq - kk - rel * 128) <= 128)
    c["swmask"] = m.astype(_BF)
    mm_ = np.zeros((128, 2, 128), np.float32)
    ss = np.arange(128)[:, None]
    tt = np.arange(128)[None, :]
    mm_[:, 0, :] = ss <= tt
    mm_[:, 1, :] = ss >= tt
    c["mlmask"] = mm_
    return c


_CACHE = {}


def _get_nc(n_layers=4, debug=False):
    key = (n_layers, debug)
    if key not in _CACHE:
        _CACHE[key] = Builder(n_layers, debug).build()
    return _CACHE[key]


def _shared_inputs(inputs, n_layers):
    sh = dict(_consts())
    sh["fng"] = _pk(inputs["final_norm_g"], 8)
    for L in range(n_layers):
        p = "l%d_" % L
        f = lambda n: np.asarray(inputs[p + n], np.float32)
        sh[p + "ada_w"] = np.ascontiguousarray(f("ada_w"))
        sh[p + "ada_b"] = _pk(f("ada_b"), 48)
        sh[p + "n1g"] = _pk(f("norm1_g"), 8)
        sh[p + "n2g"] = _pk(f("norm2_g"), 8)
        sh[p + "w_up"] = np.ascontiguousarray(f("ffn_w_up"))
        sh[p + "w_down"] = np.ascontiguousarray(f("ffn_w_down"))
        cw = f("ffn_conv_w")
        sh[p + "cw"] = np.ascontiguousarray(cw.reshape(3, 44, 128).transpose(2, 1, 0))
        sh[p + "cb"] = _pk(f("ffn_conv_b"), 44)
        k = KINDS[L]
        if k == 0:
            sh[p + "w_in"] = np.ascontiguousarray(f("da_w_in"))
            lam = np.stack([f("da_lam_q1"), f("da_lam_k1"), f("da_lam_q2"), f("da_lam_k2")], 0)
            sh[p + "lam"] = np.ascontiguousarray(np.broadcast_to(lam[None], (128, 4, 64)))
            sh[p + "subgc"] = np.ascontiguousarray(f("da_subln_g").reshape(128, 1))
            sh[p + "w_out"] = np.ascontiguousarray(f("da_w_out"))
        elif k == 1:
            sh[p + "w_in"] = np.ascontiguousarray(f("ml_w_in"))
            sh[p + "gate_b"] = np.ascontiguousarray(np.broadcast_to(f("ml_gate_b")[None], (128, 32)))
            sh[p + "mlg"] = np.ascontiguousarray(np.broadcast_to(f("ml_norm_g")[None], (128, 1024)))
            sh[p + "w_out"] = np.ascontiguousarray(f("ml_w_out"))
        else:
            sh[p + "w_in"] = np.ascontiguousarray(f("sw_w_in"))
            sh[p + "sink"] = np.ascontiguousarray(np.broadcast_to(f("sw_sink")[None], (128, 16)))
            sh[p + "w_out"] = np.ascontiguousarray(f("sw_w_out"))
    return sh


def _run(inputs, n_layers=4, debug=False):
    nc = _get_nc(n_layers, debug)
    sh = _shared_inputs(inputs, n_layers)
    x = np.asarray(inputs["x"], np.float32)
    c = np.asarray(inputs["c"], np.float32)
    ctx = np.asarray(inputs["ctx"], np.float32)
    c_ctx = np.asarray(inputs["c_ctx"], np.float32)
    in_maps = []
    for b in range(8):
        m = dict(sh)
        m["x"] = np.ascontiguousarray(x[b])
        m["ctx"] = np.ascontiguousarray(ctx[b])
        m["cc"] = np.ascontiguousarray(np.concatenate([_pk(c[b], 8), _pk(c_ctx, 8)], axis=1))
        in_maps.append(m)
    res = run_bass_kernel_spmd(nc, in_maps, core_ids=list(range(8)))
    return np.stack([np.asarray(r["out"], np.float32) for r in res.results], 0)


_INPUT_NAMES = (
    "x", "c", "ctx", "c_ctx",
    "l0_ada_w", "l0_ada_b", "l0_norm1_g", "l0_da_w_in",
    "l0_da_lam_q1", "l0_da_lam_k1", "l0_da_lam_q2", "l0_da_lam_k2",
    "l0_da_subln_g", "l0_da_w_out", "l0_norm2_g", "l0_ffn_w_up",
    "l0_ffn_conv_w", "l0_ffn_conv_b", "l0_ffn_w_down", "l1_ada_w",
    "l1_ada_b", "l1_norm1_g", "l1_ml_w_in", "l1_ml_gate_b",
    "l1_ml_norm_g", "l1_ml_w_out", "l1_norm2_g", "l1_ffn_w_up",
    "l1_ffn_conv_w", "l1_ffn_conv_b", "l1_ffn_w_down", "l2_ada_w",
    "l2_ada_b", "l2_norm1_g", "l2_sw_w_in", "l2_sw_sink",
    "l2_sw_w_out", "l2_norm2_g", "l2_ffn_w_up", "l2_ffn_conv_w",
    "l2_ffn_conv_b", "l2_ffn_w_down", "l3_ada_w", "l3_ada_b",
    "l3_norm1_g", "l3_da_w_in", "l3_da_lam_q1", "l3_da_lam_k1",
    "l3_da_lam_q2", "l3_da_lam_k2", "l3_da_subln_g", "l3_da_w_out",
    "l3_norm2_g", "l3_ffn_w_up", "l3_ffn_conv_w", "l3_ffn_conv_b",
    "l3_ffn_w_down", "final_norm_g",
)


def kernel(**inputs):
    missing = [n for n in _INPUT_NAMES if n not in inputs]
    assert not missing, missing
    return _run(inputs, 4, False)
```

```python
import math
from contextlib import ExitStack
import numpy as np
import ml_dtypes
import concourse.bass as bass
import concourse.mybir as mybir
from concourse.bass_utils import run_bass_kernel_spmd

F32 = mybir.dt.float32
BF16 = mybir.dt.bfloat16
AF = mybir.ActivationFunctionType
ALU = mybir.AluOpType
AX = mybir.AxisListType

ENGS = ("pe", "act", "dve", "pool", "sp")
RELAX_SAME = False
NDMA_SEM = 8
NDMA_Q = {"sp": 8, "pool": 3, "act": 8}


class T:
    __slots__ = ("name", "lw", "rs")

    def __init__(self, name=""):
        self.name = name
        self.lw = None
        self.rs = []


class Op:
    __slots__ = ("eng", "fn", "deps", "dma", "signal", "idx", "cnt", "dslot", "dval", "nosame", "pesync")

    def __init__(self, eng, fn, dma):
        self.eng = eng
        self.fn = fn
        self.deps = []
        self.dma = dma
        self.signal = False
        self.idx = -1
        self.cnt = -1
        self.dslot = -1
        self.dval = -1
        self.nosame = False
        self.pesync = False


class Prog:
    def __init__(self, nc, same_engine_sync=True):
        self.nc = nc
        self.same = same_engine_sync
        self.relax = RELAX_SAME
        self.sem = {}
        self.sem_ctx = []
        for e in ("pe", "act", "dve", "pool"):
            self.sem[e] = self._mksem("s_" + e)
        self.dsem = {}
        for q in ("sp", "pool", "act"):
            self.dsem[q] = [self._mksem("d_%s%d" % (q, i)) for i in range(NDMA_SEM)]
        self.bar = self._mksem("bar")
        self.bar_cnt = 0
        self.cnt = {e: 0 for e in ("pe", "act", "dve", "pool")}
        self.dcnt = {q: 0 for q in ("sp", "pool", "act")}
        self.ops = {e: [] for e in ENGS}
        self.tiles = []
        self.n_instr = 0

    def _mksem(self, name):
        ctx = self.nc.semaphore(name)
        s = ctx.__enter__()
        self.sem_ctx.append(ctx)
        return s

    def close(self):
        for ctx in reversed(self.sem_ctx):
            ctx.__exit__(None, None, None)

    def tile(self, name=""):
        t = T(name)
        self.tiles.append(t)
        return t

    def skip_same(self, o, d):
        if d.eng != o.eng or o.dma or d.dma:
            return False
        if o.eng == "pe":
            return not o.pesync
        return (not self.same) or o.nosame

    def op(self, eng, fn, reads=(), writes=(), dma=False, nosame=False, pesync=False):
        o = Op(eng, fn, dma)
        o.nosame = nosame
        o.pesync = pesync
        deps = []
        for t in reads:
            if t.lw is not None:
                deps.append(t.lw)
        for t in writes:
            if t.lw is not None and (dma or t.lw.dma or t.lw.eng != eng or not self.relax):
                deps.append(t.lw)
            for r in t.rs:
                if dma or r.dma or r.eng != eng or not self.relax:
                    deps.append(r)
        for t in reads:
            t.rs.append(o)
        for t in writes:
            t.lw = o
            t.rs = []
        seen = set()
        best = {}
        for d in deps:
            if d is o or id(d) in seen:
                continue
            seen.add(id(d))
            if d.dma:
                o.deps.append(d)
            else:
                b = best.get(d.eng)
                if b is None or d.idx > b.idx:
                    best[d.eng] = d
        o.deps.extend(best.values())
        o.idx = len(self.ops[eng])
        self.ops[eng].append(o)
        return o

    def dma(self, q, out, in_, reads=(), writes=(), **kw):
        def fn(e):
            return e.dma_start(out=out, in_=in_, **kw)
        return self.op(q, fn, reads, writes, dma=True)

    def flush(self, final=False):
        nc = self.nc
        ops = self.ops
        for e in ENGS:
            for o in ops[e]:
                for d in o.deps:
                    if not d.dma:
                        if self.skip_same(o, d):
                            continue
                        d.signal = True
        for e in ("pe", "act", "dve", "pool"):
            comp = [o for o in ops[e] if not o.dma]
            if comp:
                comp[-1].signal = True
        for e in ("pe", "act", "dve", "pool"):
            c = self.cnt[e]
            for o in ops[e]:
                if o.dma:
                    continue
                if o.signal:
                    c += 1
                o.cnt = c if o.signal else -1
            nxt = None
            for o in reversed(ops[e]):
                if o.dma:
                    continue
                if o.signal:
                    nxt = o.cnt
                else:
                    o.cnt = nxt
            self.cnt[e] = c
        for q in ("sp", "pool", "act"):
            n = self.dcnt[q]
            for o in ops[q]:
                if o.dma:
                    o.dslot = n % NDMA_Q[q]
                    o.dval = 16 * (n // NDMA_Q[q] + 1)
                    n += 1
            self.dcnt[q] = n
        final_cnt = dict(self.cnt)
        final_d = {}
        for q in ("sp", "pool", "act"):
            n = self.dcnt[q]
            final_d[q] = [16 * ((n - s + NDMA_Q[q] - 1) // NDMA_Q[q]) if s < NDMA_Q[q] else 0 for s in range(NDMA_SEM)]
        self.bar_cnt += 1
        bar_val = self.bar_cnt
        prog = self

        def emit(ename, eng):
            waited = {}

            def wait(sem, val, key):
                if waited.get(key, 0) >= val:
                    return
                waited[key] = val
                eng.wait_ge(sem, val)
                prog.n_instr += 1

            for o in ops[ename]:
                for d in o.deps:
                    if d.dma:
                        wait(prog.dsem[d.eng][d.dslot], d.dval, ("d", d.eng, d.dslot))
                    else:
                        if prog.skip_same(o, d):
                            continue
                        wait(prog.sem[d.eng], d.cnt, ("c", d.eng))
                if o.dma:
                    if o.dval > 16:
                        wait(prog.dsem[ename][o.dslot], o.dval - 16, ("d", ename, o.dslot))
                    ins = o.fn(eng)
                    ins.then_inc(prog.dsem[ename][o.dslot], 16)
                else:
                    ins = o.fn(eng)
                    if o.signal:
                        ins.then_inc(prog.sem[ename], 1)
                prog.n_instr += 1
            if ename == "sp":
                for e2 in ("pe", "act", "dve", "pool"):
                    if final_cnt[e2] > 0:
                        wait(prog.sem[e2], final_cnt[e2], ("c", e2))
                for q in ("sp", "pool", "act"):
                    for s in range(NDMA_SEM):
                        if final_d[q][s] > 0:
                            wait(prog.dsem[q][s], final_d[q][s], ("d", q, s))
                eng.sem_inc(prog.bar, 1)
                if final:
                    eng.wait_ge(prog.bar, bar_val)
            else:
                eng.wait_ge(prog.bar, bar_val)

        with nc.Block() as block:
            @block.tensor
            def _(e):
                emit("pe", e)

            @block.scalar
            def _(e):
                emit("act", e)

            @block.vector
            def _(e):
                emit("dve", e)

            @block.gpsimd
            def _(e):
                emit("pool", e)

            @block.sync
            def _(e):
                emit("sp", e)

        self.ops = {e: [] for e in ENGS}
        for t in self.tiles:
            t.lw = None
            t.rs = []
        self.tiles = []

EPS = 1e-6
NT, NL, NCX = 2304, 2048, 256
BLKS = [(0, 512), (512, 1024), (1024, 1536), (1536, 2048), (2048, 2304)]
CBLKS = [(0, 410), (410, 820), (820, 1230), (1230, 1640), (1640, 2048), (2048, 2304)]
KINDS = [0, 1, 2, 0]
FFN = 2816
NFC = 22


def blk_of(t):
    return min(t // 512, 4)


def blks_overlap(lo, hi):
    return sorted(set(blk_of(t) for t in (lo, hi - 1)) | set(range(blk_of(lo), blk_of(hi - 1) + 1)))


class Ring:
    def __init__(self, P, bufs):
        self.P = P
        self.b = list(bufs)
        self.t = [None] * len(self.b)
        self.i = 0

    def next(self):
        k = self.i % len(self.b)
        self.i += 1
        if self.t[k] is None:
            self.t[k] = self.P.tile()
        return self.b[k], self.t[k]


class Builder:
    def __init__(self, n_layers=4, debug=False):
        self.nc = bass.Bass("TRN2", target_bir_lowering=False)
        self.P = Prog(self.nc)
        self.uid = 0
        self.n_layers = n_layers
        self.debug = debug
        self.es = None

    def din(self, name, shape, dt=F32):
        return self.nc.dram_tensor(name, list(shape), dt, kind="ExternalInput").ap()

    def sb(self, shape, dt=F32, es=None):
        self.uid += 1
        return (es or self.es).enter_context(self.nc.sbuf_tensor("sb%d" % self.uid, list(shape), dt))

    def ps(self, shape, dt=F32, es=None):
        self.uid += 1
        return (es or self.es).enter_context(self.nc.psum_tensor("ps%d" % self.uid, list(shape), dt))

    def ring(self, n, shape, dt=F32, psum=False):
        return Ring(self.P, [(self.ps if psum else self.sb)(shape, dt) for _ in range(n)])

    def mm(self, out, lhsT, rhs, start, stop, R, W, sgc=False, pesync=False):
        if sgc:
            self.P.op("pe", lambda e: e.matmul(out, lhsT, rhs, start=start, stop=stop, skip_group_check=True), R, W,
                      pesync=pesync)
        else:
            self.P.op("pe", lambda e: e.matmul(out, lhsT, rhs, start=start, stop=stop), R, W, pesync=pesync)

    def act(self, out, in_, func, R, W, bias=0.0, scale=1.0, accum=None):
        if accum is None:
            self.P.op("act", lambda e: e.activation(out=out, in_=in_, func=func, bias=bias, scale=scale), R, W)
        else:
            self.P.op("act", lambda e: e.activation(out=out, in_=in_, func=func, bias=bias, scale=scale,
                                                    accum_out=accum), R, W)

    def tt(self, eng, out, in0, in1, op, R, W):
        self.P.op(eng, lambda e: e.tensor_tensor(out=out, in0=in0, in1=in1, op=op), R, W)

    def ts(self, eng, out, in0, s1, s2, op0, op1, R, W):
        self.P.op(eng, lambda e: e.tensor_scalar(out=out, in0=in0, scalar1=s1, scalar2=s2, op0=op0, op1=op1), R, W)

    def stt(self, eng, out, in0, sc, in1, op0, op1, R, W):
        self.P.op(eng, lambda e: e.scalar_tensor_tensor(out=out, in0=in0, scalar=sc, in1=in1, op0=op0, op1=op1), R, W)

    def cp(self, eng, out, in_, R, W):
        if eng == "act":
            self.P.op("act", lambda e: e.activation(out=out, in_=in_, func=AF.Copy), R, W)
        else:
            self.P.op(eng, lambda e: e.tensor_copy(out=out, in_=in_), R, W)

    def memset(self, eng, ap, val, R, W):
        self.P.op(eng, lambda e: e.memset(ap, val), R, W)

    def recip(self, out, in_, R, W):
        self.P.op("dve", lambda e: e.reciprocal(out=out, in_=in_), R, W)

    def dma(self, q, out, in_, R=(), W=()):
        self.P.dma(q, out, in_, R, W)

    def begin(self):
        self.es = ExitStack()
        self.tx = [[self.P.tile() for _ in range(8)] for _ in range(5)]
        self.th = [self.P.tile() for _ in range(5)]

    def end(self, final=False):
        self.P.flush(final=final)
        self.es.close()
        self.es = None

    def txall(self, bi):
        return list(self.tx[bi])

    def build(self):
        nc = self.nc
        d = {}
        d["x"] = self.din("x", [NL, 1024])
        d["ctx"] = self.din("ctx", [NCX, 1024])
        d["cc"] = self.din("cc", [128, 16])
        d["ident"] = self.din("ident", [128, 128])
        d["rotT"] = self.din("rotT", [128, 128])
        d["cos"] = self.din("cos", [128, NL])
        d["sin"] = self.din("sin", [128, NL])
        d["fng"] = self.din("fng", [128, 8])
        d["swmask"] = self.din("swmask", [128, 6, 512], BF16)
        d["mlmask"] = self.din("mlmask", [128, 2, 128])
        for L in range(self.n_layers):
            p = "l%d_" % L
            d[p + "ada_w"] = self.din(p + "ada_w", [1024, 6144])
            d[p + "ada_b"] = self.din(p + "ada_b", [128, 48])
            d[p + "n1g"] = self.din(p + "n1g", [128, 8])
            d[p + "n2g"] = self.din(p + "n2g", [128, 8])
            d[p + "w_up"] = self.din(p + "w_up", [1024, 2 * FFN])
            d[p + "w_down"] = self.din(p + "w_down", [FFN, 1024])
            d[p + "cw"] = self.din(p + "cw", [128, 44, 3])
            d[p + "cb"] = self.din(p + "cb", [128, 44])
            k = KINDS[L]
            if k == 0:
                d[p + "w_in"] = self.din(p + "w_in", [1024, 3072])
                d[p + "lam"] = self.din(p + "lam", [128, 4, 64])
                d[p + "subgc"] = self.din(p + "subgc", [128, 1])
            elif k == 1:
                d[p + "w_in"] = self.din(p + "w_in", [1024, 3104])
                d[p + "gate_b"] = self.din(p + "gate_b", [128, 32])
                d[p + "mlg"] = self.din(p + "mlg", [128, 1024])
            else:
                d[p + "w_in"] = self.din(p + "w_in", [1024, 1536])
                d[p + "sink"] = self.din(p + "sink", [128, 16])
            d[p + "w_out"] = self.din(p + "w_out", [1024, 1024])
        if self.debug:
            self.out_d = nc.dram_tensor("out", [NT, 1024], F32, kind="ExternalOutput").ap()
        else:
            self.out_d = nc.dram_tensor("out", [NL, 1024], F32, kind="ExternalOutput").ap()
        self.d = d

        top = ExitStack()
        self.top = top
        self.xT = self.sb([128, 8, NT], F32, top)
        self.ident = self.sb([128, 128], F32, top)
        self.ones_bf = self.sb([128, 128], BF16, top)
        self.modv = self.sb([128, 4, 48, 2], F32, top)
        self.gs1 = self.sb([128, 4, 8, 2], F32, top)
        self.gs2 = self.sb([128, 4, 8, 2], F32, top)
        self.fng = self.sb([128, 8], F32, top)

        self.phase_load()
        self.phase_ada()
        for L in range(self.n_layers):
            need_ctx = L < 3
            k = KINDS[L]
            self.phase_norm(self.gs1, L, 0, list(range(5)))
            if k == 0:
                self.phase_attn(L, need_ctx, "da")
            elif k == 1:
                self.phase_mlstm(L, need_ctx)
            else:
                self.phase_attn(L, need_ctx, "sw")
            oblks = list(range(5 if need_ctx else 4))
            if self.debug == "mix" and L == self.n_layers - 1:
                break
            self.phase_norm(self.gs2, L, 24, oblks)
            self.phase_ffn(L, oblks)
        self.phase_final()
        top.close()
        self.P.close()
        return nc

    def phase_load(self):
        self.begin()
        d = self.d
        xin = self.ring(3, [128, 1024])
        pst = self.ring(4, [128, 512], psum=True)
        tc = self.P.tile()
        self.dma("sp", self.ident[:], d["ident"], W=[tc])
        self.dma("sp", self.fng[:], d["fng"], W=[tc])
        self.memset("dve", self.ones_bf[:], 1.0, [], [tc])
        for tt_ in range(18):
            src = d["x"][tt_ * 128:(tt_ + 1) * 128, :] if tt_ < 16 else d["ctx"][(tt_ - 16) * 128:(tt_ - 15) * 128, :]
            xb, xt_ = xin.next()
            self.dma("sp", xb[:], src, W=[xt_])
            bi = blk_of(tt_ * 128)
            for half in range(2):
                pb, pt = pst.next()
                for c4 in range(4):
                    c = half * 4 + c4
                    self.mm(pb[:, c4 * 128:(c4 + 1) * 128], xb[:, c * 128:(c + 1) * 128], self.ident[:], True, True,
                            [xt_, tc], [pt])
                self.cp("dve" if half == 0 else "act",
                        self.xT[:, half * 4:(half + 1) * 4, tt_ * 128:(tt_ + 1) * 128],
                        pb[:].rearrange("p (c t) -> p c t", c=4), [pt], self.tx[bi][half * 4:(half + 1) * 4])
        self.end()

    def phase_ada(self):
        self.begin()
        d = self.d
        cc = self.sb([128, 16])
        scT = self.sb([128, 16], BF16)
        tcc = self.P.tile()
        self.dma("sp", cc[:], d["cc"], W=[tcc])
        tsc = self.P.tile()
        self.act(scT[:], cc[:], AF.Silu, [tcc], [tsc])
        wring = self.ring(2, [128, 8, 768], BF16)
        pmod = self.ring(2, [128, 512], psum=True)
        adab = self.sb([128, 4, 48])
        n1g = self.sb([128, 4, 8])
        n2g = self.sb([128, 4, 8])
        tsm = self.P.tile()
        for L in range(self.n_layers):
            p = "l%d_" % L
            self.dma("sp", adab[:, L, :], d[p + "ada_b"], W=[tsm])
            self.dma("sp", n1g[:, L, :], d[p + "n1g"], W=[tsm])
            self.dma("sp", n2g[:, L, :], d[p + "n2g"], W=[tsm])
        for L in range(self.n_layers):
            p = "l%d_" % L
            wv = d[p + "ada_w"].rearrange("(kc kp) f -> kp kc f", kp=128)
            pb, pt = pmod.next()
            for fg in range(8):
                wb, wt = wring.next()
                self.dma("pool", wb[:], wv[:, :, fg * 768:(fg + 1) * 768], W=[wt])
                for f6 in range(6):
                    f = fg * 6 + f6
                    for kc in range(8):
                        self.mm(pb[:, f * 2:(f + 1) * 2], wb[:, kc, f6 * 128:(f6 + 1) * 128], scT[:, kc:16:8],
                                kc == 0, kc == 7, [wt, tsc], [pt])
            tm = self.P.tile()
            for j in range(2):
                self.tt("dve", self.modv[:, L, :, j], pb[:, j:96:2], adab[:, L, :], ALU.add, [pt, tsm], [tm])
            for j in range(2):
                self.stt("dve", self.gs1[:, L, :, j], self.modv[:, L, 8:16, j], 1.0, n1g[:, L, :], ALU.add, ALU.mult,
                         [tm, tsm], [tm])
                self.stt("dve", self.gs2[:, L, :, j], self.modv[:, L, 32:40, j], 1.0, n2g[:, L, :], ALU.add, ALU.mult,
                         [tm, tsm], [tm])
        self.end()

    def alloc_hT(self):
        self.hes = ExitStack()
        self.hT = self.sb([128, 8, NT], BF16, self.hes)

    def free_hT(self):
        self.hes.close()

    def phase_norm(self, gs, L, shift_base, blks):
        self.alloc_hT()
        self.begin()
        sq = self.ring(2, [128, 8, 512], BF16)
        pss = self.ring(2, [128, 512], psum=True)
        rr = self.ring(2, [128, 512])
        tmp = self.ring(4, [128, 512])
        for bi in blks:
            s, e = BLKS[bi]
            w = e - s
            j = 1 if bi == 4 else 0
            sqb, sqt = sq.next()
            self.act(sqb[:, :, :w], self.xT[:, :, s:e], AF.Square, self.txall(bi), [sqt])
            pb, pt = pss.next()
            for c in range(8):
                self.mm(pb[:, :w], self.ones_bf[:], sqb[:, c, :w], c == 0, c == 7, [sqt], [pt])
            rb, rt = rr.next()
            self.act(rb[:, :w], pb[:, :w], AF.Sqrt, [pt], [rt], bias=EPS, scale=1.0 / 1024)
            self.recip(rb[:, :w], rb[:, :w], [rt], [rt])
            for c in range(8):
                tb, tt_ = tmp.next()
                self.stt("dve", tb[:, :w], self.xT[:, c, s:e], gs[:, L, c, j:j + 1], rb[:, :w], ALU.mult, ALU.mult,
                         [self.tx[bi][c], rt], [tt_])
                self.act(self.hT[:, c, s:e], tb[:, :w], AF.Identity, [tt_], [self.th[bi]],
                         bias=self.modv[:, L, shift_base + c, j:j + 1])
        self.end()

    def phase_ffn(self, L, oblks):
        self.begin()
        d = self.d
        p = "l%d_" % L
        cblks = CBLKS if 4 in oblks else CBLKS[:5]
        cw = self.sb([128, 44, 3])
        cb = self.sb([128, 44])
        tcw = self.P.tile()
        self.dma("sp", cw[:], d[p + "cw"], W=[tcw])
        self.dma("sp", cb[:], d[p + "cb"], W=[tcw])
        wuv = d[p + "w_up"].rearrange("(kc kp) f -> kp kc f", kp=128)
        wuring = self.ring(2, [128, 8, 512], BF16)
        GS = 4
        groups = [list(range(a, min(a + GS, NFC))) for a in range(0, NFC, GS)]
        zring = self.ring(2, [128, GS, NT], BF16)
        wdring = self.ring(2, [128, GS, 1024], BF16)
        psr = self.ring(4, [128, 512], psum=True)
        pdr = self.ring(3, [128, 512], psum=True)
        vr = self.ring(6, [128, 512])
        wall = {}

        def load_pair(jp):
            if jp in wall or jp >= NFC:
                return
            wub, wut = wuring.next()
            self.dma("pool", wub[:, :, 0:256], wuv[:, :, jp * 128:(jp + 2) * 128], W=[wut])
            self.dma("pool", wub[:, :, 256:512], wuv[:, :, FFN + jp * 128:FFN + (jp + 2) * 128], W=[wut])
            wall[jp] = (wub, wut)
            wall[jp + 1] = (wub, wut)

        for grp in groups:
            G = len(grp)
            wd, twd = wdring.next()
            zb, _ = zring.next()
            tz = [[self.P.tile() for _ in cblks] for _ in range(G)]
            for gi, jf in enumerate(grp):
                self.dma("pool", wd[:, gi, :], d[p + "w_down"][jf * 128:(jf + 1) * 128, :], W=[twd])
            its = []
            for gi, jf in enumerate(grp):
                for cbi in range(len(cblks)):
                    its.append((gi, jf, cbi))
            wcur = {}

            def stage1(it):
                gi, jf, cbi = it
                if cbi == 0 and gi % 2 == 0:
                    load_pair(jf)
                    load_pair(jf + 2)
                wub, wut = wall[jf]
                qa = (gi % 2) * 128
                qg = 256 + (gi % 2) * 128
                s, e = cblks[cbi]
                seq0, seq1 = (0, NL) if s < NL else (NL, NT)
                lo = max(s - 1, seq0)
                hi = min(e + 1, seq1)
                n = hi - lo
                m = e - s
                hb = [self.th[b_] for b_ in blks_overlap(lo, hi)]
                pa, pat = psr.next()
                for kc in range(8):
                    self.mm(pa[:, :n], wub[:, kc, qa:qa + 128], self.hT[:, kc, lo:hi], kc == 0, kc == 7, [wut] + hb, [pat])
                pg, pgt = psr.next()
                for kc in range(8):
                    self.mm(pg[:, :n], wub[:, kc, qg:qg + 128], self.hT[:, kc, lo:hi], kc == 0, kc == 7, [wut] + hb, [pgt])
                va, vat = vr.next()
                vg, vgt = vr.next()
                ag = ((pa, pat, va, vat, jf), (pg, pgt, vg, vgt, NFC + jf))
                o0 = s - lo
                for (pp, ppt, vv, vvt, fi) in ag:
                    self.act(vv[:, :m], pp[:, o0:o0 + m], AF.Identity, [ppt, tcw], [vvt],
                             bias=cb[:, fi:fi + 1], scale=cw[:, fi, 1:2])
                return (ag, s, e, lo, m, seq0, seq1)

            def stage2(st):
                ag, s, e, lo, m, seq0, seq1 = st
                ls = max(s, seq0 + 1)
                re_ = min(e, seq1 - 1)
                for (pp, ppt, vv, vvt, fi) in ag:
                    self.stt("dve", vv[:, ls - s:m], pp[:, ls - 1 - lo:e - 1 - lo], cw[:, fi, 0:1], vv[:, ls - s:m],
                             ALU.mult, ALU.add, [ppt, vvt, tcw], [vvt])
                for (pp, ppt, vv, vvt, fi) in ag:
                    self.stt("dve", vv[:, 0:re_ - s], pp[:, s + 1 - lo:re_ + 1 - lo], cw[:, fi, 2:3], vv[:, 0:re_ - s],
                             ALU.mult, ALU.add, [ppt, vvt, tcw], [vvt])

            def stage3(it, st):
                gi, jf, cbi = it
                ag, s, e, lo, m, seq0, seq1 = st
                (pa, pat, va, vat, _), (pg, pgt, vg, vgt, _) = ag
                self.act(vg[:, :m], vg[:, :m], AF.Silu, [vgt], [vgt])
                self.tt("pool", zb[:, gi, s:e], va[:, :m], vg[:, :m], ALU.mult, [vat, vgt], [tz[gi][cbi]])

            pend = stage1(its[0])
            for k_it, it in enumerate(its):
                cur = pend
                if k_it + 1 < len(its):
                    pend = stage1(its[k_it + 1])
                stage2(cur)
                stage3(it, cur)
            for bi in oblks:
                s, e = BLKS[bi]
                w = e - s
                j = 1 if bi == 4 else 0
                zt = []
                for gi in range(G):
                    for cbi, (cs, ce) in enumerate(cblks):
                        if cs < e and ce > s:
                            zt.append(tz[gi][cbi])
                for dc in range(8):
                    pb, pt = pdr.next()
                    for gi in range(G):
                        self.mm(pb[:, :w], wd[:, gi, dc * 128:(dc + 1) * 128], zb[:, gi, s:e], gi == 0, gi == G - 1,
                                [twd] + zt, [pt])
                    self.stt("dve", self.xT[:, dc, s:e], pb[:, :w], self.modv[:, L, 40 + dc, j:j + 1],
                             self.xT[:, dc, s:e], ALU.mult, ALU.add, [pt, self.tx[bi][dc]], [self.tx[bi][dc]])
        self.end()
        self.free_hT()

    def phase_final(self):
        self.begin()
        sq = self.ring(2, [128, 8, 512], BF16)
        pss = self.ring(2, [128, 512], psum=True)
        rr = self.ring(2, [128, 512])
        yb = self.ring(2, [128, 8, 512])
        ptr = self.ring(4, [128, 512], psum=True)
        ob = self.ring(3, [128, 1024])
        nb = 5 if self.debug else 4
        for bi in range(nb):
            s, e = BLKS[bi]
            w = e - s
            y, yt = yb.next()
            if self.debug:
                for c in range(8):
                    self.cp("dve", y[:, c, :w], self.xT[:, c, s:e], [self.tx[bi][c]], [yt])
            else:
                sqb, sqt = sq.next()
                self.act(sqb[:, :, :w], self.xT[:, :, s:e], AF.Square, self.txall(bi), [sqt])
                pb, pt = pss.next()
                for c in range(8):
                    self.mm(pb[:, :w], self.ones_bf[:], sqb[:, c, :w], c == 0, c == 7, [sqt], [pt])
                rb, rt = rr.next()
                self.act(rb[:, :w], pb[:, :w], AF.Sqrt, [pt], [rt], bias=EPS, scale=1.0 / 1024)
                self.recip(rb[:, :w], rb[:, :w], [rt], [rt])
                for c in range(8):
                    self.stt("dve", y[:, c, :w], self.xT[:, c, s:e], self.fng[:, c:c + 1], rb[:, :w], ALU.mult, ALU.mult,
                             [self.tx[bi][c], rt], [yt])
            for q in range(w // 128):
                o, ot = ob.next()
                for half in range(2):
                    pb2, pt2 = ptr.next()
                    for c4 in range(4):
                        c = half * 4 + c4
                        self.mm(pb2[:, c4 * 128:(c4 + 1) * 128], y[:, c, q * 128:(q + 1) * 128], self.ident[:], True, True,
                                [yt], [pt2])
                    self.cp("act" if half == 0 else "dve", o[:, half * 512:(half + 1) * 512], pb2[:], [pt2], [ot])
                r0 = s + q * 128
                self.dma("sp", self.out_d[r0:r0 + 128, :], o[:], R=[ot])
        self.end(final=True)

    def phase_attn(self, L, need_ctx, mode):
        self.begin()
        d = self.d
        p = "l%d_" % L
        da = mode == "da"
        DV = 128 if da else 64
        AW = DV + 1
        inv_sqrt = 0.125
        tconst = self.P.tile()
        rotT = self.sb([128, 128])
        cosT = self.sb([128, NL])
        sinT = self.sb([128, NL])
        self.dma("sp", rotT[:], d["rotT"], W=[tconst])
        self.dma("sp", cosT[:], d["cos"], W=[tconst])
        self.dma("sp", sinT[:], d["sin"], W=[tconst])
        if da:
            lam_init = 0.8 - 0.6 * math.exp(-0.3 * L)
            lamv = self.sb([128, 4, 64])
            self.dma("sp", lamv[:], d[p + "lam"], W=[tconst])
            lt = self.sb([128, 2, 64])
            ls = self.sb([128, 2])
            nlam = self.sb([128, 1])
            tl = self.P.tile()
            self.tt("dve", lt[:, 0, :], lamv[:, 0, :], lamv[:, 1, :], ALU.mult, [tconst], [tl])
            self.tt("dve", lt[:, 1, :], lamv[:, 2, :], lamv[:, 3, :], ALU.mult, [tconst], [tl])
            self.P.op("dve", lambda e: e.reduce_sum(out=ls[:], in_=lt[:], axis=AX.X), [tl], [tl])
            self.act(ls[:], ls[:], AF.Exp, [tl], [tl])
            self.tt("dve", nlam[:], ls[:, 1:2], ls[:, 0:1], ALU.subtract, [tl], [tl])
            self.ts("dve", nlam[:], nlam[:], -lam_init, 0.0, ALU.add, ALU.add, [tl], [tl])
        else:
            esink = self.sb([128, 16])
            self.dma("sp", esink[:], d[p + "sink"], W=[tconst])
            self.act(esink[:], esink[:], AF.Exp, [tconst], [tconst])
            swm = self.sb([128, 6, 512], BF16)
            self.dma("sp", swm[:], d["swmask"], W=[tconst])

        wv = d[p + "w_in"].rearrange("(kc kp) f -> kp kc f", kp=128)
        wring = self.ring(2, [128, 8, 384], BF16)
        woring = self.ring(2, [128, 1024], BF16)
        QT = self.sb([128, NT], BF16)
        KT = self.sb([128, NT], BF16)
        Vt = self.sb([128, 18, 128], BF16)
        aT = self.sb([128, NT], BF16)
        tq = [self.P.tile() for _ in range(5)]
        tk = [self.P.tile() for _ in range(5)]
        tv = [self.P.tile() for _ in range(5)]
        ta = [self.P.tile() for _ in range(5)]
        q32r = self.ring(2, [128, 512])
        tmpr = self.ring(4, [128, 512])
        misc = self.ring(2, [128, 512], psum=True)
        sring = self.ring(2, [128, 1024], psum=True)
        accO = self.ps([128, 512])
        accZ = self.ps([128, 512])
        tacc = [self.P.tile(), self.P.tile()]
        ering = self.ring(3, [128, 1024], BF16)
        rring = self.ring(2, [128, 512])
        o0ring = self.ring(2, [128, 512])
        if da:
            oring = self.ring(2, [128, 512])
            sqring = self.ring(2, [128, 512], BF16)
        qblocks = [0, 1, 2, 3] + ([4] if need_ctx else [])
        if da:
            subgc = self.sb([128, 1])
            self.dma("sp", subgc[:], d[p + "subgc"], W=[tconst])
            self.ts("dve", subgc[:], subgc[:], 1.0 - lam_init, 0.0, ALU.mult, ALU.add, [tconst], [tconst])

        def proj_fm(wb, wt, col0, dst, tdst, rope=True, blks=range(5)):
            for bi in blks:
                s, e = BLKS[bi]
                w = e - s
                pb, pt = misc.next()
                for kc in range(8):
                    self.mm(pb[:, :w], wb[:, kc, col0:col0 + 128], self.hT[:, kc, s:e], kc == 0, kc == 7,
                            [wt, self.th[bi]], [pt])
                if bi == 4 or not rope:
                    self.cp("act", dst[:, s:e], pb[:, :w], [pt], [tdst[bi]])
                else:
                    qb, qt = q32r.next()
                    self.cp("act", qb[:, :w], pb[:, :w], [pt], [qt])
                    pb2, pt2 = misc.next()
                    self.mm(pb2[:, :w], rotT[:], qb[:, :w], True, True, [qt, tconst], [pt2])
                    t1, t1t = tmpr.next()
                    self.tt("pool", t1[:, :w], qb[:, :w], cosT[:, s:e], ALU.mult, [qt, tconst], [t1t])
                    t2, t2t = tmpr.next()
                    self.tt("dve", t2[:, :w], pb2[:, :w], sinT[:, s:e], ALU.mult, [pt2, tconst], [t2t])
                    self.tt("dve", dst[:, s:e], t1[:, :w], t2[:, :w], ALU.add, [t1t, t2t], [tdst[bi]])

        def proj_v(wb, wt, col0):
            for g in range(5):
                tts = list(range(g * 4, min(g * 4 + 4, 18)))
                n = len(tts)
                pb, pt = misc.next()
                for i, t_ in enumerate(tts):
                    for kc in range(8):
                        self.mm(pb[:, i * 128:(i + 1) * 128], self.hT[:, kc, t_ * 128:(t_ + 1) * 128],
                                wb[:, kc, col0:col0 + 128], kc == 0, kc == 7, [self.th[blk_of(t_ * 128)], wt], [pt])
                self.cp("act", Vt[:, g * 4:g * 4 + n, :], pb[:, :n * 128].rearrange("p (a b) -> p a b", a=n), [pt], [tv[g]])

        nunits = 8

        def load_w(u):
            wb, wt = wring.next()
            if da:
                for i in range(3):
                    self.dma("pool", wb[:, :, i * 128:(i + 1) * 128], wv[:, :, i * 1024 + u * 128:i * 1024 + (u + 1) * 128],
                             W=[wt])
            else:
                g = u // 2
                self.dma("pool", wb[:, :, 0:128], wv[:, :, u * 128:(u + 1) * 128], W=[wt])
                if u % 2 == 0:
                    for i in range(2):
                        self.dma("pool", wb[:, :, 128 + i * 64:128 + (i + 1) * 64], wv[:, :, 1024 + g * 64:1024 + (g + 1) * 64],
                                 W=[wt])
                        self.dma("pool", wb[:, :, 256 + i * 64:256 + (i + 1) * 64], wv[:, :, 1280 + g * 64:1280 + (g + 1) * 64],
                                 W=[wt])
            return wb, wt

        def do_proj(u, wts):
            wb, wt = wts
            wob, wot = woring.next()
            self.dma("pool", wob[:], d[p + "w_out"][u * 128:(u + 1) * 128, :], W=[wot])
            if da:
                proj_fm(wb, wt, 0, QT, tq)
                proj_fm(wb, wt, 128, KT, tk)
                proj_v(wb, wt, 256)
            else:
                proj_fm(wb, wt, 0, QT, tq)
                if u % 2 == 0:
                    proj_fm(wb, wt, 128, KT, tk)
                    proj_v(wb, wt, 256)
            return wob, wot

        def outproj_block(wpair, bi):
            wob_, wot_ = wpair
            s, e = BLKS[bi]
            w = e - s
            j = 1 if bi == 4 else 0
            for dc in range(8):
                pb, pt = misc.next()
                self.mm(pb[:, :w], wob_[:, dc * 128:(dc + 1) * 128], aT[:, s:e], True, True, [wot_, ta[bi]], [pt])
                self.stt("dve", self.xT[:, dc, s:e], pb[:, :w], self.modv[:, L, 16 + dc, j:j + 1], self.xT[:, dc, s:e],
                         ALU.mult, ALU.add, [pt, self.tx[bi][dc]], [self.tx[bi][dc]])

        wts_next = load_w(0)
        pend_w = do_proj(0, wts_next)
        prev = None
        for u in range(nunits):
            wob, wot = pend_w
            if u + 1 < nunits:
                wts_next = load_w(u + 1)
            ginfo = {}
            items = []
            for qi in qblocks:
                q0, q1 = BLKS[qi]
                w = q1 - q0
                isctx = qi == 4
                qt0 = q0 // 128
                if isctx:
                    kts = [16, 17]
                elif da:
                    kts = list(range(18))
                else:
                    kts = [k_ for k_ in range(qt0 - 1, qt0 + 5) if 0 <= k_ < 16] + [16, 17]
                ginfo[qi] = (q0, q1, w, isctx, qt0, kts)
                for c in range(2):
                    for pi in range(0, len(kts), 2):
                        items.append((qi, c, pi))

            def issue_S(it):
                qi, c, pi = it
                q0, q1, w, isctx, qt0, kts = ginfo[qi]
                pair = kts[pi:pi + 2]
                sbuf_, st_ = sring.next()
                for i, kt in enumerate(pair):
                    self.mm(sbuf_[:, i * 512:i * 512 + w], KT[c * 64:(c + 1) * 64, kt * 128:(kt + 1) * 128],
                            QT[c * 64:(c + 1) * 64, q0:q1], True, True, [tk[blk_of(kt * 128)], tq[qi]], [st_])
                return sbuf_, st_

            def finish(qi, c, o0):
                q0, q1, w, isctx, qt0, kts = ginfo[qi]
                rb, rt = rring.next()
                self.cp("dve" if da else "act", rb[:, :w], accZ[:, :w], [tacc[1]], [rt])
                if da:
                    if c == 0:
                        ob_, ot_ = o0ring.next()
                        self.cp("dve", ob_[:, :w], accO[:, :w], [tacc[0]], [ot_])
                        self.recip(rb[:, :w], rb[:, :w], [rt], [rt])
                        self.tt("dve", ob_[:, :w], ob_[:, :w], rb[:, :w], ALU.mult, [ot_, rt], [ot_])
                        return (ob_, ot_)
                    ob0, ot0 = o0
                    o1, o1t = oring.next()
                    self.cp("dve", o1[:, :w], accO[:, :w], [tacc[0]], [o1t])
                    self.recip(rb[:, :w], rb[:, :w], [rt], [rt])
                    self.tt("dve", o1[:, :w], o1[:, :w], rb[:, :w], ALU.mult, [o1t, rt], [o1t])
                    self.stt("dve", aT[:, q0:q1], o1[:, :w], nlam[:, 0:1], ob0[:, :w], ALU.mult, ALU.add, [o1t, ot0, tl], [ta[qi]])
                    return None
                hq = 2 * u + c
                ob_, ot_ = o0ring.next()
                self.cp("act", ob_[c * 64:(c + 1) * 64, :w], accO[c * 64:(c + 1) * 64, :w], [tacc[0]], [ot_])
                self.ts("dve", rb[:, :w], rb[:, :w], esink[:, hq:hq + 1], 0.0, ALU.add, ALU.add, [rt, tconst], [rt])
                self.recip(rb[:, :w], rb[:, :w], [rt], [rt])
                self.tt("dve", aT[c * 64:(c + 1) * 64, q0:q1], ob_[c * 64:(c + 1) * 64, :w], rb[c * 64:(c + 1) * 64, :w],
                        ALU.mult, [ot_, rt], [ta[qi]])
                return None

            def subln_head():
                for qi in qblocks:
                    q0, q1, w, isctx, qt0, kts = ginfo[qi]
                    sq, sqt = sqring.next()
                    self.act(sq[:, :w], aT[:, q0:q1], AF.Square, [ta[qi]], [sqt])
                    pb, pt = misc.next()
                    self.mm(pb[:, :w], self.ones_bf[:], sq[:, :w], True, True, [sqt], [pt])
                    rb, rt = (rring, rring, o0ring, o0ring, oring)[len(rstd_jobs)].next()
                    self.cp("dve", rb[:, :w], pb[:, :w], [pt], [rt])
                    rstd_jobs.append((qi, rb, rt))
                for (qi, rb, rt) in rstd_jobs:
                    q0, q1, w, isctx, qt0, kts = ginfo[qi]
                    self.act(rb[:, :w], rb[:, :w], AF.Sqrt, [rt], [rt], bias=EPS, scale=1.0 / 128)
                for (qi, rb, rt) in rstd_jobs:
                    q0, q1, w, isctx, qt0, kts = ginfo[qi]
                    self.recip(rb[:, :w], rb[:, :w], [rt], [rt])
                    self.stt("dve", aT[:, q0:q1], aT[:, q0:q1], subgc[:, 0:1], rb[:, :w], ALU.mult, ALU.mult,
                             [ta[qi], rt, tconst], [ta[qi]])
                del rstd_jobs[:]

            rstd_jobs = []
            pend = issue_S(items[0])
            if prev is not None and da:
                subln_head()
            o0 = None
            for k_it, it in enumerate(items):
                sbuf_, st_ = pend
                if k_it + 1 < len(items):
                    pend = issue_S(items[k_it + 1])
                qi, c, pi = it
                q0, q1, w, isctx, qt0, kts = ginfo[qi]
                pair = kts[pi:pi + 2]
                eb, et = ering.next()
                npair = len(pair)
                if w == 512:
                    self.act(eb[:, :npair * 512], sbuf_[:, :npair * 512], AF.Exp, [st_], [et], scale=inv_sqrt)
                else:
                    self.act(eb[:].rearrange("p (a b) -> p a b", a=2)[:, :npair, :w],
                             sbuf_[:].rearrange("p (a b) -> p a b", a=2)[:, :npair, :w], AF.Exp, [st_], [et],
                             scale=inv_sqrt)
                for i, kt in enumerate(pair):
                    if (not da) and (not isctx) and kt < 16:
                        rel = kt - qt0 + 1
                        self.tt("dve", eb[:, i * 512:(i + 1) * 512], eb[:, i * 512:(i + 1) * 512], swm[:, rel, :],
                                ALU.mult, [et, tconst], [et])
                    self.mm(accO[:, :w], Vt[:, kt, :], eb[:, i * 512:i * 512 + w], kt == kts[0], kt == kts[-1],
                            [et, tv[kt // 4]], [tacc[0]])
                    self.mm(accZ[:, :w], self.ones_bf[:], eb[:, i * 512:i * 512 + w], kt == kts[0], kt == kts[-1],
                            [et], [tacc[1]])
                if pi + 2 >= len(kts):
                    if c == 0 and prev is not None:
                        outproj_block(prev, qi)
                    o0 = finish(qi, c, o0)
            if u + 1 < nunits:
                pend_w = do_proj(u + 1, wts_next)
            prev = (wob, wot)
        if da:
            subln_head()
        for bi in qblocks:
            outproj_block(prev, bi)
        self.end()
        self.free_hT()


    def phase_mlstm(self, L, need_ctx):
        import os
        MLSTOP = int(os.environ.get('MLSTOP', '0'))
        MLSUB = int(os.environ.get('MLSUB', '9'))
        MLPART = int(os.environ.get('MLPART', '0'))
        d = self.d
        p = "l%d_" % L
        self.P.relax = False
        wv = d[p + "w_in"].rearrange("(kc kp) f -> kp kc f", kp=128)
        oblks = list(range(5 if need_ctx else 4))
        order = {0: [16, 17] + list(range(16)), 1: [17, 16] + list(range(15, -1, -1))}
        for u in range(4):
            if MLSTOP == 9:
                continue
            ues = ExitStack()
            msk = self.sb([128, 2, 128], F32, ues)
            ones32 = self.sb([128, 128], F32, ues)
            gball = self.sb([128, 32], F32, ues)
            mlg = self.sb([128, 256], F32, ues)
            gbu = self.sb([128, 8], F32, ues)
            wo2 = self.sb([128, 2, 1024], BF16, ues)
            QT = self.sb([128, NT], BF16, ues)
            KT = self.sb([128, NT], BF16, ues)
            KV = self.sb([128, 18, 384], BF16, ues)
            sigo = self.sb([128, 18, 256], BF16, ues)
            G = self.sb([128, 18, 8], F32, ues)
            nlf = self.sb([128, 18, 2, 2], F32, ues)
            nb = self.sb([128, 18, 2, 2], F32, ues)
            nbt = self.sb([128, 18, 2, 2], F32, ues)
            cexp = self.sb([128, 18, 2, 2], F32, ues)
            ebt = self.sb([128, 18, 2, 2], F32, ues)
            aK = self.sb([128, 18, 2, 2], F32, ues)
            aKc = self.sb([128, 18, 2], F32, ues)
            self.begin()
            tconst = self.P.tile()
            self.dma("sp", msk[:], d["mlmask"], W=[tconst])
            self.dma("sp", gball[:], d[p + "gate_b"], W=[tconst])
            self.dma("sp", mlg[:], d[p + "mlg"][:, u * 256:(u + 1) * 256], W=[tconst])
            self.memset("pool", ones32[:], 1.0, [], [tconst])
            wb = self.sb([128, 8, 800], BF16)
            wt = self.P.tile()
            self.dma("pool", wb[:, :, 0:128], wv[:, :, u * 128:(u + 1) * 128], W=[wt])
            self.dma("pool", wb[:, :, 128:256], wv[:, :, 512 + u * 128:512 + (u + 1) * 128], W=[wt])
            self.dma("pool", wb[:, :, 256:512], wv[:, :, 1024 + u * 256:1024 + (u + 1) * 256], W=[wt])
            self.dma("pool", wb[:, :, 512:768], wv[:, :, 2048 + u * 256:2048 + (u + 1) * 256], W=[wt])
            wg32 = self.sb([128, 8, 32])
            twg = self.P.tile()
            self.dma("sp", wg32[:], wv[:, :, 3072:3104], W=[twg])
            for kd in range(4):
                self.cp("dve", wb[:, :, 768 + kd * 2:770 + kd * 2], wg32[:, :, kd * 8 + 2 * u:kd * 8 + 2 * u + 2], [twg], [wt])
            wot = self.P.tile()
            for c in range(2):
                self.dma("pool", wo2[:, c, :], d[p + "w_out"][(2 * u + c) * 128:(2 * u + c + 1) * 128, :], W=[wot])
            for kd in range(4):
                self.cp("dve", gbu[:, kd * 2:kd * 2 + 2], gball[:, kd * 8 + 2 * u:kd * 8 + 2 * u + 2], [tconst], [tconst])

            tqk = [self.P.tile() for _ in range(5)]
            tkv = [self.P.tile() for _ in range(18)]
            tso = self.P.tile()
            tg = self.P.tile()
            misc = self.ring(3, [128, 512], psum=True)

            if MLSUB == 0:
                self.end()
                ues.close()
                continue
            for which, dst in ((0, QT), (1, KT)):
                for bi in range(5):
                    s, e = BLKS[bi]
                    w = e - s
                    pb, pt = misc.next()
                    for kc in range(8):
                        self.mm(pb[:, :w], wb[:, kc, which * 128:(which + 1) * 128], self.hT[:, kc, s:e], kc == 0, kc == 7,
                                [wt, self.th[bi]], [pt])
                    self.act(dst[:, s:e], pb[:, :w], AF.Copy, [pt], [tqk[bi]], scale=(0.125 if which == 0 else 1.0))
            if MLSUB == 1:
                self.end()
                ues.close()
                continue
            for t_ in range(18):
                pb, pt = misc.next()
                for kc in range(8):
                    self.mm(pb[:, 0:384], self.hT[:, kc, t_ * 128:(t_ + 1) * 128], wb[:, kc, 128:512], kc == 0, kc == 7,
                            [wt, self.th[blk_of(t_ * 128)]], [pt])
                self.cp("dve", KV[:, t_, :], pb[:, 0:384], [pt], [tkv[t_]])
            if MLSUB == 2:
                self.end()
                ues.close()
                continue
            sgr = self.ring(2, [128, 256])
            for t_ in range(18):
                pb, pt = misc.next()
                for kc in range(8):
                    self.mm(pb[:, 0:264], self.hT[:, kc, t_ * 128:(t_ + 1) * 128], wb[:, kc, 512:776], kc == 0, kc == 7,
                            [wt, self.th[blk_of(t_ * 128)]], [pt])
                if MLSUB != 3:
                    self.tt("dve", G[:, t_, :], pb[:, 256:264], gbu[:], ALU.add, [pt, tconst], [tg])
                if MLSUB != 4:
                    eb_, et_ = sgr.next()
                    self.act(eb_[:], pb[:, 0:256], AF.Exp, [pt, tg], [et_], scale=-1.0)
                    self.ts("dve", eb_[:], eb_[:], 1.0, 0.0, ALU.add, ALU.add, [et_], [et_])
                    self.recip(eb_[:], eb_[:], [et_], [et_])
                    self.cp("act", sigo[:, t_, :], eb_[:], [et_], [tso])
            if MLSTOP == 1:
                self.end()
                ues.close()
                continue
            Gv = G[:].rearrange("p t (k c) -> p t k c", c=2)
            tgp = self.P.tile()
            self.act(nlf[:], Gv[:, :, 1:4:2, :], AF.Exp, [tg], [tgp], scale=-1.0)
            self.act(nlf[:], nlf[:], AF.Ln, [tgp], [tgp], bias=1.0)
            if MLSUB == 5:
                self.end()
                ues.close()
                continue
            for dr in range(2):
                pb, pt = misc.next()
                self.mm(pb[:, 0:36], msk[:, dr, :], nlf[:, :, dr, :], True, True, [tgp, tconst], [pt])
                self.cp("dve", nb[:, :, dr, :], pb[:, 0:36].rearrange("p (t c) -> p t c", c=2), [pt], [tgp])
            pb, pt = misc.next()
            self.mm(pb[:, 0:72], ones32[:], nlf[:].rearrange("p t a c -> p (t a c)"), True, True, [tgp, tconst], [pt])
            self.cp("dve", nbt[:].rearrange("p t a c -> p (t a c)"), pb[:, 0:72], [pt], [tgp])
            if MLSUB == 6:
                self.end()
                ues.close()
                continue
            self.tt("dve", cexp[:], Gv[:, :, 0:4:2, :], nb[:], ALU.add, [tg, tgp], [tgp])
            self.act(cexp[:], cexp[:], AF.Exp, [tgp], [tgp])
            self.act(ebt[:], nb[:], AF.Exp, [tgp], [tgp], scale=-1.0)
            self.act(aK[:], nbt[:], AF.Exp, [tgp], [tgp], scale=-1.0)
            for c in range(2):
                self.cp("dve", aKc[c * 64:(c + 1) * 64, :, :], aK[c * 64:(c + 1) * 64, :, :, c], [tgp], [tgp])

            if MLSTOP == 2:
                self.end()
                ues.close()
                continue
            if MLSUB == 7:
                self.end()
                ues.close()
                continue
            self.end()
            self.begin()
            tconst = self.P.tile()
            tgp = self.P.tile()
            tso = self.P.tile()
            wot = self.P.tile()
            tqk = [self.P.tile() for _ in range(5)]
            tkv = [self.P.tile() for _ in range(18)]
            ths = [self.P.tile() for _ in range(18)]
            hs = self.sb([128, 18, 256])
            self.memset("pool", hs[:], 0.0, [], ths)
            misc = self.ring(3, [128, 512], psum=True)
            outp = self.ring(2, [128, 512], psum=True)
            kvp = self.ring(2, [128, 512], psum=True)
            S = [self.sb([128, 130]) for _ in range(2)]
            Sbf = [self.sb([128, 130], BF16) for _ in range(2)]
            tS = [self.P.tile(), self.P.tile()]
            tSb = [self.P.tile(), self.P.tile()]
            for dr in range(2):
                self.memset("pool", S[dr][:], 0.0, [], [tS[dr]])
                self.memset("pool", Sbf[dr][:], 0.0, [], [tSb[dr]])
            amr = self.ring(4, [128, 2, 128], BF16)
            vpr = self.ring(4, [128, 2, 130], BF16)
            ndr = self.ring(3, [128, 2, 130])
            smr = self.ring(4, [128, 4])
            hsteps = [(order[dr][step], dr) for step in range(18) for dr in range(2)]

            def stageA(t_, dr):
                bi = blk_of(t_ * 128)
                cs = slice(t_ * 128, (t_ + 1) * 128)
                pa, pat = misc.next()
                for c in range(2):
                    self.mm(pa[:, c * 128:(c + 1) * 128], KT[c * 64:(c + 1) * 64, cs], QT[c * 64:(c + 1) * 64, cs], True, True,
                            [tqk[bi]], [pat], sgc=True, pesync=(c == 1))
                am, amt = amr.next()
                self.tt("dve", am[:], pa[:, 0:256].rearrange("p (c t) -> p c t", c=2),
                        msk[:, dr:dr + 1, :].to_broadcast([128, 2, 128]), ALU.mult, [pat, tconst], [amt])
                vp, vpt = vpr.next()
                for c in range(2):
                    self.ts("pool" if c == 0 else "dve", vp[:, c, 0:128], KV[:, t_, 128 + c * 128:256 + c * 128],
                            cexp[:, t_, dr, c:c + 1], 0.0, ALU.mult, ALU.add, [tkv[t_], tgp], [vpt])
                self.cp("dve", vp[:, :, 128], cexp[:, t_, dr, :], [tgp], [vpt])
                pk, pkt = kvp.next()
                for c in range(2):
                    self.mm(pk[:, c * 129:c * 129 + 129], KV[:, t_, 0:128], vp[:, c, 0:129], c == 0, True, [tkv[t_], vpt], [pkt],
                            sgc=True)
                return (am, amt, vp, vpt, pk, pkt)

            def stageB(t_, dr, st):
                am, amt, vp, vpt, pk, pkt = st
                bi = blk_of(t_ * 128)
                cs = slice(t_ * 128, (t_ + 1) * 128)
                po, pot = outp.next()
                for c in range(2):
                    self.mm(po[:, c * 129:c * 129 + 129], am[:, c, :], vp[:, c, 0:129], c == 0, False, [amt, vpt], [pot], sgc=True)
                for c in range(2):
                    self.mm(po[:, c * 129:c * 129 + 129], QT[c * 64:(c + 1) * 64, cs], Sbf[dr][c * 64:(c + 1) * 64, 0:129],
                            False, True, [tqk[bi], tSb[dr]], [pot], sgc=True, pesync=(c == 1))
                for c in range(2):
                    self.tt("dve", S[dr][c * 64:(c + 1) * 64, 0:129], S[dr][c * 64:(c + 1) * 64, 0:129],
                            pk[c * 64:(c + 1) * 64, c * 129:c * 129 + 129], ALU.add, [pkt, tS[dr]], [tS[dr]])
                self.ts("dve", S[dr][:, 0:129], S[dr][:, 0:129], aKc[:, t_, dr:dr + 1], 0.0, ALU.mult, ALU.add,
                        [tS[dr], tgp], [tS[dr]])
                self.cp("act", Sbf[dr][:, 0:129], S[dr][:, 0:129], [tS[dr]], [tSb[dr]])
                nd, ndt = ndr.next()
                for c in range(2):
                    self.act(nd[:, c, 0:129], po[:, c * 129:c * 129 + 129], AF.Copy, [pot, tgp], [ndt],
                             scale=ebt[:, t_, dr, c:c + 1])
                sm, smt = smr.next()
                self.act(sm[:, 0:2], nd[:, :, 128], AF.Abs, [ndt], [smt])
                self.ts("dve", sm[:, 0:2], sm[:, 0:2], 1.0, 0.0, ALU.max, ALU.add, [smt], [smt])
                self.recip(sm[:, 0:2], sm[:, 0:2], [smt], [smt])
                for c in range(2):
                    self.stt("dve", hs[:, t_, c * 128:(c + 1) * 128], nd[:, c, 0:128], sm[:, c:c + 1],
                             hs[:, t_, c * 128:(c + 1) * 128], ALU.mult, ALU.add, [ndt, smt, ths[t_]], [ths[t_]])

            pend = stageA(*hsteps[0])
            for k_hs, (t_, dr) in enumerate(hsteps):
                cur = pend
                if k_hs + 1 < len(hsteps):
                    pend = stageA(*hsteps[k_hs + 1])
                stageB(t_, dr, cur)

            if MLSTOP == 3:
                self.end()
                ues.close()
                continue
            hnr = self.ring(2, [128, 256])
            jr = self.ring(2, [128, 128])
            for t_ in range(18):
                bi = blk_of(t_ * 128)
                if bi not in oblks:
                    continue
                sm, smt = smr.next()
                self.memset("dve", sm[:, 0:2], 0.0, [], [smt])
                for c in range(2):
                    jb, jt = jr.next()
                    self.act(jb[:], hs[:, t_, c * 128:(c + 1) * 128], AF.Square, [ths[t_], smt], [jt, smt], accum=sm[:, c:c + 1])
                self.act(sm[:, 2:4], sm[:, 0:2], AF.Sqrt, [smt], [smt], bias=EPS, scale=1.0 / 128)
                self.recip(sm[:, 2:4], sm[:, 2:4], [smt], [smt])
                hn, hnt = hnr.next()
                for c in range(2):
                    self.stt("dve", hn[:, c * 128:(c + 1) * 128], hs[:, t_, c * 128:(c + 1) * 128], sm[:, 2 + c:3 + c],
                             mlg[:, c * 128:(c + 1) * 128], ALU.mult, ALU.mult, [ths[t_], smt, tconst], [hnt])
                self.tt("pool", hn[:], hn[:], sigo[:, t_, :], ALU.mult, [hnt, tso], [hnt])
                for c in range(2):
                    pb, pt = misc.next()
                    self.mm(pb[:, 0:128], hn[:, c * 128:(c + 1) * 128], self.ident[:], True, True, [hnt], [pt])
                    self.cp("act", (QT if c == 0 else KT)[:, t_ * 128:(t_ + 1) * 128], pb[:, 0:128], [pt], [tqk[bi]])
            for bi in oblks:
                s, e = BLKS[bi]
                w = e - s
                j = 1 if bi == 4 else 0
                for dc in range(8):
                    pb, pt = misc.next()
                    for c in range(2):
                        self.mm(pb[:, :w], wo2[:, c, dc * 128:(dc + 1) * 128], (QT if c == 0 else KT)[:, s:e], c == 0, c == 1, [wot, tqk[bi]], [pt])
                    self.stt("dve", self.xT[:, dc, s:e], pb[:, :w], self.modv[:, L, 16 + dc, j:j + 1], self.xT[:, dc, s:e],
                             ALU.mult, ALU.add, [pt, self.tx[bi][dc]], [self.tx[bi][dc]])
            self.end()
            ues.close()
        self.P.relax = RELAX_SAME
        self.free_hT()


_BF = ml_dtypes.bfloat16


def _pk(v, k):
    return np.ascontiguousarray(np.asarray(v, np.float32).reshape(k, 128).T)


def _consts():
    c = {}
    c["ident"] = np.eye(128, dtype=np.float32)
    R = np.zeros((128, 128), np.float32)
    for dd in range(128):
        i = dd % 32
        if i < 16:
            R[dd, dd + 16] = -1.0
        else:
            R[dd, dd - 16] = 1.0
    c["rotT"] = np.ascontiguousarray(R.T)
    t = np.arange(NL)
    row = (t // 64).astype(np.float32)
    col = (t % 64).astype(np.float32)
    inv = (10000.0 ** (-np.arange(16, dtype=np.float32) / 16)).astype(np.float32)
    cos = np.zeros((128, NL), np.float32)
    sin = np.zeros((128, NL), np.float32)
    for p in range(128):
        dd = p % 64
        j = dd % 16
        pos = row if dd < 32 else col
        ang = (pos * inv[j]).astype(np.float32)
        cos[p] = np.cos(ang)
        sin[p] = np.sin(ang)
    c["cos"] = cos
    c["sin"] = sin
    m = np.zeros((128, 6, 512), np.float32)
    kk = np.arange(128)[:, None]
    qq = np.arange(512)[None, :]
    for r in range(6):
        rel = r - 1
        m[:, r, :] = (np.abs(qq - kk - rel * 128) <= 128)
    c["swmask"] = m.astype(_BF)
    mm_ = np.zeros((128, 2, 128), np.float32)
    ss = np.arange(128)[:, None]
    tt = np.arange(128)[None, :]
    mm_[:, 0, :] = ss <= tt
    mm_[:, 1, :] = ss >= tt
    c["mlmask"] = mm_
    return c


_CACHE = {}


def _get_nc(n_layers=4, debug=False):
    key = (n_layers, debug)
    if key not in _CACHE:
        _CACHE[key] = Builder(n_layers, debug).build()
    return _CACHE[key]


def _shared_inputs(inputs, n_layers):
    sh = dict(_consts())
    sh["fng"] = _pk(inputs["final_norm_g"], 8)
    for L in range(n_layers):
        p = "l%d_" % L
        f = lambda n: np.asarray(inputs[p + n], np.float32)
        sh[p + "ada_w"] = np.ascontiguousarray(f("ada_w"))
        sh[p + "ada_b"] = _pk(f("ada_b"), 48)
        sh[p + "n1g"] = _pk(f("norm1_g"), 8)
        sh[p + "n2g"] = _pk(f("norm2_g"), 8)
        sh[p + "w_up"] = np.ascontiguousarray(f("ffn_w_up"))
        sh[p + "w_down"] = np.ascontiguousarray(f("ffn_w_down"))
        cw = f("ffn_conv_w")
        sh[p + "cw"] = np.ascontiguousarray(cw.reshape(3, 44, 128).transpose(2, 1, 0))
        sh[p + "cb"] = _pk(f("ffn_conv_b"), 44)
        k = KINDS[L]
        if k == 0:
            sh[p + "w_in"] = np.ascontiguousarray(f("da_w_in"))
            lam = np.stack([f("da_lam_q1"), f("da_lam_k1"), f("da_lam_q2"), f("da_lam_k2")], 0)
            sh[p + "lam"] = np.ascontiguousarray(np.broadcast_to(lam[None], (128, 4, 64)))
            sh[p + "subgc"] = np.ascontiguousarray(f("da_subln_g").reshape(128, 1))
            sh[p + "w_out"] = np.ascontiguousarray(f("da_w_out"))
        elif k == 1:
            sh[p + "w_in"] = np.ascontiguousarray(f("ml_w_in"))
            sh[p + "gate_b"] = np.ascontiguousarray(np.broadcast_to(f("ml_gate_b")[None], (128, 32)))
            sh[p + "mlg"] = np.ascontiguousarray(np.broadcast_to(f("ml_norm_g")[None], (128, 1024)))
            sh[p + "w_out"] = np.ascontiguousarray(f("ml_w_out"))
        else:
            sh[p + "w_in"] = np.ascontiguousarray(f("sw_w_in"))
            sh[p + "sink"] = np.ascontiguousarray(np.broadcast_to(f("sw_sink")[None], (128, 16)))
            sh[p + "w_out"] = np.ascontiguousarray(f("sw_w_out"))
    return sh


def _run(inputs, n_layers=4, debug=False):
    nc = _get_nc(n_layers, debug)
    sh = _shared_inputs(inputs, n_layers)
    x = np.asarray(inputs["x"], np.float32)
    c = np.asarray(inputs["c"], np.float32)
    ctx = np.asarray(inputs["ctx"], np.float32)
    c_ctx = np.asarray(inputs["c_ctx"], np.float32)
    in_maps = []
    for b in range(8):
        m = dict(sh)
        m["x"] = np.ascontiguousarray(x[b])
        m["ctx"] = np.ascontiguousarray(ctx[b])
        m["cc"] = np.ascontiguousarray(np.concatenate([_pk(c[b], 8), _pk(c_ctx, 8)], axis=1))
        in_maps.append(m)
    res = run_bass_kernel_spmd(nc, in_maps, core_ids=list(range(8)))
    return np.stack([np.asarray(r["out"], np.float32) for r in res.results], 0)


_INPUT_NAMES = (
    "x", "c", "ctx", "c_ctx",
    "l0_ada_w", "l0_ada_b", "l0_norm1_g", "l0_da_w_in",
    "l0_da_lam_q1", "l0_da_lam_k1", "l0_da_lam_q2", "l0_da_lam_k2",
    "l0_da_subln_g", "l0_da_w_out", "l0_norm2_g", "l0_ffn_w_up",
    "l0_ffn_conv_w", "l0_ffn_conv_b", "l0_ffn_w_down", "l1_ada_w",
    "l1_ada_b", "l1_norm1_g", "l1_ml_w_in", "l1_ml_gate_b",
    "l1_ml_norm_g", "l1_ml_w_out", "l1_norm2_g", "l1_ffn_w_up",
    "l1_ffn_conv_w", "l1_ffn_conv_b", "l1_ffn_w_down", "l2_ada_w",
    "l2_ada_b", "l2_norm1_g", "l2_sw_w_in", "l2_sw_sink",
    "l2_sw_w_out", "l2_norm2_g", "l2_ffn_w_up", "l2_ffn_conv_w",
    "l2_ffn_conv_b", "l2_ffn_w_down", "l3_ada_w", "l3_ada_b",
    "l3_norm1_g", "l3_da_w_in", "l3_da_lam_q1", "l3_da_lam_k1",
    "l3_da_lam_q2", "l3_da_lam_k2", "l3_da_subln_g", "l3_da_w_out",
    "l3_norm2_g", "l3_ffn_w_up", "l3_ffn_conv_w", "l3_ffn_conv_b",
    "l3_ffn_w_down", "final_norm_g",
)


def kernel(**inputs):
    missing = [n for n in _INPUT_NAMES if n not in inputs]
    assert not missing, missing
    return _run(inputs, 4, False)
```
